# Optimizing a Trainium2 kernel written in Bass

```python
import jax, jax.numpy as jnp
from jax import lax
import numpy as np

D_MODEL = 1024
BATCH = 8
SEQ = 4096
DEPTH = 4

GRID_W = 64
CTX_LEN = 256
MIX_WIDTH = D_MODEL
NA_HEADS = 8
NA_HEAD_DIM = 64
NA_WIDTH = NA_HEADS * NA_HEAD_DIM
NA_WIN_ROWS = 8
NA_WIN_COLS = 16
DN_HEADS = 4
DN_HEAD_DIM = 128
DN_WIDTH = DN_HEADS * DN_HEAD_DIM
DN_CONV = 5
DN_CHUNK = 64
FFN_DIM = 2816
N_MOD = 9
IN_COLS = 3 * NA_WIDTH + 4 * DN_WIDTH + 4 * DN_HEADS
ROPE_BASE = 10000.0
RMS_EPS = 1e-6

kernel_name = 'hybrid_na_gdn_macaron_dit'


def rms_norm(x, gain):
    xf = x.astype(jnp.float32)
    y = xf * lax.rsqrt(jnp.mean(xf * xf, axis=-1, keepdims=True) + RMS_EPS)
    return (y * gain.astype(jnp.float32)).astype(x.dtype)


def l2_normalise(u):
    return u * lax.rsqrt(jnp.sum(u * u, axis=-1, keepdims=True) + RMS_EPS)


def modulate(h, shift, scale):
    return h * (1 + scale) + shift


def swiglu(h, w_in, w_out):
    gate, up = jnp.split(h @ w_in, 2, axis=-1)
    return (jax.nn.silu(gate) * up) @ w_out


def macaron_half(x, mods, gain, w_in, w_out):
    shift, scale, gate = mods
    h = modulate(rms_norm(x, gain), shift, scale)
    return x + 0.5 * gate * swiglu(h, w_in, w_out)


def axial_rope_tables(n_tok, head_dim):
    n_freq = head_dim // 4
    inv_freq = ROPE_BASE ** (-jnp.arange(n_freq, dtype=jnp.float32) / n_freq)
    t = jnp.arange(n_tok)
    row = (t // GRID_W).astype(jnp.float32)[:, None] * inv_freq
    col = (t % GRID_W).astype(jnp.float32)[:, None] * inv_freq
    ang = jnp.concatenate([row, row, col, col], axis=-1)
    return jnp.cos(ang), jnp.sin(ang)


def apply_axial_rope(u, cos, sin):
    hd = u.shape[-1]
    half, quarter = hd // 2, hd // 4
    def rot_half(z):
        return jnp.concatenate([-z[..., quarter:], z[..., :quarter]], axis=-1)
    rotated = jnp.concatenate([rot_half(u[..., :half]), rot_half(u[..., half:])], axis=-1)
    return u * cos[:, None, :] + rotated * sin[:, None, :]


def centred_depthwise_conv(u, w):
    k_taps, ch = w.shape
    return lax.conv_general_dilated(
        u, w[:, None, :].astype(u.dtype), window_strides=(1,),
        padding=[(k_taps // 2, k_taps // 2)],
        dimension_numbers=('NWC', 'WIO', 'NWC'), feature_group_count=ch)


def na_prepare(p, q_gain, k_gain):
    b, t, _ = p.shape
    q, k, v = jnp.split(p[..., :3 * NA_WIDTH], 3, axis=-1)
    q = rms_norm(q.reshape(b, t, NA_HEADS, NA_HEAD_DIM), q_gain) * NA_HEAD_DIM ** -0.5
    k = rms_norm(k.reshape(b, t, NA_HEADS, NA_HEAD_DIM), k_gain)
    v = v.reshape(b, t, NA_HEADS, NA_HEAD_DIM)
    return q, k, v


def neighbourhood_attention(q, k, v, k_ctx, v_ctx, rpb):
    b, t, h, d = q.shape
    rows = t // GRID_W
    wr = min(NA_WIN_ROWS, rows)
    cols = jnp.arange(GRID_W)
    col_start = jnp.clip(cols - NA_WIN_COLS // 2, 0, GRID_W - NA_WIN_COLS)
    col_ok = (cols[None, :] >= col_start[:, None]) & (cols[None, :] < col_start[:, None] + NA_WIN_COLS)
    dc_idx = jnp.clip(cols[None, :] - cols[:, None] + NA_WIN_COLS - 1, 0, 2 * NA_WIN_COLS - 2)
    rpb_cols = rpb[:, :, dc_idx]
    k_grid = k.reshape(b, rows, GRID_W, h, d)
    v_grid = v.reshape(b, rows, GRID_W, h, d)
    q_rows = jnp.moveaxis(q.reshape(b, rows, GRID_W, h, d), 1, 0)
    n_lat = wr * GRID_W

    def row_block(args):
        q_r, r = args
        start = jnp.clip(r - wr // 2, 0, rows - wr)
        k_blk = lax.dynamic_slice_in_dim(k_grid, start, wr, axis=1)
        v_blk = lax.dynamic_slice_in_dim(v_grid, start, wr, axis=1)
        dr_idx = start + jnp.arange(wr) - r + NA_WIN_ROWS - 1
        bias = jnp.transpose(rpb_cols[:, dr_idx], (0, 2, 1, 3))
        s_lat = jnp.einsum('bqhd,brkhd->bhqrk', q_r, k_blk, preferred_element_type=jnp.float32)
        s_lat = jnp.where(col_ok[:, None, :], s_lat + bias.astype(jnp.float32), -jnp.inf)
        s_lat = s_lat.reshape(b, h, GRID_W, n_lat)
        s_ctx = jnp.einsum('bqhd,bkhd->bhqk', q_r, k_ctx, preferred_element_type=jnp.float32)
        p = jax.nn.softmax(jnp.concatenate([s_lat, s_ctx], axis=-1), axis=-1).astype(v.dtype)
        o = jnp.einsum('bhqk,bkhd->bqhd', p[..., :n_lat], v_blk.reshape(b, n_lat, h, d))
        return o + jnp.einsum('bhqk,bkhd->bqhd', p[..., n_lat:], v_ctx)

    o = lax.map(row_block, (q_rows, jnp.arange(rows)))
    return jnp.moveaxis(o, 0, 1).reshape(b, t, h * d)


def context_attention(q, k, v):
    b, t, h, d = q.shape
    s = jnp.einsum('bqhd,bkhd->bhqk', q, k, preferred_element_type=jnp.float32)
    p = jax.nn.softmax(s, axis=-1).astype(v.dtype)
    return jnp.einsum('bhqk,bkhd->bqhd', p, v).reshape(b, t, h * d)


def dn_prepare(p, conv_w):
    b, t, _ = p.shape
    off = 3 * NA_WIDTH
    qkv = jax.nn.silu(centred_depthwise_conv(p[..., off:off + 3 * DN_WIDTH], conv_w)).astype(jnp.float32)
    q, k, v = jnp.split(qkv, 3, axis=-1)
    q = l2_normalise(q.reshape(b, t, DN_HEADS, DN_HEAD_DIM)) * DN_HEAD_DIM ** -0.5
    k = l2_normalise(k.reshape(b, t, DN_HEADS, DN_HEAD_DIM))
    v = v.reshape(b, t, DN_HEADS, DN_HEAD_DIM)
    off = off + 3 * DN_WIDTH
    gate = p[..., off:off + DN_WIDTH]
    off = off + DN_WIDTH
    beta = jax.nn.sigmoid(p[..., off:off + 2 * DN_HEADS].astype(jnp.float32)).reshape(b, t, 2, DN_HEADS)
    a_raw = p[..., off + 2 * DN_HEADS:off + 4 * DN_HEADS].astype(jnp.float32).reshape(b, t, 2, DN_HEADS)
    return q, k, v, gate, beta, a_raw


def dn_log_decay(a_raw, a_log, dt_bias):
    return -jnp.exp(a_log.astype(jnp.float32)) * jax.nn.softplus(a_raw + dt_bias.astype(jnp.float32))


def gated_delta_rule_chunked(q, k, v, g, beta, state0):
    b, t, h, dk = q.shape
    dv = v.shape[-1]
    n_chunks = t // DN_CHUNK

    def chunks(u):
        u = u.reshape((b, n_chunks, DN_CHUNK, h) + u.shape[3:])
        return jnp.moveaxis(u, (1, 3), (0, 2))

    qc, kc, vc, bc = chunks(q), chunks(k), chunks(v), chunks(beta)
    gc = jnp.cumsum(chunks(g), axis=-1)
    idx = jnp.arange(DN_CHUNK)
    incl = idx[:, None] >= idx[None, :]
    strict = idx[:, None] > idx[None, :]
    gamma = jnp.exp(jnp.where(incl, gc[..., :, None] - gc[..., None, :], -jnp.inf))
    k_beta = kc * bc[..., None]
    a_mat = jnp.where(strict, jnp.einsum('nbhid,nbhjd->nbhij', k_beta, kc) * gamma, 0.0)
    rhs = jnp.concatenate([vc * bc[..., None], k_beta * jnp.exp(gc)[..., None]], axis=-1)
    sol = lax.linalg.triangular_solve(a_mat, rhs, left_side=True, lower=True, unit_diagonal=True)
    u, w = sol[..., :dv], sol[..., dv:]
    qk = jnp.einsum('nbhid,nbhjd->nbhij', qc, kc) * gamma

    def step(state, xs):
        q_i, k_i, u_i, w_i, qk_i, g_i = xs
        v_new = u_i - jnp.einsum('bhck,bhkv->bhcv', w_i, state)
        o_i = (jnp.einsum('bhck,bhkv->bhcv', q_i * jnp.exp(g_i)[..., None], state)
               + jnp.einsum('bhij,bhjv->bhiv', qk_i, v_new))
        g_last = g_i[..., -1:]
        state = (state * jnp.exp(g_last)[..., None]
                 + jnp.einsum('bhck,bhcv->bhkv', k_i * jnp.exp(g_last - g_i)[..., None], v_new))
        return state, o_i

    state, o = lax.scan(step, state0, (qc, kc, u, w, qk, gc))
    return jnp.moveaxis(o, (0, 2), (1, 3)).reshape(b, t, h, dv), state


def _rev(u, reverse):
    return u[:, ::-1] if reverse else u


def dn_output(o, gate, out_gain):
    b, t = o.shape[:2]
    g = jax.nn.silu(gate.reshape(b, t, DN_HEADS, DN_HEAD_DIM).astype(jnp.float32))
    return (rms_norm(o, out_gain) * g).reshape(b, t, DN_WIDTH).astype(gate.dtype)


def delta_heads(p_lat, p_ctx, conv_w, a_log, dt_bias, out_gain, rope_cos, rope_sin, ctx_out):
    ql, kl, vl, gate_l, beta_l, a_l = dn_prepare(p_lat, conv_w)
    qc, kc, vc, gate_c, beta_c, a_c = dn_prepare(p_ctx, conv_w)
    ql = apply_axial_rope(ql, rope_cos, rope_sin)
    kl = apply_axial_rope(kl, rope_cos, rope_sin)
    b = p_lat.shape[0]
    o_lat = jnp.zeros(vl.shape, jnp.float32)
    o_ctx = jnp.zeros(vc.shape, jnp.float32)
    for direction in range(2):
        rev = direction == 1
        g_l = dn_log_decay(a_l[..., direction, :], a_log[direction], dt_bias[direction])
        g_c = dn_log_decay(a_c[..., direction, :], a_log[direction], dt_bias[direction])
        s0 = jnp.zeros((b, DN_HEADS, DN_HEAD_DIM, DN_HEAD_DIM), jnp.float32)
        oc, s_ctx = gated_delta_rule_chunked(_rev(qc, rev), _rev(kc, rev), _rev(vc, rev),
                                             _rev(g_c, rev), _rev(beta_c[..., direction, :], rev), s0)
        ol, _ = gated_delta_rule_chunked(_rev(ql, rev), _rev(kl, rev), _rev(vl, rev),
                                         _rev(g_l, rev), _rev(beta_l[..., direction, :], rev), s_ctx)
        o_lat = o_lat + _rev(ol, rev)
        if ctx_out:
            o_ctx = o_ctx + _rev(oc, rev)
    y_lat = dn_output(o_lat, gate_l, out_gain)
    y_ctx = dn_output(o_ctx, gate_c, out_gain) if ctx_out else None
    return y_lat, y_ctx


def hybrid_mixer(h_lat, h_ctx, w_in, q_gain, k_gain, rpb, conv_w, a_log, dt_bias, out_gain,
                 rope_cos, rope_sin, ctx_out):
    p_lat = h_lat @ w_in
    p_ctx = h_ctx @ w_in
    qa_l, ka_l, va_l = na_prepare(p_lat, q_gain, k_gain)
    qa_c, ka_c, va_c = na_prepare(p_ctx, q_gain, k_gain)
    na_lat = neighbourhood_attention(qa_l, ka_l, va_l, ka_c, va_c, rpb)
    dn_lat, dn_ctx = delta_heads(p_lat, p_ctx, conv_w, a_log, dt_bias, out_gain,
                                 rope_cos, rope_sin, ctx_out)
    y_lat = jnp.concatenate([na_lat, dn_lat], axis=-1)
    if not ctx_out:
        return y_lat, None
    y_ctx = jnp.concatenate([context_attention(qa_c, ka_c, va_c), dn_ctx], axis=-1)
    return y_lat, y_ctx


def setup_inputs(seed: int = 0) -> dict:
    key = jax.random.key(seed)
    ks = jax.random.split(key, 24)
    L, D, F = DEPTH, D_MODEL, FFN_DIM

    def nrm(k, shape, scale):
        return jax.random.normal(k, shape, jnp.float32) * scale

    dt = jnp.exp(jax.random.uniform(ks[16], (L, 2, DN_HEADS), jnp.float32, np.log(1e-3), np.log(1e-1)))
    return {
        'x': nrm(ks[0], (BATCH, SEQ, D), 1.0),
        'c': nrm(ks[1], (BATCH, D), 1.0),
        'ctx': nrm(ks[2], (BATCH, CTX_LEN, D), 1.0),
        'c_ctx': nrm(ks[3], (D,), 1.0),
        'w_mod': nrm(ks[4], (L, D, N_MOD * D), 0.5 * D ** -0.5),
        'b_mod': nrm(ks[5], (L, N_MOD * D), 0.02),
        'norm_ffn1': 1.0 + nrm(ks[6], (L, D), 0.05),
        'w_ffn1_in': nrm(ks[7], (L, D, 2 * F), D ** -0.5),
        'w_ffn1_out': nrm(ks[8], (L, F, D), F ** -0.5),
        'norm_mix': 1.0 + nrm(ks[9], (L, D), 0.05),
        'w_in': nrm(ks[10], (L, D, IN_COLS), D ** -0.5),
        'na_q_gain': 1.0 + nrm(ks[11], (L, NA_HEAD_DIM), 0.05),
        'na_k_gain': 1.0 + nrm(ks[12], (L, NA_HEAD_DIM), 0.05),
        'na_rpb': nrm(ks[13], (L, NA_HEADS, 2 * NA_WIN_ROWS - 1, 2 * NA_WIN_COLS - 1), 0.1),
        'dn_conv': nrm(ks[14], (L, DN_CONV, 3 * DN_WIDTH), DN_CONV ** -0.5),
        'dn_a_log': jnp.log(jax.random.uniform(ks[15], (L, 2, DN_HEADS), jnp.float32, 1.0, 16.0)),
        'dn_dt_bias': dt + jnp.log(-jnp.expm1(-dt)),
        'dn_out_gain': 1.0 + nrm(ks[17], (L, DN_HEAD_DIM), 0.05),
        'w_out': nrm(ks[18], (L, MIX_WIDTH, D), MIX_WIDTH ** -0.5),
        'norm_ffn2': 1.0 + nrm(ks[19], (L, D), 0.05),
        'w_ffn2_in': nrm(ks[20], (L, D, 2 * F), D ** -0.5),
        'w_ffn2_out': nrm(ks[21], (L, F, D), F ** -0.5),
    }


def reference(x, c, ctx, c_ctx, w_mod, b_mod, norm_ffn1, w_ffn1_in, w_ffn1_out, norm_mix, w_in,
              na_q_gain, na_k_gain, na_rpb, dn_conv, dn_a_log, dn_dt_bias, dn_out_gain, w_out,
              norm_ffn2, w_ffn2_in, w_ffn2_out):
    n_tok = x.shape[1]
    rope_cos, rope_sin = axial_rope_tables(n_tok, DN_HEAD_DIM)
    silu_c = jax.nn.silu(c)
    silu_cc = jax.nn.silu(c_ctx)
    for l in range(DEPTH):
        ctx_out = l < DEPTH - 1
        mod_lat = jnp.split((silu_c @ w_mod[l] + b_mod[l])[:, None, :], N_MOD, axis=-1)
        mod_ctx = jnp.split((silu_cc @ w_mod[l] + b_mod[l])[None, None, :], N_MOD, axis=-1)
        x = macaron_half(x, mod_lat[0:3], norm_ffn1[l], w_ffn1_in[l], w_ffn1_out[l])
        ctx = macaron_half(ctx, mod_ctx[0:3], norm_ffn1[l], w_ffn1_in[l], w_ffn1_out[l])
        h_lat = modulate(rms_norm(x, norm_mix[l]), mod_lat[3], mod_lat[4])
        h_ctx = modulate(rms_norm(ctx, norm_mix[l]), mod_ctx[3], mod_ctx[4])
        y_lat, y_ctx = hybrid_mixer(h_lat, h_ctx, w_in[l], na_q_gain[l], na_k_gain[l], na_rpb[l],
                                    dn_conv[l], dn_a_log[l], dn_dt_bias[l], dn_out_gain[l],
                                    rope_cos, rope_sin, ctx_out)
        x = x + mod_lat[5] * (y_lat @ w_out[l])
        x = macaron_half(x, mod_lat[6:9], norm_ffn2[l], w_ffn2_in[l], w_ffn2_out[l])
        if ctx_out:
            ctx = ctx + mod_ctx[5] * (y_ctx @ w_out[l])
            ctx = macaron_half(ctx, mod_ctx[6:9], norm_ffn2[l], w_ffn2_in[l], w_ffn2_out[l])
    return x
```

```python
import numpy as np
import ml_dtypes
import concourse.bass as bass
import concourse.mybir as mybir
from concourse.bass_utils import run_bass_kernel_spmd
from contextlib import ExitStack

F32 = mybir.dt.float32
F32R = mybir.dt.float32r
BF16 = mybir.dt.bfloat16
AF = mybir.ActivationFunctionType
ALU = mybir.AluOpType
AX = mybir.AxisListType

NLAYERS = 4
D = 1024
FF = 2816
TLAT = 4096
TCTX = 256
TT = TLAT + TCTX
NTILE = TT // 128
INC = 3600
EPS = 1e-6


class Buf:
    __slots__ = ("name", "wev", "rev", "sg")

    def __init__(self, name, sg=None):
        self.name = name
        self.wev = {}
        self.rev = {}
        self.sg = sg if sg is not None else name


class V:
    __slots__ = ("ap", "bufs")

    def __init__(self, ap, bufs):
        self.ap = ap
        self.bufs = bufs


def DV(ap, *bufs):
    return V(ap, tuple(bufs))


class Tile:
    def __init__(self, h, name, sg=None):
        self.h = h
        self.buf = Buf(name, sg)

    def __getitem__(self, idx):
        return V(self.h[idx], (self.buf,))

    def v(self, ap):
        return V(ap, (self.buf,))


class Eng:
    def __init__(self, name, h, sem, key):
        self.name = name
        self.h = h
        self.sem = sem
        self.key = key
        self.cnt = 0
        self.seen = {}


class KB:
    def __init__(self, nc, es):
        self.nc = nc
        self.ges = es
        self.es = es
        self.sems = {}
        self.engs = {}
        for name, h in (("pe", nc.tensor), ("act", nc.scalar), ("dve", nc.vector),
                        ("pool", nc.gpsimd), ("sp", nc.sync)):
            sem = es.enter_context(nc.semaphore("s_" + name))
            key = "E" + name
            self.sems[key] = [sem, 0]
            self.engs[name] = Eng(name, h, sem, key)
        self.ninstr = 0
        self.uid = 0
        self.dbufs = {}

    def push_scope(self):
        if not hasattr(self, "stack"):
            self.stack = []
        self.stack.append(self.es)
        self.es = ExitStack()
        self.es.__enter__()

    def pop_scope(self):
        self.barrier_all()
        self.es.__exit__(None, None, None)
        self.es = self.stack.pop()

    def region(self, tile, name):
        return tile

    def sb(self, name, shape, dt, sg=None):
        self.uid += 1
        h = self.es.enter_context(self.nc.sbuf_tensor(f"{name}_{self.uid}", list(shape), dt))
        return Tile(h, name, sg)

    def ps(self, name, shape, dt=F32):
        h = self.es.enter_context(self.nc.psum_tensor(name, list(shape), dt))
        return Tile(h, name)

    def dbuf(self, key, sg=None):
        b = self.dbufs.get(key)
        if b is None:
            b = Buf("@" + str(key), sg)
            self.dbufs[key] = b
        return b

    def dsem_key(self, buf):
        key = "D" + buf.sg
        if key not in self.sems:
            sem = self.ges.enter_context(self.nc.semaphore("d%d" % len(self.sems)))
            self.sems[key] = [sem, 0]
        return key

    def _waits(self, E, rb, wb):
        need = {}
        for b in rb:
            for k_, v in b.wev.items():
                if need.get(k_, 0) < v:
                    need[k_] = v
        for b in wb:
            for k_, v in b.wev.items():
                if need.get(k_, 0) < v:
                    need[k_] = v
            for k_, v in b.rev.items():
                if need.get(k_, 0) < v:
                    need[k_] = v
        for k_, v in need.items():
            if E.name == "pe" and k_ == E.key:
                continue
            if k_[0] == "D":
                v = self.sems[k_][1]
            if E.seen.get(k_, 0) >= v:
                continue
            E.h.wait_ge(self.sems[k_][0], v)
            E.seen[k_] = v

    def op(self, eng, fn, reads, writes, sig=True):
        E = self.engs[eng]
        rb = [b for v in reads for b in v.bufs]
        wb = [b for v in writes for b in v.bufs]
        self._waits(E, rb, wb)
        ins = fn(E.h)
        if sig:
            E.cnt += 1
            ins.then_inc(E.sem, 1)
            self.sems[E.key][1] = E.cnt
            ev = E.cnt
        else:
            ev = E.cnt + 1
        for b in rb:
            b.rev[E.key] = ev
        for b in wb:
            b.wev = {E.key: ev}
            b.rev = {}
        self.ninstr += 1
        return ins

    def dma(self, eng, out, in_, sembuf=None, **kw):
        E = self.engs[eng]
        rb = list(in_.bufs)
        wb = list(out.bufs)
        self._waits(E, rb, wb)
        if sembuf is None:
            sembuf = wb[0] if wb[0].name[0] != "@" else rb[0]
        key = self.dsem_key(sembuf)
        ins = E.h.dma_start(out=out.ap, in_=in_.ap, **kw)
        self.sems[key][1] += 16
        ins.then_inc(self.sems[key][0], 16)
        val = self.sems[key][1]
        for b in rb:
            b.rev[key] = val
        for b in wb:
            b.wev = {key: val}
            b.rev = {}
        self.ninstr += 1
        return ins

    def barrier_all(self):
        for E in self.engs.values():
            for k_, (sem, tot) in self.sems.items():
                if tot == 0 or (k_ == E.key and E.name == "pe"):
                    continue
                if E.seen.get(k_, 0) >= tot:
                    continue
                E.h.wait_ge(sem, tot)
                E.seen[k_] = tot

    def finish(self):
        E = self.engs["sp"]
        for k_, (sem, tot) in self.sems.items():
            if tot == 0 or E.seen.get(k_, 0) >= tot:
                continue
            E.h.wait_ge(sem, tot)
            E.seen[k_] = tot

    def _pe_guard(self, is_r):
        if getattr(self, "pe_last_r", False) and not is_r and getattr(self, "dummy", None) is not None:
            o, l, r = self.dummy
            self.op("pe", lambda h: h.matmul(o.ap, l.ap, r.ap, start=True, stop=True), [l, r], [o])
        self.pe_last_r = is_r

    def mm(self, out, lhsT, rhs, start=True, stop=True, sig=None, **kw):
        self._pe_guard(lhsT.ap.dtype == F32R)
        if sig is None:
            sig = bool(stop)
        return self.op("pe", lambda h: h.matmul(out.ap, lhsT.ap, rhs.ap, start=start, stop=stop, **kw),
                       [lhsT, rhs], [out], sig=sig)

    def tr(self, out, in_, ident):
        self._pe_guard(False)
        return self.op("pe", lambda h: h.transpose(out.ap, in_.ap, ident.ap), [in_, ident], [out])

    def act(self, out, in_, func, bias=None, scale=None, accum=None):
        kw = {}
        reads = [in_]
        writes = [out]
        if bias is not None:
            if isinstance(bias, V):
                kw["bias"] = bias.ap
                reads.append(bias)
            else:
                kw["bias"] = bias
        if scale is not None:
            if isinstance(scale, V):
                kw["scale"] = scale.ap
                reads.append(scale)
            else:
                kw["scale"] = scale
        if accum is not None:
            kw["accum_out"] = accum.ap
            writes.append(accum)
        return self.op("act", lambda h: h.activation(out.ap, in_.ap, func, **kw), reads, writes)

    def tt(self, out, a, b, op, eng="dve"):
        return self.op(eng, lambda h: h.tensor_tensor(out.ap, a.ap, b.ap, op), [a, b], [out])

    def ts(self, out, a, s1, s2, op0, op1=None, eng="dve"):
        reads = [a]
        x1 = s1.ap if isinstance(s1, V) else s1
        x2 = s2.ap if isinstance(s2, V) else s2
        if isinstance(s1, V):
            reads.append(s1)
        if isinstance(s2, V):
            reads.append(s2)
        kw = {}
        if op1 is not None:
            kw["op1"] = op1
        return self.op(eng, lambda h: h.tensor_scalar(out.ap, a.ap, x1, x2, op0, **kw), reads, [out])

    def stt(self, out, a, s, b, op0, op1, eng="dve"):
        reads = [a, b]
        x = s.ap if isinstance(s, V) else s
        if isinstance(s, V):
            reads.append(s)
        return self.op(eng, lambda h: h.scalar_tensor_tensor(out.ap, a.ap, x, b.ap, op0, op1), reads, [out])

    def copy(self, out, in_, eng="dve"):
        return self.op(eng, lambda h: h.tensor_copy(out.ap, in_.ap), [in_], [out])

    def memset(self, out, val, eng="dve"):
        return self.op(eng, lambda h: h.memset(out.ap, val), [], [out])

    def recip(self, out, in_, eng="dve"):
        return self.op(eng, lambda h: h.reciprocal(out.ap, in_.ap), [in_], [out])

    def reduce(self, out, in_, op, axis=AX.X, eng="dve"):
        return self.op(eng, lambda h: h.tensor_reduce(out.ap, in_.ap, axis, op), [in_], [out])


C_U = 0
C_NEG = 256
C_LM = 512
C_RM = 512 + 14 * 128
C_CM = C_RM + 128
CST_COLS = C_CM + 12 * 128

NA_DR0 = [1, 3, 5, 7, 9, 11, 13, 3, 5, 7, 9, 11]


def host_consts():
    i = np.arange(128)
    cst = np.zeros((128, CST_COLS), np.float32)
    cst[:, C_U:C_U + 128] = (i[:, None] <= i[None, :])
    cst[:, C_U + 128:C_U + 256] = (i[:, None] >= i[None, :])
    jj, ii = i[:, None], i[None, :]
    cst[:, C_NEG:C_NEG + 128] = np.where(ii >= jj, 0.0, -30000.0)
    cst[:, C_NEG + 128:C_NEG + 256] = np.where(ii <= jj, 0.0, -30000.0)
    for k in range(7):
        b = 2 ** k
        same = (ii // (2 * b)) == (jj // (2 * b))
        cst[:, C_LM + k * 128:C_LM + (k + 1) * 128] = same & ((ii % (2 * b)) >= b) & ((jj % (2 * b)) < b)
        cst[:, C_LM + (7 + k) * 128:C_LM + (8 + k) * 128] = same & ((ii % (2 * b)) < b) & ((jj % (2 * b)) >= b)
    rm = np.zeros((128, 128), np.float32)
    for d in range(128):
        q = (d % 64) // 32
        if q == 0:
            rm[d + 32, d] = -1.0
        else:
            rm[d - 32, d] = 1.0
    cst[:, C_RM:C_RM + 128] = rm
    cols = np.arange(64)
    cs = np.clip(cols - 8, 0, 48)
    col_ok = (cols[None, :] >= cs[:, None]) & (cols[None, :] < cs[:, None] + 16)
    nm = np.zeros((2, 64, 12, 2, 64), np.float32)
    for s, dr0 in enumerate(NA_DR0):
        for kl in range(2):
            for ql in range(2):
                dr = dr0 + kl - ql
                ok = 1.0 if (s < 7 or (3 <= dr <= 10)) else 0.0
                nm[kl, :, s, ql, :] = col_ok.T * ok
    cst[:, C_CM:C_CM + 12 * 128] = nm.reshape(128, 12 * 128)
    inv = 10000.0 ** (-np.arange(32, dtype=np.float64) / 32.0)
    t = np.arange(TLAT)
    row = (t // 64).astype(np.float64)[:, None] * inv
    col = (t % 64).astype(np.float64)[:, None] * inv
    ang = np.concatenate([row, row, col, col], axis=-1)
    rope = np.stack([np.cos(ang).T, np.sin(ang).T], axis=1).astype(np.float32)
    return cst, np.ascontiguousarray(rope)


SV_NORM = 0
SV_BMOD = SV_NORM + NLAYERS * 24
SV_CC = SV_BMOD + NLAYERS * 72
SV_QKG = SV_CC + 16
SV_CONV = SV_QKG + NLAYERS * 2
SV_ALOG = SV_CONV + NLAYERS * 60
SV_DTB = SV_ALOG + NLAYERS * 8
SV_OG = SV_DTB + NLAYERS * 8
SV_COLS = SV_OG + NLAYERS * 128


def host_smallvec(inp, b):
    sv = np.zeros((128, SV_COLS), np.float32)
    for l in range(NLAYERS):
        for n, nm in enumerate(("norm_ffn1", "norm_mix", "norm_ffn2")):
            sv[:, SV_NORM + (l * 3 + n) * 8:SV_NORM + (l * 3 + n + 1) * 8] = inp[nm][l].reshape(8, 128).T
        sv[:, SV_BMOD + l * 72:SV_BMOD + (l + 1) * 72] = inp["b_mod"][l].reshape(72, 128).T
        sv[:, SV_QKG + l * 2] = np.tile(inp["na_q_gain"][l], 2)
        sv[:, SV_QKG + l * 2 + 1] = np.tile(inp["na_k_gain"][l], 2)
        sv[:, SV_CONV + l * 60:SV_CONV + (l + 1) * 60] = \
            inp["dn_conv"][l].reshape(5, 12, 128).transpose(2, 0, 1).reshape(128, 60)
        sv[:, SV_ALOG + l * 8:SV_ALOG + (l + 1) * 8] = inp["dn_a_log"][l].reshape(1, 8)
        sv[:, SV_DTB + l * 8:SV_DTB + (l + 1) * 8] = inp["dn_dt_bias"][l].reshape(1, 8)
        sv[:, SV_OG + l * 128:SV_OG + (l + 1) * 128] = inp["dn_out_gain"][l].reshape(1, 128)
    cc = np.stack([inp["c"][b], inp["c_ctx"]], axis=-1)
    sv[:, SV_CC:SV_CC + 16] = cc.reshape(8, 128, 2).transpose(1, 0, 2).reshape(128, 16)
    return sv


def host_rpb(rpb):
    kc = np.arange(64)[:, None]
    qc = np.arange(64)[None, :]
    dc = np.clip(kc - qc + 15, 0, 30)
    out = np.zeros((rpb.shape[0], 8, 2, 64, 12, 2, 64), np.float32)
    for s, dr0 in enumerate(NA_DR0):
        for kl in range(2):
            for ql in range(2):
                dr = int(np.clip(dr0 + kl - ql, 0, 14))
                out[:, :, kl, :, s, ql, :] = rpb[:, :, dr][:, :, dc]
    return np.ascontiguousarray(out.reshape(rpb.shape[0], 8, 128, 12 * 128))


BLOCKS = [(i * 512, 512, 0) for i in range(8)] + [(TLAT, TCTX, 1)]


class Prog:
    def __init__(self, nl=NLAYERS, dbg=(), dump=()):
        self.nl = nl
        self.dbg = set(dbg)
        dump = set(dump)
        nc = bass.Bass("TRN2", target_bir_lowering=False)
        self.nc = nc
        di = lambda n, s, dt=F32: nc.dram_tensor(n, list(s), dt, kind="ExternalInput")
        self.x_d = di("x", [TLAT, D])
        self.ctx_d = di("ctx", [TCTX, D])
        self.sv_d = di("sv", [128, SV_COLS])
        self.cst_d = di("cst", [128, CST_COLS])
        self.rope_d = di("rope", [128, 2, TLAT])
        self.rpb_d = di("rpbg", [NLAYERS, 8, 128, 12 * 128])
        self.wmod_d = di("w_mod", [NLAYERS, D, 9 * D])
        self.w1_d = [di("w_ffn1_in", [NLAYERS, D, 2 * FF]), di("w_ffn2_in", [NLAYERS, D, 2 * FF])]
        self.w2_d = [di("w_ffn1_out", [NLAYERS, FF, D]), di("w_ffn2_out", [NLAYERS, FF, D])]
        self.win_d = di("w_in", [NLAYERS, D, INC])
        self.wout_d = di("w_out", [NLAYERS, D, D])
        self.out_d = nc.dram_tensor("out", [TLAT, D], F32, kind="ExternalOutput")
        ds = lambda n, s, dt: nc.dram_tensor(n, list(s), dt, kind=("ExternalOutput" if n in dump else "Internal"))
        self.XT = ds("XT", [D, TT], F32)
        self.QKT = ds("QKT", [8, 128, TT], BF16)
        self.VA = ds("VA", [TT, 512], BF16)
        self.GQ = ds("GQ", [12, 128, TT], F32)
        self.GG = ds("GG", [TT, 512], F32)
        self.BD = ds("BD", [TT, 16], F32)
        self.YT = ds("YT", [8, 128, TT], BF16)
        self.W1s = [[ds(f"W1s_{l}_{w}", [22, 128, 8, 256], BF16) for w in range(2)] for l in range(nl)]
        self.W2s = [[ds(f"W2s_{l}_{w}", [8, 128, 22, 128], BF16) for w in range(2)] for l in range(nl)]
        self.WA = [ds(f"WA_{l}", [20, 128, 8, 128], BF16) for l in range(nl)]
        self.WB = [ds(f"WB_{l}", [128, 8, 1040], BF16) for l in range(nl)]
        self.WO = [ds(f"WO_{l}", [8, 128, 8, 128], BF16) for l in range(nl)]
        self.dbg_out = {}
        self.e_eng = "pool" if "epool" in self.dbg else "dve"
        self.dbg_t = {}
        if "gdbg" in self.dbg:
            do = lambda n, sh, dt: nc.dram_tensor("dbg_" + n, list(sh), dt, kind="ExternalOutput")
            self.dbg_t = dict(dq=do("dq", [128, TT], BF16), dk=do("dk", [128, TT], BF16), dv=do("dv", [128, NTILE, 128], F32),
                              dT=do("dT", [128, 2, NTILE, 128], BF16), dQKM=do("dQKM", [128, 2, NTILE, 128], BF16),
                              dKTL=do("dKTL", [128, 2, NTILE, 128], BF16), dOA=do("dOA", [128, NTILE, 128], F32),
                              dE=do("dE", [128, 4, 2, NTILE], F32), dGL=do("dGL", [128, NTILE, 8], F32),
                              dBETA=do("dBETA", [128, NTILE, 8], F32))

    def dbg_tensor(self, name, shape, dt=F32):
        t = self.nc.dram_tensor("dbg_" + name, list(shape), dt, kind="ExternalOutput")
        self.dbg_out[name] = t
        return t

    def build(self):
        nc = self.nc
        with ExitStack() as es:
            k = KB(nc, es)
            self.k = k
            self.PS = [k.ps(f"ps{i}", [128, 512]) for i in range(8)]
            self.psd = k.region(self.PS[7], "psd")
            self.setup_consts()
            k.dummy = None
            for l in range(self.nl):
                self.convert_weights(l)
            self.compute_mods([0])
            self.init_xt()
            for l in range(self.nl + 1):
                self.row_pass(l)
                if l < self.nl:
                    if "stop_inproj" in self.dbg and l == 0:
                        break
                    self.mixer(l)
                    if ("stop_na" in self.dbg or "stop_gdn" in self.dbg) and l == 0:
                        break
            k.finish()
        return nc

    def setup_consts(self):
        k = self.k
        self.cst = k.sb("cst", [128, CST_COLS], F32)
        k.dma("sp", self.cst[:, :], DV(self.cst_d[:, :], k.dbuf("cst_d")))
        self.sv = k.sb("sv", [128, SV_COLS], F32)
        k.dma("sp", self.sv[:, :], DV(self.sv_d[:, :], k.dbuf("sv_d")))
        self.identf = k.sb("identf", [128, 128], F32)
        k.memset(self.identf[:, :], 0.0)
        k.op("pool", lambda h: h.affine_select(out=self.identf.h[:, :], in_=self.identf.h[:, :],
                                               pattern=[[-1, 128]], compare_op=ALU.not_equal, fill=1.0,
                                               base=0, channel_multiplier=1),
             [self.identf[:, :]], [self.identf[:, :]])
        self.identb = k.sb("identb", [128, 128], BF16)
        k.copy(self.identb[:, :], self.identf[:, :])
        self.onesb = k.sb("onesb", [128, 128], BF16)
        k.memset(self.onesb[:, :], 1.0)
        self.onesf = k.sb("onesf", [128, 128], F32)
        k.memset(self.onesf[:, :], 1.0)
        self.epsc = k.sb("epsc", [128, 1], F32)
        k.memset(self.epsc[:, :], EPS)
        self.modT = [k.sb(f"modT{l}", [128, 72, 2], F32) for l in range(self.nl)]
        self.modA = [k.sb(f"modA{l}", [128, 3, 8, 2], F32) for l in range(self.nl)]
        self.modH = [k.sb(f"modH{l}", [128, 3, 8, 2], F32) for l in range(self.nl)]

    def convert_weights(self, l):
        k = self.k
        for w in range(2):
            src = self.w1_d[w][l].rearrange("(c p) (two j n) -> j p c two n", p=128, two=2, j=22, n=128)
            sg = f"cv1_{l}_{w}"
            for j in range(22):
                b = k.dbuf(("W1s", l, w, j), sg)
                for two in range(2):
                    k.dma("pool", DV(self.W1s[l][w][j][:, :, two * 128:(two + 1) * 128], b),
                          DV(src[j][:, :, two, :], k.dbuf("win")), sembuf=b)
            src = self.w2_d[w][l].rearrange("(j p) (f n) -> f p j n", p=128, n=128)
            sg = f"cv2_{l}_{w}"
            for f in range(8):
                b = k.dbuf(("W2s", l, w, f), sg)
                k.dma("pool", DV(self.W2s[l][w][f], b), DV(src[f], k.dbuf("win")), sembuf=b)
        sg = f"cva_{l}"
        colsA = [j * 128 for j in range(8)] + [1536 + j * 128 for j in range(12)]
        srcw = self.win_d[l].rearrange("(c p) n -> p c n", p=128)
        for j in range(20):
            b = k.dbuf(("WA", l, j), sg)
            k.dma("pool", DV(self.WA[l][j], b), DV(srcw[:, :, colsA[j]:colsA[j] + 128], k.dbuf("win")), sembuf=b)
        b = k.dbuf(("WB", l), sg)
        k.dma("pool", DV(self.WB[l][:, :, 0:512], b), DV(srcw[:, :, 1024:1536], k.dbuf("win")), sembuf=b)
        k.dma("pool", DV(self.WB[l][:, :, 512:1040], b), DV(srcw[:, :, 3072:3600], k.dbuf("win")), sembuf=b)
        src = self.wout_d[l].rearrange("(c p) (f n) -> f p c n", p=128, n=128)
        for f in range(8):
            b = k.dbuf(("WO", l, f), sg)
            k.dma("pool", DV(self.WO[l][f], b), DV(src[f], k.dbuf("win")), sembuf=b)

    def compute_mods(self, layers):
        k = self.k
        sv = self.sv
        k.push_scope()
        cs = k.sb("mod_cs", [128, 8, 2], F32)
        k.act(cs[:, :, :], sv.v(sv.h[:, SV_CC:SV_CC + 16].rearrange("p (c s) -> p c s", s=2)), AF.Silu)
        wt = [k.sb(f"mod_w{i}", [128, 1024], F32) for i in range(3)]
        n = 0
        for l in layers:
            for cb in range(9):
                for kc in range(8):
                    t = wt[n % 3]
                    n += 1
                    k.dma("sp", t[:, :], DV(self.wmod_d[l][kc * 128:(kc + 1) * 128, cb * 1024:(cb + 1) * 1024],
                                            k.dbuf("win")))
                    for jj in range(8):
                        k.mm(self.PS[jj][:, 0:2], t[:, jj * 128:(jj + 1) * 128], cs[:, kc, :],
                             start=(kc == 0), stop=(kc == 7), sig=True)
                for jj in range(8):
                    j = cb * 8 + jj
                    k.ts(self.modT[l][:, j, :], self.PS[jj][:, 0:2],
                         sv[:, SV_BMOD + l * 72 + j:SV_BMOD + l * 72 + j + 1], None, ALU.add)
            for nn in range(3):
                g = sv.v(sv.h[:, SV_NORM + (l * 3 + nn) * 8:SV_NORM + (l * 3 + nn + 1) * 8]
                         .unsqueeze(2).broadcast_to([128, 8, 2]))
                sc = self.modT[l][:, (nn * 3 + 1) * 8:(nn * 3 + 2) * 8, :]
                k.stt(self.modA[l][:, nn, :, :], sc, 1.0, g, ALU.add, ALU.mult)
                gt = self.modT[l][:, (nn * 3 + 2) * 8:(nn * 3 + 3) * 8, :]
                k.ts(self.modH[l][:, nn, :, :], gt, 0.5 if nn != 1 else 1.0, None, ALU.mult)
        k.pop_scope()

    def init_xt(self):
        k = self.k
        k.push_scope()
        xin = [k.sb(f"ix{i}", [128, D], F32) for i in range(2)]
        xo = [k.sb(f"ixo{i}", [128, 8, 128], F32) for i in range(2)]
        for ti in range(NTILE):
            t = xin[ti % 2]
            if ti < 32:
                src = DV(self.x_d[ti * 128:(ti + 1) * 128, :], k.dbuf("x_d"))
            else:
                src = DV(self.ctx_d[(ti - 32) * 128:(ti - 31) * 128, :], k.dbuf("x_d"))
            k.dma("sp", t[:, :], src)
            o = xo[ti % 2]
            for half in range(2):
                ps = self.PS[(ti % 2) * 2 + half]
                for c4 in range(4):
                    c = half * 4 + c4
                    k.tr(ps[:, c4 * 128:(c4 + 1) * 128], t[:, c * 128:(c + 1) * 128], self.identf[:, :])
                eng_copy = k.copy if half == 0 else (lambda o_, i_: k.act(o_, i_, AF.Copy))
                eng_copy(o.v(o.h[:, half * 4:(half + 1) * 4, :]),
                         ps.v(ps.h[:, :].rearrange("p (c n) -> p c n", n=128)))
            blk = min(ti // 4, 8)
            dst = self.XT[:, ti * 128:(ti + 1) * 128].rearrange("(c p) n -> p c n", p=128)
            k.dma("sp", DV(dst, *[k.dbuf(("XT", blk, c, ti % 4)) for c in range(8)]), o[:, :, :])
        k.pop_scope()

    def xbufs(self, blk, c):
        return [self.k.dbuf(("XT", blk, c, q)) for q in range(4)]

    def row_pass(self, l):
        k = self.k
        nl = self.nl
        k.push_scope()
        P = self
        P.xT = [[k.sb(f"xT{s}_{c}", [128, 512], F32, sg=f"xT{s}") for c in range(8)] for s in range(2)]
        P.hT = [k.sb(f"hT{c}", [128, 512], BF16) for c in range(8)]
        P.actT = [k.sb(f"actT{j}", [128, 512], BF16) for j in range(22)]
        P.sq = [k.sb(f"sq{i}", [128, 512], BF16) for i in range(2)]
        P.rs = k.sb("rs", [128, 512], F32)
        P.tmp = [k.sb(f"tmp{i}", [128, 512], F32) for i in range(2)]
        P.sgt = [k.sb(f"sgt{i}", [128, 512], F32) for i in range(2)]
        P.w1t = [k.sb(f"w1t{i}", [128, 8, 256], BF16) for i in range(5)]
        P.w2t = [k.sb(f"w2t{i}", [128, 22, 128], BF16) for i in range(3)]
        P.w1n = 0
        P.w2n = 0
        if l > 0:
            P.yT = [k.sb(f"yT{s}", [128, 8, 512], BF16) for s in range(2)]
            P.wot = [k.sb(f"wot{i}", [128, 8, 128], BF16) for i in range(2)]
        if l < nl:
            P.wat = [k.sb(f"wat{i}", [128, 8, 128], BF16) for i in range(4)]
            P.wbt = k.sb("wbt", [128, 8, 1040], BF16)
            k.dma("sp", P.wbt[:, :, :], DV(self.WB[l][:, :, :], k.dbuf(("WB", l))))
            P.blk64 = k.sb("blk64", [128, 128], BF16)
            k.memset(P.blk64[:, :], 0.0)
            k.memset(P.blk64[0:64, 0:64], 1.0)
            k.memset(P.blk64[64:128, 64:128], 1.0)
            P.qkg = k.sb("qkg", [128, 2], F32)
            k.ts(P.qkg[:, 0:1], self.sv[:, SV_QKG + 2 * l:SV_QKG + 2 * l + 1], 0.125, None, ALU.mult)
            k.copy(P.qkg[:, 1:2], self.sv[:, SV_QKG + 2 * l + 1:SV_QKG + 2 * l + 2])
            P.rq = [k.sb(f"rq{i}", [128, 512], F32) for i in range(2)]
            P.qo = [k.sb(f"qo{i}", [128, 512], BF16) for i in range(2)]
            P.go = [k.sb(f"go{i}", [128, 512], F32) for i in range(2)]
            P.vo = [k.sb(f"vo{i}", [128, 512], BF16) for i in range(2)]
            P.gto = [k.sb(f"gto{i}", [128, 512], F32) for i in range(2)]
            P.bdo = [k.sb(f"bdo{i}", [128, 16], F32) for i in range(2)]
        else:
            P.oo = [k.sb(f"oo{i}", [128, D], F32) for i in range(2)]
        blocks = list(range(9)) if l < nl else list(range(8))

        def load(bi):
            blk = blocks[bi]
            t0, N, s = BLOCKS[blk]
            slot = bi % 2
            for c in range(8):
                k.dma("sp", P.xT[slot][c][:, :N], DV(self.XT[c * 128:(c + 1) * 128, t0:t0 + N], *self.xbufs(blk, c)))
            if l > 0:
                src = self.YT[:, :, t0:t0 + N].rearrange("c p n -> p c n")
                k.dma("sp", P.yT[slot][:, :, :N], DV(src, *[k.dbuf(("YT", c, blk, q)) for c in range(8) for q in range(4)]))

        load(0)
        for bi, blk in enumerate(blocks):
            if bi + 1 < len(blocks):
                load(bi + 1)
            t0, N, s = BLOCKS[blk]
            slot = bi % 2
            xT = P.xT[slot]
            if l > 0:
                lp = l - 1
                for f in range(8):
                    wo = P.wot[f % 2]
                    k.dma("sp", wo[:, :, :], DV(self.WO[lp][f], k.dbuf(("WO", lp, f))))
                    po = self.PS[5 + f % 2]
                    for c in range(8):
                        k.mm(po[:, :N], wo[:, c, :], P.yT[slot][:, c, :N], start=(c == 0), stop=(c == 7))
                    k.stt(xT[f][:, :N], po[:, :N], self.modH[lp][:, 1, f, s:s + 1], xT[f][:, :N], ALU.mult, ALU.add)
                self.norm(lp, 2, xT, N, s)
                self.ffn(lp, 1, xT, N, s, 2)
            if l < nl:
                self.norm(l, 0, xT, N, s)
                self.ffn(l, 0, xT, N, s, 0)
                for c in range(8):
                    k.dma("sp", DV(self.XT[c * 128:(c + 1) * 128, t0:t0 + N], *self.xbufs(blk, c)), xT[c][:, :N])
                if "x_ffn1" in self.dbg and l == 0:
                    pass
                self.norm(l, 1, xT, N, s)
                self.inproj(l, blk)
            else:
                for tt in range(N // 128):
                    o = P.oo[tt % 2]
                    for half in range(2):
                        ps = self.PS[1 + (tt % 2) * 2 + half]
                        for c4 in range(4):
                            c = half * 4 + c4
                            k.tr(ps[:, c4 * 128:(c4 + 1) * 128], xT[c][:, tt * 128:(tt + 1) * 128], self.identf[:, :])
                        if half == 0:
                            k.copy(o[:, 0:512], ps[:, :])
                        else:
                            k.act(o[:, 512:1024], ps[:, :], AF.Copy)
                    k.dma("sp", DV(self.out_d[t0 + tt * 128:t0 + (tt + 1) * 128, :], k.dbuf(("out", blk, tt))), o[:, :])
        k.pop_scope()

    def norm(self, l, nn, xT, N, s):
        k = self.k
        P = self
        ps = self.PS[0]
        for c in range(8):
            sq = P.sq[c % 2]
            k.act(sq[:, :N], xT[c][:, :N], AF.Square)
            k.mm(ps[:, :N], self.onesb[:, :], sq[:, :N], start=(c == 0), stop=(c == 7), sig=True)
        k.act(P.rs[:, :N], ps[:, :N], AF.Sqrt, scale=1.0 / D, bias=self.epsc[:, 0:1])
        k.recip(P.rs[:, :N], P.rs[:, :N])
        for c in range(8):
            tmp = P.tmp[c % 2]
            k.stt(tmp[:, :N], xT[c][:, :N], self.modA[l][:, nn, c, s:s + 1], P.rs[:, :N], ALU.mult, ALU.mult)
            k.act(P.hT[c][:, :N], tmp[:, :N], AF.Identity, bias=self.modT[l][:, nn * 24 + c, s:s + 1])

    def ffn(self, l, w, xT, N, s, nn):
        k = self.k
        P = self
        for j in range(22):
            wt = P.w1t[P.w1n % 5]
            P.w1n += 1
            k.dma("sp", wt[:, :, :], DV(self.W1s[l][w][j], k.dbuf(("W1s", l, w, j))))
            pg = self.PS[1 + 2 * (j % 2)]
            pu = self.PS[2 + 2 * (j % 2)]
            for c in range(8):
                k.mm(pg[:, :N], wt[:, c, 0:128], P.hT[c][:, :N], start=(c == 0), stop=(c == 7))
            for c in range(8):
                k.mm(pu[:, :N], wt[:, c, 128:256], P.hT[c][:, :N], start=(c == 0), stop=(c == 7))
            sg = P.sgt[j % 2]
            k.act(sg[:, :N], pg[:, :N], AF.Silu)
            k.tt(P.actT[j][:, :N], sg[:, :N], pu[:, :N], ALU.mult)
        for f in range(8):
            w2 = P.w2t[P.w2n % 3]
            P.w2n += 1
            k.dma("sp", w2[:, :, :], DV(self.W2s[l][w][f], k.dbuf(("W2s", l, w, f))))
            po = self.PS[5 + f % 2]
            for j in range(22):
                k.mm(po[:, :N], w2[:, j, :], P.actT[j][:, :N], start=(j == 0), stop=(j == 21))
            k.stt(xT[f][:, :N], po[:, :N], self.modH[l][:, nn, f, s:s + 1], xT[f][:, :N], ALU.mult, ALU.add)

    def inproj(self, l, blk):
        k = self.k
        P = self
        t0, N, s = BLOCKS[blk]
        n = 0
        for j in range(20):
            wa = P.wat[j % 4]
            k.dma("sp", wa[:, :, :], DV(self.WA[l][j], k.dbuf(("WA", l, j))))
            ps = self.PS[1 + 2 * (j % 2)]
            for c in range(8):
                k.mm(ps[:, :N], wa[:, c, :], P.hT[c][:, :N], start=(c == 0), stop=(c == 7))
            if j < 8:
                ps2 = self.PS[2 + 2 * (j % 2)]
                sq = P.sq[j % 2]
                k.act(sq[:, :N], ps[:, :N], AF.Square)
                k.mm(ps2[:, :N], P.blk64[:, :], sq[:, :N])
                rq = P.rq[j % 2]
                k.act(rq[:, :N], ps2[:, :N], AF.Sqrt, scale=1.0 / 64, bias=self.epsc[:, 0:1])
                k.recip(rq[:, :N], rq[:, :N])
                qo = P.qo[j % 2]
                gcol = P.qkg[:, 0:1] if j < 4 else P.qkg[:, 1:2]
                k.stt(qo[:, :N], ps[:, :N], gcol, rq[:, :N], ALU.mult, ALU.mult)
                k.dma("sp", DV(self.QKT[j][:, t0:t0 + N], k.dbuf(("QKT", j, blk))), qo[:, :N])
            else:
                go = P.go[j % 2]
                if j % 2 == 0:
                    k.copy(go[:, :N], ps[:, :N])
                else:
                    k.act(go[:, :N], ps[:, :N], AF.Copy)
                k.dma("sp", DV(self.GQ[j - 8][:, t0:t0 + N], k.dbuf(("GQ", j - 8, blk))), go[:, :N])
        for tt in range(N // 128):
            r0 = t0 + tt * 128
            ti = r0 // 128
            pv = self.PS[5]
            pgt = self.PS[6]
            pbd = self.PS[7]
            for c in range(8):
                k.mm(pv[:, :], P.hT[c][:, tt * 128:(tt + 1) * 128], P.wbt[:, c, 0:512], start=(c == 0), stop=(c == 7))
            for c in range(8):
                k.mm(pgt[:, :], P.hT[c][:, tt * 128:(tt + 1) * 128], P.wbt[:, c, 512:1024], start=(c == 0), stop=(c == 7))
            for c in range(8):
                k.mm(pbd[:, 0:16], P.hT[c][:, tt * 128:(tt + 1) * 128], P.wbt[:, c, 1024:1040], start=(c == 0), stop=(c == 7))
            vo = P.vo[tt % 2]
            k.copy(vo[:, :], pv[:, :])
            k.dma("sp", DV(self.VA[r0:r0 + 128, :], k.dbuf(("VA", ti))), vo[:, :])
            gto = P.gto[tt % 2]
            k.act(gto[:, :], pgt[:, :], AF.Silu)
            k.dma("sp", DV(self.GG[r0:r0 + 128, :], k.dbuf(("GG", ti))), gto[:, :])
            bdo = P.bdo[tt % 2]
            k.copy(bdo[:, :], pbd[:, 0:16])
            k.dma("sp", DV(self.BD[r0:r0 + 128, :], k.dbuf(("BD", ti))), bdo[:, :])

    def mixer(self, l):
        self.na_phase(l)
        if l + 1 < self.nl:
            self.compute_mods([l + 1])
        if "stop_na" in self.dbg:
            return
        self.gdn_phase(l)

    def na_phase(self, l):
        k = self.k
        ctx_out = l < self.nl - 1 or ("force_ctx" in self.dbg)
        k.push_scope()
        KQ = k.sb("naKQ", [128, 8, TT], BF16)
        for j in range(8):
            k.dma("sp", KQ[:, j, :], DV(self.QKT[j], *[k.dbuf(("QKT", j, blk)) for blk in range(9)]))
        Vt = k.sb("naV", [128, NTILE, 8, 65], BF16)
        k.memset(Vt[:, :, :, 64:65], 1.0)
        for ti in range(NTILE):
            k.dma("sp", Vt[:, ti, :, 0:64],
                  DV(self.VA[ti * 128:(ti + 1) * 128, :].rearrange("p (h d) -> p h d", d=64), k.dbuf(("VA", ti))))
        EB = k.sb("naEB", [128, 8, 12 * 128], BF16)
        st = [k.sb(f"naST{i}", [128, 12 * 128], F32) for i in range(2)]
        for h in range(8):
            t = st[h % 2]
            k.dma("sp", t[:, :], DV(self.rpb_d[l][h], k.dbuf("rpb_d")))
            k.act(t[:, :], t[:, :], AF.Exp)
            k.tt(EB[:, h, :], t[:, :], self.cst[:, C_CM:C_CM + 12 * 128], ALU.mult)
        E32 = [k.sb(f"naE{i}", [128, 5 * 128], F32) for i in range(2)]
        PT = [k.sb(f"naPT{i}", [128, 7 * 128], BF16) for i in range(2)]
        rden = [k.sb(f"naRD{i}", [128, 8], F32) for i in range(2)]
        yna = [k.sb(f"naY{i}", [128, 512], BF16) for i in range(2)]
        ynaT = [k.sb(f"naYT{i}", [128, 4, 128], BF16) for i in range(2)]
        psT = self.PS[0].v(self.PS[0].h[:, :].bitcast(BF16))
        groups = []
        for rp in range(32):
            if rp <= 1:
                groups.append((rp, [0, 1, 2, 3], 3 - rp))
            elif rp >= 30:
                groups.append((rp, [28, 29, 30, 31], 31 - rp))
            else:
                groups.append((rp, [rp - 2, rp - 1, rp, rp + 1, rp + 2], 7))
        if ctx_out:
            groups.append((32, [], None))
            groups.append((33, [], None))
        hn = 0
        for gi, (qt, lat, s0) in enumerate(groups):
            nlat = len(lat)
            po = [self.PS[4 + 2 * (gi % 2)], self.PS[5 + 2 * (gi % 2)]]
            slots = [32, 33] + lat
            ns = len(slots)
            for h in range(8):
                hc, pb = h // 2, (h % 2) * 64
                pA = self.PS[2 * (hn % 2)]
                pB = self.PS[2 * (hn % 2) + 1]
                e32 = E32[hn % 2]
                pt = PT[hn % 2]
                hn += 1
                q = KQ[pb:pb + 64, hc, qt * 128:(qt + 1) * 128]
                for si, kt in enumerate(slots):
                    dst = pA[:, si * 128:(si + 1) * 128] if si < 4 else pB[:, (si - 4) * 128:(si - 3) * 128]
                    k.mm(dst, KQ[pb:pb + 64, 4 + hc, kt * 128:(kt + 1) * 128], q)
                k.act(pt[:, 0:256], pA[:, 0:256], AF.Exp)
                if nlat > 0:
                    k.act(e32[:, 0:256], pA[:, 256:512], AF.Exp)
                    k.act(e32[:, 256:nlat * 128], pB[:, 0:(nlat - 2) * 128], AF.Exp)
                    k.tt(pt[:, 256:256 + nlat * 128], e32[:, 0:nlat * 128],
                         EB[:, h, s0 * 128:(s0 + nlat) * 128], ALU.mult)
                for si, kt in enumerate(slots):
                    k.mm(po[h // 4][:, (h % 4) * 65:(h % 4) * 65 + 65], pt[:, si * 128:(si + 1) * 128],
                         Vt[:, kt, h, :], start=(si == 0), stop=(si == ns - 1))
            rd = rden[gi % 2]
            y = yna[gi % 2]
            for half in range(2):
                pv = po[half].v(po[half].h[:, 0:260].rearrange("p (h d) -> p h d", d=65))
                k.recip(rd.v(rd.h[:, half * 4:half * 4 + 4].unsqueeze(2)), DV(pv.ap[:, :, 64:65], *pv.bufs))
                k.tt(y.v(y.h[:, half * 256:(half + 1) * 256].rearrange("p (h d) -> p h d", d=64)),
                     DV(pv.ap[:, :, 0:64], *pv.bufs),
                     rd.v(rd.h[:, half * 4:half * 4 + 4].unsqueeze(2).broadcast_to([128, 4, 64])), ALU.mult)
            yt = ynaT[gi % 2]
            for c in range(4):
                k.tr(DV(psT.ap[:, c * 128:(c + 1) * 128], *psT.bufs), y[:, c * 128:(c + 1) * 128], self.identb[:, :])
            k.copy(yt[:, :, :], DV(psT.ap[:, 0:512].rearrange("p (c n) -> p c n", n=128), *psT.bufs))
            blk = min(qt // 4, 8)
            dst = self.YT[0:4, :, qt * 128:(qt + 1) * 128].rearrange("c p n -> p c n")
            k.dma("sp", DV(dst, *[k.dbuf(("YT", c, blk, qt % 4)) for c in range(4)]), yt[:, :, :])
        k.pop_scope()

    def gdn_phase(self, l):
        k = self.k
        sv, cst = self.sv, self.cst
        ctx_out = l < self.nl - 1 or ("force_ctx" in self.dbg)
        k.push_scope()
        BDt = k.sb("gBD", [128, NTILE, 16], F32)
        k.dma("sp", BDt[:, :, :], DV(self.BD.rearrange("(t p) c -> p t c", p=128), *[k.dbuf(("BD", ti)) for ti in range(NTILE)]))
        BETA = k.sb("gBETA", [128, NTILE, 8], F32)
        k.act(BETA[:, :, :], BDt[:, :, 0:8], AF.Sigmoid)
        xg = k.sb("gXG", [128, NTILE, 8], F32)
        k.tt(xg[:, :, :], BDt[:, :, 8:16],
             sv.v(sv.h[:, SV_DTB + l * 8:SV_DTB + l * 8 + 8].unsqueeze(1).broadcast_to([128, NTILE, 8])), ALU.add)
        ax = k.sb("gAX", [128, NTILE, 8], F32)
        k.act(ax[:, :, :], xg[:, :, :], AF.Abs)
        k.act(ax[:, :, :], ax[:, :, :], AF.Exp, scale=-1.0)
        k.act(ax[:, :, :], ax[:, :, :], AF.Ln, bias=1.0)
        k.ts(xg[:, :, :], xg[:, :, :], 0.0, None, ALU.max)
        k.tt(xg[:, :, :], xg[:, :, :], ax[:, :, :], ALU.add)
        nea = k.sb("gNEA", [128, 8], F32)
        k.act(nea[:, :], sv[:, SV_ALOG + l * 8:SV_ALOG + l * 8 + 8], AF.Exp)
        k.ts(nea[:, :], nea[:, :], -1.0, None, ALU.mult)
        GL = k.sb("gGL", [128, NTILE, 8], F32)
        k.tt(GL[:, :, :], xg[:, :, :], nea.v(nea.h[:, :].unsqueeze(1).broadcast_to([128, NTILE, 8])), ALU.mult)
        Ur = [k.sb(f"gUr{d}", [128, 128], F32R) for d in range(2)]
        NEGr = [k.sb(f"gNEGr{d}", [128, 128], F32R) for d in range(2)]
        for d in range(2):
            k.copy(Ur[d][:, :], cst[:, C_U + d * 128:C_U + (d + 1) * 128])
            k.copy(NEGr[d][:, :], cst[:, C_NEG + d * 128:C_NEG + (d + 1) * 128])
        identr = k.sb("gIr", [128, 128], F32R)
        k.copy(identr[:, :], self.identf[:, :])
        onesr = k.sb("gOr", [128, 128], F32R)
        k.copy(onesr[:, :], self.onesf[:, :])
        rmb = k.sb("gRm", [128, 128], BF16)
        k.copy(rmb[:, :], cst[:, C_RM:C_RM + 128])
        OG = sv[:, SV_OG + l * 128:SV_OG + (l + 1) * 128]
        QT = k.sb("gQT", [128, TT], BF16)
        KT = k.sb("gKT", [128, TT], BF16)
        Ktok = k.sb("gKtok", [128, NTILE, 128], BF16)
        Vtok = k.sb("gVtok", [128, NTILE, 128], F32)

        for h in range(1 if "gdbg" in self.dbg else 4):
            k.push_scope()
            raw = k.sb("gRaw", [128, TT], F32)
            cv = k.sb("gCv", [128, TT], F32)
            sqb = [k.sb(f"gSq{i}", [128, 512], BF16) for i in range(2)]
            rs_ = [k.sb(f"gRs{i}", [128, 512], F32) for i in range(2)]
            un = [k.sb(f"gUn{i}", [128, 512], F32) for i in range(2)]
            unb = [k.sb(f"gUnb{i}", [128, 512], BF16) for i in range(2)]
            t1 = [k.sb(f"gT1{i}", [128, 512], F32) for i in range(2)]
            t2 = [k.sb(f"gT2{i}", [128, 512], F32) for i in range(2)]
            rp_ = [k.sb(f"gRope{i}", [128, 2, 512], F32) for i in range(2)]
            for ui, ch in enumerate((h, 4 + h, 8 + h)):
                k.dma("sp", raw[:, :], DV(self.GQ[ch], *[k.dbuf(("GQ", ch, blk)) for blk in range(9)]))
                wc = lambda tap: sv[:, SV_CONV + l * 60 + tap * 12 + ch:SV_CONV + l * 60 + tap * 12 + ch + 1]
                for (a, b) in ((0, TLAT), (TLAT, TT)):
                    k.ts(cv[:, a:b], raw[:, a:b], wc(2), None, ALU.mult)
                    for tap in (0, 1, 3, 4):
                        o = tap - 2
                        lo, hi = max(a, a - o), min(b, b - o)
                        k.stt(cv[:, lo:hi], raw[:, lo + o:hi + o], wc(tap), cv[:, lo:hi], ALU.mult, ALU.add)
                k.act(cv[:, :], cv[:, :], AF.Silu)
                if ui == 2:
                    for g4 in range(0, NTILE, 4):
                        ps = self.PS[(g4 // 4) % 2]
                        nt = min(4, NTILE - g4)
                        for q in range(nt):
                            k.tr(ps[:, q * 128:(q + 1) * 128], cv[:, (g4 + q) * 128:(g4 + q + 1) * 128], self.identf[:, :])
                        src = ps.v(ps.h[:, 0:nt * 128].rearrange("p (t n) -> p t n", n=128))
                        if (g4 // 4) % 2 == 0:
                            k.copy(Vtok[:, g4:g4 + nt, :], src)
                        else:
                            k.act(Vtok[:, g4:g4 + nt, :], src, AF.Copy)
                    continue
                dstT = QT if ui == 0 else KT
                for bi, (t0, N, s) in enumerate(BLOCKS):
                    i2 = bi % 2
                    if s == 0:
                        k.dma("sp", rp_[i2][:, :, :], DV(self.rope_d[:, :, t0:t0 + N], k.dbuf("rope_d")))
                    k.act(sqb[i2][:, :N], cv[:, t0:t0 + N], AF.Square)
                    pss = self.PS[2 + i2]
                    k.mm(pss[:, :N], self.onesb[:, :], sqb[i2][:, :N])
                    k.act(rs_[i2][:, :N], pss[:, :N], AF.Sqrt, bias=self.epsc[:, 0:1])
                    k.recip(rs_[i2][:, :N], rs_[i2][:, :N])
                    if s == 1:
                        if ui == 0:
                            k.stt(dstT[:, t0:t0 + N], cv[:, t0:t0 + N], 128.0 ** -0.5, rs_[i2][:, :N], ALU.mult, ALU.mult)
                        else:
                            k.tt(dstT[:, t0:t0 + N], cv[:, t0:t0 + N], rs_[i2][:, :N], ALU.mult)
                        continue
                    if ui == 0:
                        k.stt(un[i2][:, :N], cv[:, t0:t0 + N], 128.0 ** -0.5, rs_[i2][:, :N], ALU.mult, ALU.mult)
                    else:
                        k.tt(un[i2][:, :N], cv[:, t0:t0 + N], rs_[i2][:, :N], ALU.mult)
                    k.act(unb[i2][:, :N], un[i2][:, :N], AF.Copy)
                    psr = self.PS[4 + i2]
                    k.mm(psr[:, :N], rmb[:, :], unb[i2][:, :N])
                    k.tt(t1[i2][:, :N], un[i2][:, :N], rp_[i2][:, 0, :N], ALU.mult)
                    k.tt(t2[i2][:, :N], psr[:, :N], rp_[i2][:, 1, :N], ALU.mult)
                    k.tt(dstT[:, t0:t0 + N], t1[i2][:, :N], t2[i2][:, :N], ALU.add)
            for g8 in range(0, NTILE, 8):
                ps = self.PS[5 + (g8 // 8) % 2]
                psb = ps.v(ps.h[:, :].bitcast(BF16))
                nt = min(8, NTILE - g8)
                for q in range(nt):
                    k.tr(DV(psb.ap[:, q * 128:(q + 1) * 128], *psb.bufs), KT[:, (g8 + q) * 128:(g8 + q + 1) * 128], self.identb[:, :])
                k.copy(Ktok[:, g8:g8 + nt, :], DV(psb.ap[:, 0:nt * 128].rearrange("p (t n) -> p t n", n=128), *psb.bufs))
            k.pop_scope()
            if "gdbg" in self.dbg:
                k.dma("sp", DV(self.dbg_t["dq"][:, :], k.dbuf("dq")), QT[:, :])
                k.dma("sp", DV(self.dbg_t["dk"][:, :], k.dbuf("dk")), KT[:, :])
                k.dma("sp", DV(self.dbg_t["dv"][:, :, :], k.dbuf("dv")), Vtok[:, :, :])
                k.dma("sp", DV(self.dbg_t["dGL"][:, :, :], k.dbuf("dGL")), GL[:, :, :])
                k.dma("sp", DV(self.dbg_t["dBETA"][:, :, :], k.dbuf("dBETA")), BETA[:, :, :])
                if "g_pre_only" in self.dbg:
                    continue

            k.push_scope()
            QKM = k.sb("gQKM", [128, 2, NTILE, 128], BF16)
            KTL = k.sb("gKTL", [128, 2, NTILE, 128], BF16)
            TTb = k.sb("gTTb", [128, 2, NTILE, 128], BF16)
            OA = k.sb("gOA", [128, NTILE, 128], F32)
            k.memset(OA[:, :, :], 0.0)
            ECUM = k.sb("gECUM", [128, 2, NTILE], F32)
            NECUM = k.sb("gNECUM", [128, 2, NTILE], F32)
            ETOT = k.sb("gETOT", [128, 2, NTILE], F32)
            EK = k.sb("gEK", [128, 2, NTILE], F32)
            Gd = k.sb("gGd", [128, 2, NTILE], F32R)
            for d in range(2):
                col = d * 4 + h
                k.copy(Gd[:, d, :], GL[:, :, col])
                pc = self.PS[d]
                k.mm(pc[:, 0:NTILE], Ur[d][:, :], Gd[:, d, :])
                k.mm(pc[:, 64:64 + NTILE], onesr[:, :], Gd[:, d, :])
                k.act(ECUM[:, d, :], pc[:, 0:NTILE], AF.Exp)
                k.ts(NECUM[:, d, :], ECUM[:, d, :], -1.0, None, ALU.mult)
                k.act(ETOT[:, d, :], pc[:, 64:64 + NTILE], AF.Exp)
                k.copy(EK[:, d, :], pc[:, 0:NTILE])
                k.tt(EK[:, d, :], pc[:, 64:64 + NTILE], EK[:, d, :], ALU.subtract)
                k.act(EK[:, d, :], EK[:, d, :], AF.Exp)
            k.barrier_all()
            NP = 8
            RG = []
            for p in range(NP):
                b0 = self.PS[p]
                RG.append(dict(D=(b0, 0), G=(b0, 128), KQ=(b0, 256), TR=(b0, 384), Y=(b0, 0), Z=(b0, 128)))
            k.push_scope()
            WK = []
            for p in range(NP):
                WK.append(dict(
                    gB=k.sb(f"gB{p}", [128, 128], F32R), ngB=k.sb(f"gnB{p}", [128, 128], F32R),
                    GT=k.sb(f"gGT{p}", [128, 128], F32), AT=k.sb(f"gAT{p}", [128, 128], F32),
                    E=[k.sb(f"gE{p}_{i}", [128, 128], F32R) for i in range(2)],
                    Ysb=k.sb(f"gYs{p}", [128, 128], F32R),
                    Vm=k.sb(f"gVm{p}", [128, 128], F32R), Wm=k.sb(f"gWm{p}", [128, 128], F32R)))

            def rv(p, nm):
                t, c0 = RG[p][nm]
                return t[:, c0:c0 + 128]

            probs = [(n, d) for n in range(NTILE) for d in range(2)]
            g1cut = 9
            for f_ in self.dbg:
                if f_.startswith("g1cut="):
                    g1cut = int(f_[6:])
            for g0 in range(0, len(probs), NP):
                grp = probs[g0:g0 + NP]
                if g1cut == 0 or ("g1one" in self.dbg and g0 > 0):
                    break
                for p, (n, d) in enumerate(grp):
                    w = WK[p]
                    gcol = GL[:, n, d * 4 + h:d * 4 + h + 1]
                    k.ts(w["gB"][:, :], self.onesf[:, :], gcol, None, ALU.mult)
                    k.ts(w["ngB"][:, :], self.onesf[:, :], gcol, -1.0, ALU.mult, ALU.mult)
                for p, (n, d) in enumerate(grp):
                    w = WK[p]
                    k.mm(rv(p, "D"), w["gB"][:, :], Ur[d][:, :], start=True, stop=False)
                    k.mm(rv(p, "D"), Ur[d][:, :], w["ngB"][:, :], start=False, stop=False)
                    k.mm(rv(p, "D"), identr[:, :], NEGr[d][:, :], start=False, stop=True)
                    kc = KT[:, n * 128:(n + 1) * 128]
                    k.mm(rv(p, "G"), kc, kc)
                    k.mm(rv(p, "KQ"), kc, QT[:, n * 128:(n + 1) * 128])
                for p, (n, d) in enumerate(grp):
                    w = WK[p]
                    k.act(w["GT"][:, :], rv(p, "D"), AF.Exp)
                if g1cut <= 1:
                    continue
                for p, (n, d) in enumerate(grp):
                    w = WK[p]
                    k.stt(w["AT"][:, :], rv(p, "G"), BETA[:, n, d * 4 + h:d * 4 + h + 1], w["GT"][:, :], ALU.mult, ALU.mult)
                    k.tt(QKM[:, d, n, :], rv(p, "KQ"), w["GT"][:, :], ALU.mult)
                    k.act(KTL[:, d, n, :], Ktok[:, n, :], AF.Copy, scale=EK[:, d, n:n + 1])
                lm = lambda d, lev: cst[:, C_LM + (d * 7 + lev) * 128:C_LM + (d * 7 + lev + 1) * 128]
                if g1cut <= 2:
                    continue
                for p, (n, d) in enumerate(grp):
                    w = WK[p]
                    k.tt(w["E"][0][:, :], w["AT"][:, :], lm(d, 0), ALU.mult)
                for p, (n, d) in enumerate(grp):
                    w = WK[p]
                    e0 = w["E"][0]
                    k.tr(rv(p, "TR"), e0.v(e0.h[:, :].bitcast(F32)), self.identf[:, :])
                for p, (n, d) in enumerate(grp):
                    w = WK[p]
                    e0 = w["E"][0]
                    k.tt(w["Wm"][:, :], self.identf[:, :], e0.v(e0.h[:, :].bitcast(F32)), ALU.subtract)
                    k.tt(w["Vm"][:, :], self.identf[:, :], rv(p, "TR"), ALU.subtract)
                for lev in range(1, 7 if g1cut > 3 else 1):
                    for p, (n, d) in enumerate(grp):
                        w = WK[p]
                        k.tt(w["E"][lev % 2][:, :], w["AT"][:, :], lm(d, lev), ALU.mult, eng=self.e_eng)
                    for p, (n, d) in enumerate(grp):
                        w = WK[p]
                        k.mm(rv(p, "Y"), w["E"][lev % 2][:, :], w["Vm"][:, :])
                    for p, (n, d) in enumerate(grp):
                        w = WK[p]
                        k.act(w["Ysb"][:, :], rv(p, "Y"), AF.Copy)
                    for p, (n, d) in enumerate(grp):
                        w = WK[p]
                        k.mm(rv(p, "Z"), w["Wm"][:, :], w["Ysb"][:, :])
                    for p, (n, d) in enumerate(grp):
                        w = WK[p]
                        vm = w["Vm"]
                        k.tt(vm[:, :], vm.v(vm.h[:, :].bitcast(F32)), rv(p, "Z"), ALU.subtract)
                    for p, (n, d) in enumerate(grp):
                        w = WK[p]
                        vm = w["Vm"]
                        k.tr(rv(p, "TR"), vm.v(vm.h[:, :].bitcast(F32)), self.identf[:, :])
                    for p, (n, d) in enumerate(grp):
                        w = WK[p]
                        if lev < 6:
                            k.act(w["Wm"][:, :], rv(p, "TR"), AF.Copy)
                        else:
                            k.act(TTb[:, d, n, :], rv(p, "TR"), AF.Copy)
            k.pop_scope()
            if "gdbg" in self.dbg:
                k.dma("sp", DV(self.dbg_t["dT"][:, :, :, :], k.dbuf("dT")), TTb[:, :, :, :])
                k.dma("sp", DV(self.dbg_t["dQKM"][:, :, :, :], k.dbuf("dQKM")), QKM[:, :, :, :])
                k.dma("sp", DV(self.dbg_t["dKTL"][:, :, :, :], k.dbuf("dKTL")), KTL[:, :, :, :])
                for ii, tt_ in enumerate((ECUM, NECUM, ETOT, EK)):
                    k.dma("sp", DV(self.dbg_t["dE"][:, ii, :, :], k.dbuf("dE")), tt_[:, :, :])
                if "g_g1_only" in self.dbg:
                    k.pop_scope()
                    continue
            SR = []
            for d in range(2):
                b = [self.PS[4 * d + i] for i in range(4)]
                SR.append(dict(KS=(k.region(b[0], f"sKS{d}"), 0), QS=(k.region(b[0], f"sQS{d}"), 128),
                               VN=(k.region(b[1], f"sVN{d}"), 0), O=(k.region(b[2], f"sO{d}"), 0),
                               SD=(k.region(b[3], f"sSD{d}"), 0)))

            def sv_(d, nm):
                t, c0 = SR[d][nm]
                return t[:, c0:c0 + 128]

            S = [k.sb(f"gS{d}", [128, 128], F32) for d in range(2)]
            Sb = [k.sb(f"gSb{d}", [128, 128], BF16) for d in range(2)]
            Rb = [k.sb(f"gRb{d}", [128, 128], BF16) for d in range(2)]
            VNb = [k.sb(f"gVNb{d}", [128, 128], BF16) for d in range(2)]
            for d in range(2):
                k.memset(S[d][:, :], 0.0)
                k.memset(Sb[d][:, :], 0.0)
            order = [[32, 33] + list(range(32)), [33, 32] + list(range(31, -1, -1))]
            for step in range(NTILE):
                for d in range(2):
                    n = order[d][step]
                    col = d * 4 + h
                    need_o = (n < 32) or ctx_out
                    k.mm(sv_(d, "KS"), KT[:, n * 128:(n + 1) * 128], Sb[d][:, :])
                    if need_o:
                        k.mm(sv_(d, "QS"), QT[:, n * 128:(n + 1) * 128], Sb[d][:, :])
                    k.stt(Rb[d][:, :], sv_(d, "KS"), NECUM[:, d, n:n + 1], Vtok[:, n, :], ALU.mult, ALU.add)
                    k.mm(sv_(d, "VN"), TTb[:, d, n, :], Rb[d][:, :])
                    k.act(VNb[d][:, :], sv_(d, "VN"), AF.Copy, scale=BETA[:, n, col:col + 1])
                    k.mm(sv_(d, "SD"), KTL[:, d, n, :], VNb[d][:, :])
                    if need_o:
                        k.mm(sv_(d, "O"), QKM[:, d, n, :], VNb[d][:, :])
                    k.stt(S[d][:, :], S[d][:, :], ETOT[:, d, n:n + 1], sv_(d, "SD"), ALU.mult, ALU.add)
                    k.act(Sb[d][:, :], S[d][:, :], AF.Copy)
                    if need_o:
                        k.stt(OA[:, n, :], sv_(d, "QS"), ECUM[:, d, n:n + 1], OA[:, n, :], ALU.mult, ALU.add, eng="pool") \
                            if False else k.stt(OA[:, n, :], sv_(d, "QS"), ECUM[:, d, n:n + 1], OA[:, n, :], ALU.mult, ALU.add)
                        k.tt(OA[:, n, :], OA[:, n, :], sv_(d, "O"), ALU.add)
            k.barrier_all()
            if "gdbg" in self.dbg:
                k.dma("sp", DV(self.dbg_t["dOA"][:, :, :], k.dbuf("dOA")), OA[:, :, :])
            ntl = NTILE if ctx_out else 32
            o2 = k.sb("gO2", [128, 4, 128], F32)
            junk = k.sb("gJunk", [128, 128], F32)
            ssq = k.sb("gSSQ", [128, NTILE], F32)
            k.memset(ssq[:, :], 0.0)
            for n in range(ntl):
                k.act(junk[:, :], OA[:, n, :], AF.Square, accum=ssq[:, n:n + 1])
            k.act(ssq[:, 0:ntl], ssq[:, 0:ntl], AF.Sqrt, scale=1.0 / 128, bias=self.epsc[:, 0:1])
            k.recip(ssq[:, 0:ntl], ssq[:, 0:ntl])
            ggt = [k.sb(f"gGG{i}", [128, 128], F32) for i in range(2)]
            yb = [k.sb(f"gYb{i}", [128, 4, 128], BF16) for i in range(2)]
            ybT = [k.sb(f"gYbT{i}", [128, 512], BF16) for i in range(2)]
            for g4 in range(0, ntl, 4):
                gi = (g4 // 4) % 2
                nt = min(4, ntl - g4)
                for q in range(nt):
                    n = g4 + q
                    gg = ggt[n % 2]
                    k.dma("sp", gg[:, :], DV(self.GG[n * 128:(n + 1) * 128, h * 128:(h + 1) * 128], k.dbuf(("GG", n))))
                    k.stt(o2[:, n % 4, :], OA[:, n, :], ssq[:, n:n + 1], OG, ALU.mult, ALU.mult)
                    k.tt(yb[gi][:, q, :], o2[:, n % 4, :], gg[:, :], ALU.mult)
                ps = self.PS[gi]
                psb = ps.v(ps.h[:, :].bitcast(BF16))
                for q in range(nt):
                    k.tr(DV(psb.ap[:, q * 128:(q + 1) * 128], *psb.bufs), yb[gi][:, q, :], self.identb[:, :])
                k.act(ybT[gi][:, 0:nt * 128], DV(psb.ap[:, 0:nt * 128], *psb.bufs), AF.Copy)
                blk = min(g4 // 4, 8)
                k.dma("sp", DV(self.YT[4 + h][:, g4 * 128:(g4 + nt) * 128],
                               *[k.dbuf(("YT", 4 + h, blk, q)) for q in range(4)]), ybT[gi][:, 0:nt * 128])
            k.pop_scope()
        k.pop_scope()


_CACHE = {}


def _prep_inputs(inp, ncores=8):
    cst, rope = host_consts()
    rpbg = host_rpb(np.asarray(inp["na_rpb"], np.float32))
    maps = []
    shared = {n: np.ascontiguousarray(inp[n], dtype=np.float32) for n in
              ("w_mod", "w_ffn1_in", "w_ffn2_in", "w_ffn1_out", "w_ffn2_out", "w_in", "w_out")}
    for b in range(ncores):
        m = {"x": np.ascontiguousarray(inp["x"][b], dtype=np.float32),
             "ctx": np.ascontiguousarray(inp["ctx"][b], dtype=np.float32),
             "sv": host_smallvec(inp, b), "cst": cst, "rope": rope, "rpbg": rpbg}
        m.update(shared)
        maps.append(m)
    return maps


def kernel(**inputs):
    inp = {k_: np.asarray(v) for k_, v in inputs.items()}
    if "nc" not in _CACHE:
        _CACHE["nc"] = Prog().build()
    nc = _CACHE["nc"]
    maps = _prep_inputs(inp)
    res = run_bass_kernel_spmd(nc, maps, core_ids=list(range(8)))
    out = np.stack([np.asarray(r["out"], dtype=np.float32) for r in res.results], axis=0)
    return out
```

```python
import numpy as np
import ml_dtypes
import concourse.bass as bass
import concourse.mybir as mybir
from concourse.bass_utils import run_bass_kernel_spmd
from contextlib import ExitStack

F32 = mybir.dt.float32
F32R = mybir.dt.float32r
BF16 = mybir.dt.bfloat16
AF = mybir.ActivationFunctionType
ALU = mybir.AluOpType
AX = mybir.AxisListType

NLAYERS = 4
D = 1024
FF = 2816
TLAT = 4096
TCTX = 256
TT = TLAT + TCTX
NTILE = TT // 128
INC = 3600
EPS = 1e-6


class Buf:
    __slots__ = ("name", "wev", "rev", "sg")

    def __init__(self, name, sg=None):
        self.name = name
        self.wev = {}
        self.rev = {}
        self.sg = sg if sg is not None else name


class V:
    __slots__ = ("ap", "bufs")

    def __init__(self, ap, bufs):
        self.ap = ap
        self.bufs = bufs


def DV(ap, *bufs):
    return V(ap, tuple(bufs))


class Tile:
    def __init__(self, h, name, sg=None):
        self.h = h
        self.buf = Buf(name, sg)

    def __getitem__(self, idx):
        return V(self.h[idx], (self.buf,))

    def v(self, ap):
        return V(ap, (self.buf,))


class Eng:
    def __init__(self, name, h, sem, key):
        self.name = name
        self.h = h
        self.sem = sem
        self.key = key
        self.cnt = 0
        self.seen = {}


class KB:
    def __init__(self, nc, es):
        self.nc = nc
        self.ges = es
        self.es = es
        self.sems = {}
        self.engs = {}
        for name, h in (("pe", nc.tensor), ("act", nc.scalar), ("dve", nc.vector),
                        ("pool", nc.gpsimd), ("sp", nc.sync)):
            sem = es.enter_context(nc.semaphore("s_" + name))
            key = "E" + name
            self.sems[key] = [sem, 0]
            self.engs[name] = Eng(name, h, sem, key)
        self.ninstr = 0
        self.uid = 0
        self.dbufs = {}

    def push_scope(self):
        if not hasattr(self, "stack"):
            self.stack = []
        self.stack.append(self.es)
        self.es = ExitStack()
        self.es.__enter__()

    def pop_scope(self):
        self.barrier_all()
        self.es.__exit__(None, None, None)
        self.es = self.stack.pop()

    def region(self, tile, name):
        return tile

    def sb(self, name, shape, dt, sg=None):
        self.uid += 1
        h = self.es.enter_context(self.nc.sbuf_tensor(f"{name}_{self.uid}", list(shape), dt))
        return Tile(h, name, sg)

    def ps(self, name, shape, dt=F32):
        h = self.es.enter_context(self.nc.psum_tensor(name, list(shape), dt))
        return Tile(h, name)

    def dbuf(self, key, sg=None):
        b = self.dbufs.get(key)
        if b is None:
            b = Buf("@" + str(key), sg)
            self.dbufs[key] = b
        return b

    def dsem_key(self, buf):
        key = "D" + buf.sg
        if key not in self.sems:
            sem = self.ges.enter_context(self.nc.semaphore("d%d" % len(self.sems)))
            self.sems[key] = [sem, 0]
        return key

    def _waits(self, E, rb, wb):
        need = {}
        for b in rb:
            for k_, v in b.wev.items():
                if need.get(k_, 0) < v:
                    need[k_] = v
        for b in wb:
            for k_, v in b.wev.items():
                if need.get(k_, 0) < v:
                    need[k_] = v
            for k_, v in b.rev.items():
                if need.get(k_, 0) < v:
                    need[k_] = v
        for k_, v in need.items():
            if E.name == "pe" and k_ == E.key:
                continue
            if k_[0] == "D":
                v = self.sems[k_][1]
            if E.seen.get(k_, 0) >= v:
                continue
            E.h.wait_ge(self.sems[k_][0], v)
            E.seen[k_] = v

    def op(self, eng, fn, reads, writes, sig=True):
        E = self.engs[eng]
        rb = [b for v in reads for b in v.bufs]
        wb = [b for v in writes for b in v.bufs]
        self._waits(E, rb, wb)
        ins = fn(E.h)
        if sig:
            E.cnt += 1
            ins.then_inc(E.sem, 1)
            self.sems[E.key][1] = E.cnt
            ev = E.cnt
        else:
            ev = E.cnt + 1
        for b in rb:
            b.rev[E.key] = ev
        for b in wb:
            b.wev = {E.key: ev}
            b.rev = {}
        self.ninstr += 1
        return ins

    def dma(self, eng, out, in_, sembuf=None, **kw):
        E = self.engs[eng]
        rb = list(in_.bufs)
        wb = list(out.bufs)
        self._waits(E, rb, wb)
        if sembuf is None:
            sembuf = wb[0] if wb[0].name[0] != "@" else rb[0]
        key = self.dsem_key(sembuf)
        ins = E.h.dma_start(out=out.ap, in_=in_.ap, **kw)
        self.sems[key][1] += 16
        ins.then_inc(self.sems[key][0], 16)
        val = self.sems[key][1]
        for b in rb:
            b.rev[key] = val
        for b in wb:
            b.wev = {key: val}
            b.rev = {}
        self.ninstr += 1
        return ins

    def barrier_all(self):
        for E in self.engs.values():
            for k_, (sem, tot) in self.sems.items():
                if tot == 0 or (k_ == E.key and E.name == "pe"):
                    continue
                if E.seen.get(k_, 0) >= tot:
                    continue
                E.h.wait_ge(sem, tot)
                E.seen[k_] = tot

    def finish(self):
        E = self.engs["sp"]
        for k_, (sem, tot) in self.sems.items():
            if tot == 0 or E.seen.get(k_, 0) >= tot:
                continue
            E.h.wait_ge(sem, tot)
            E.seen[k_] = tot

    def _pe_guard(self, is_r):
        if getattr(self, "pe_last_r", False) and not is_r and getattr(self, "dummy", None) is not None:
            o, l, r = self.dummy
            self.op("pe", lambda h: h.matmul(o.ap, l.ap, r.ap, start=True, stop=True), [l, r], [o])
        self.pe_last_r = is_r

    def mm(self, out, lhsT, rhs, start=True, stop=True, sig=None, **kw):
        self._pe_guard(lhsT.ap.dtype == F32R)
        if sig is None:
            sig = bool(stop)
        return self.op("pe", lambda h: h.matmul(out.ap, lhsT.ap, rhs.ap, start=start, stop=stop, **kw),
                       [lhsT, rhs], [out], sig=sig)

    def tr(self, out, in_, ident):
        self._pe_guard(False)
        return self.op("pe", lambda h: h.transpose(out.ap, in_.ap, ident.ap), [in_, ident], [out])

    def act(self, out, in_, func, bias=None, scale=None, accum=None):
        kw = {}
        reads = [in_]
        writes = [out]
        if bias is not None:
            if isinstance(bias, V):
                kw["bias"] = bias.ap
                reads.append(bias)
            else:
                kw["bias"] = bias
        if scale is not None:
            if isinstance(scale, V):
                kw["scale"] = scale.ap
                reads.append(scale)
            else:
                kw["scale"] = scale
        if accum is not None:
            kw["accum_out"] = accum.ap
            writes.append(accum)
        return self.op("act", lambda h: h.activation(out.ap, in_.ap, func, **kw), reads, writes)

    def tt(self, out, a, b, op, eng="dve"):
        return self.op(eng, lambda h: h.tensor_tensor(out.ap, a.ap, b.ap, op), [a, b], [out])

    def ts(self, out, a, s1, s2, op0, op1=None, eng="dve"):
        reads = [a]
        x1 = s1.ap if isinstance(s1, V) else s1
        x2 = s2.ap if isinstance(s2, V) else s2
        if isinstance(s1, V):
            reads.append(s1)
        if isinstance(s2, V):
            reads.append(s2)
        kw = {}
        if op1 is not None:
            kw["op1"] = op1
        return self.op(eng, lambda h: h.tensor_scalar(out.ap, a.ap, x1, x2, op0, **kw), reads, [out])

    def stt(self, out, a, s, b, op0, op1, eng="dve"):
        reads = [a, b]
        x = s.ap if isinstance(s, V) else s
        if isinstance(s, V):
            reads.append(s)
        return self.op(eng, lambda h: h.scalar_tensor_tensor(out.ap, a.ap, x, b.ap, op0, op1), reads, [out])

    def copy(self, out, in_, eng="dve"):
        return self.op(eng, lambda h: h.tensor_copy(out.ap, in_.ap), [in_], [out])

    def memset(self, out, val, eng="dve"):
        return self.op(eng, lambda h: h.memset(out.ap, val), [], [out])

    def recip(self, out, in_, eng="dve"):
        return self.op(eng, lambda h: h.reciprocal(out.ap, in_.ap), [in_], [out])

    def reduce(self, out, in_, op, axis=AX.X, eng="dve"):
        return self.op(eng, lambda h: h.tensor_reduce(out.ap, in_.ap, axis, op), [in_], [out])


C_U = 0
C_NEG = 256
C_LM = 512
C_RM = 512 + 14 * 128
C_CM = C_RM + 128
CST_COLS = C_CM + 12 * 128

NA_DR0 = [1, 3, 5, 7, 9, 11, 13, 3, 5, 7, 9, 11]


def host_consts():
    i = np.arange(128)
    cst = np.zeros((128, CST_COLS), np.float32)
    cst[:, C_U:C_U + 128] = (i[:, None] <= i[None, :])
    cst[:, C_U + 128:C_U + 256] = (i[:, None] >= i[None, :])
    jj, ii = i[:, None], i[None, :]
    cst[:, C_NEG:C_NEG + 128] = np.where(ii >= jj, 0.0, -30000.0)
    cst[:, C_NEG + 128:C_NEG + 256] = np.where(ii <= jj, 0.0, -30000.0)
    for k in range(7):
        b = 2 ** k
        same = (ii // (2 * b)) == (jj // (2 * b))
        cst[:, C_LM + k * 128:C_LM + (k + 1) * 128] = same & ((ii % (2 * b)) >= b) & ((jj % (2 * b)) < b)
        cst[:, C_LM + (7 + k) * 128:C_LM + (8 + k) * 128] = same & ((ii % (2 * b)) < b) & ((jj % (2 * b)) >= b)
    rm = np.zeros((128, 128), np.float32)
    for d in range(128):
        q = (d % 64) // 32
        if q == 0:
            rm[d + 32, d] = -1.0
        else:
            rm[d - 32, d] = 1.0
    cst[:, C_RM:C_RM + 128] = rm
    cols = np.arange(64)
    cs = np.clip(cols - 8, 0, 48)
    col_ok = (cols[None, :] >= cs[:, None]) & (cols[None, :] < cs[:, None] + 16)
    nm = np.zeros((2, 64, 12, 2, 64), np.float32)
    for s, dr0 in enumerate(NA_DR0):
        for kl in range(2):
            for ql in range(2):
                dr = dr0 + kl - ql
                ok = 1.0 if (s < 7 or (3 <= dr <= 10)) else 0.0
                nm[kl, :, s, ql, :] = col_ok.T * ok
    cst[:, C_CM:C_CM + 12 * 128] = nm.reshape(128, 12 * 128)
    inv = 10000.0 ** (-np.arange(32, dtype=np.float64) / 32.0)
    t = np.arange(TLAT)
    row = (t // 64).astype(np.float64)[:, None] * inv
    col = (t % 64).astype(np.float64)[:, None] * inv
    ang = np.concatenate([row, row, col, col], axis=-1)
    rope = np.stack([np.cos(ang).T, np.sin(ang).T], axis=1).astype(np.float32)
    return cst, np.ascontiguousarray(rope)


SV_NORM = 0
SV_BMOD = SV_NORM + NLAYERS * 24
SV_CC = SV_BMOD + NLAYERS * 72
SV_QKG = SV_CC + 16
SV_CONV = SV_QKG + NLAYERS * 2
SV_ALOG = SV_CONV + NLAYERS * 60
SV_DTB = SV_ALOG + NLAYERS * 8
SV_OG = SV_DTB + NLAYERS * 8
SV_COLS = SV_OG + NLAYERS * 128


def host_smallvec(inp, b):
    sv = np.zeros((128, SV_COLS), np.float32)
    for l in range(NLAYERS):
        for n, nm in enumerate(("norm_ffn1", "norm_mix", "norm_ffn2")):
            sv[:, SV_NORM + (l * 3 + n) * 8:SV_NORM + (l * 3 + n + 1) * 8] = inp[nm][l].reshape(8, 128).T
        sv[:, SV_BMOD + l * 72:SV_BMOD + (l + 1) * 72] = inp["b_mod"][l].reshape(72, 128).T
        sv[:, SV_QKG + l * 2] = np.tile(inp["na_q_gain"][l], 2)
        sv[:, SV_QKG + l * 2 + 1] = np.tile(inp["na_k_gain"][l], 2)
        sv[:, SV_CONV + l * 60:SV_CONV + (l + 1) * 60] = \
            inp["dn_conv"][l].reshape(5, 12, 128).transpose(2, 0, 1).reshape(128, 60)
        sv[:, SV_ALOG + l * 8:SV_ALOG + (l + 1) * 8] = inp["dn_a_log"][l].reshape(1, 8)
        sv[:, SV_DTB + l * 8:SV_DTB + (l + 1) * 8] = inp["dn_dt_bias"][l].reshape(1, 8)
        sv[:, SV_OG + l * 128:SV_OG + (l + 1) * 128] = inp["dn_out_gain"][l].reshape(1, 128)
    cc = np.stack([inp["c"][b], inp["c_ctx"]], axis=-1)
    sv[:, SV_CC:SV_CC + 16] = cc.reshape(8, 128, 2).transpose(1, 0, 2).reshape(128, 16)
    return sv


def host_rpb(rpb):
    kc = np.arange(64)[:, None]
    qc = np.arange(64)[None, :]
    dc = np.clip(kc - qc + 15, 0, 30)
    out = np.zeros((rpb.shape[0], 8, 2, 64, 12, 2, 64), np.float32)
    for s, dr0 in enumerate(NA_DR0):
        for kl in range(2):
            for ql in range(2):
                dr = int(np.clip(dr0 + kl - ql, 0, 14))
                out[:, :, kl, :, s, ql, :] = rpb[:, :, dr][:, :, dc]
    return np.ascontiguousarray(out.reshape(rpb.shape[0], 8, 128, 12 * 128))


BLOCKS = [(i * 512, 512, 0) for i in range(8)] + [(TLAT, TCTX, 1)]


class Prog:
    def __init__(self, nl=NLAYERS, dbg=(), dump=()):
        self.nl = nl
        self.dbg = set(dbg)
        dump = set(dump)
        nc = bass.Bass("TRN2", target_bir_lowering=False)
        self.nc = nc
        di = lambda n, s, dt=F32: nc.dram_tensor(n, list(s), dt, kind="ExternalInput")
        self.x_d = di("x", [TLAT, D])
        self.ctx_d = di("ctx", [TCTX, D])
        self.sv_d = di("sv", [128, SV_COLS])
        self.cst_d = di("cst", [128, CST_COLS])
        self.rope_d = di("rope", [128, 2, TLAT])
        self.rpb_d = di("rpbg", [NLAYERS, 8, 128, 12 * 128])
        self.wmod_d = di("w_mod", [NLAYERS, D, 9 * D])
        self.w1_d = [di("w_ffn1_in", [NLAYERS, D, 2 * FF]), di("w_ffn2_in", [NLAYERS, D, 2 * FF])]
        self.w2_d = [di("w_ffn1_out", [NLAYERS, FF, D]), di("w_ffn2_out", [NLAYERS, FF, D])]
        self.win_d = di("w_in", [NLAYERS, D, INC])
        self.wout_d = di("w_out", [NLAYERS, D, D])
        self.out_d = nc.dram_tensor("out", [TLAT, D], F32, kind="ExternalOutput")
        ds = lambda n, s, dt: nc.dram_tensor(n, list(s), dt, kind=("ExternalOutput" if n in dump else "Internal"))
        self.XT = ds("XT", [D, TT], F32)
        self.QKT = ds("QKT", [8, 128, TT], BF16)
        self.VA = ds("VA", [TT, 512], BF16)
        self.GQ = ds("GQ", [12, 128, TT], F32)
        self.GG = ds("GG", [TT, 512], F32)
        self.BD = ds("BD", [TT, 16], F32)
        self.YT = ds("YT", [8, 128, TT], BF16)
        self.W1s = [[ds(f"W1s_{l}_{w}", [22, 128, 8, 256], BF16) for w in range(2)] for l in range(nl)]
        self.W2s = [[ds(f"W2s_{l}_{w}", [8, 128, 22, 128], BF16) for w in range(2)] for l in range(nl)]
        self.WA = [ds(f"WA_{l}", [20, 128, 8, 128], BF16) for l in range(nl)]
        self.WB = [ds(f"WB_{l}", [128, 8, 1040], BF16) for l in range(nl)]
        self.WO = [ds(f"WO_{l}", [8, 128, 8, 128], BF16) for l in range(nl)]
        self.dbg_out = {}
        self.e_eng = "pool" if "epool" in self.dbg else "dve"
        self.dbg_t = {}
        if "gdbg" in self.dbg:
            do = lambda n, sh, dt: nc.dram_tensor("dbg_" + n, list(sh), dt, kind="ExternalOutput")
            self.dbg_t = dict(dq=do("dq", [128, TT], BF16), dk=do("dk", [128, TT], BF16), dv=do("dv", [128, NTILE, 128], F32),
                              dT=do("dT", [128, 2, NTILE, 128], BF16), dQKM=do("dQKM", [128, 2, NTILE, 128], BF16),
                              dKTL=do("dKTL", [128, 2, NTILE, 128], BF16), dOA=do("dOA", [128, NTILE, 128], F32),
                              dE=do("dE", [128, 4, 2, NTILE], F32), dGL=do("dGL", [128, NTILE, 8], F32),
                              dBETA=do("dBETA", [128, NTILE, 8], F32))

    def dbg_tensor(self, name, shape, dt=F32):
        t = self.nc.dram_tensor("dbg_" + name, list(shape), dt, kind="ExternalOutput")
        self.dbg_out[name] = t
        return t

    def build(self):
        nc = self.nc
        with ExitStack() as es:
            k = KB(nc, es)
            self.k = k
            self.PS = [k.ps(f"ps{i}", [128, 512]) for i in range(8)]
            self.psd = k.region(self.PS[7], "psd")
            self.setup_consts()
            k.dummy = None
            self.convert_weights(0)
            self.compute_mods([0])
            self.init_xt()
            for l in range(self.nl + 1):
                self.row_pass(l)
                if l < self.nl:
                    if "stop_inproj" in self.dbg and l == 0:
                        break
                    self.mixer(l)
                    if ("stop_na" in self.dbg or "stop_gdn" in self.dbg) and l == 0:
                        break
            k.finish()
        return nc

    def setup_consts(self):
        k = self.k
        self.cst = k.sb("cst", [128, CST_COLS], F32)
        k.dma("sp", self.cst[:, :], DV(self.cst_d[:, :], k.dbuf("cst_d")))
        self.sv = k.sb("sv", [128, SV_COLS], F32)
        k.dma("sp", self.sv[:, :], DV(self.sv_d[:, :], k.dbuf("sv_d")))
        self.identf = k.sb("identf", [128, 128], F32)
        k.memset(self.identf[:, :], 0.0)
        k.op("pool", lambda h: h.affine_select(out=self.identf.h[:, :], in_=self.identf.h[:, :],
                                               pattern=[[-1, 128]], compare_op=ALU.not_equal, fill=1.0,
                                               base=0, channel_multiplier=1),
             [self.identf[:, :]], [self.identf[:, :]])
        self.identb = k.sb("identb", [128, 128], BF16)
        k.copy(self.identb[:, :], self.identf[:, :])
        self.onesb = k.sb("onesb", [128, 128], BF16)
        k.memset(self.onesb[:, :], 1.0)
        self.onesf = k.sb("onesf", [128, 128], F32)
        k.memset(self.onesf[:, :], 1.0)
        self.epsc = k.sb("epsc", [128, 1], F32)
        k.memset(self.epsc[:, :], EPS)
        self.modT = [k.sb(f"modT{l}", [128, 72, 2], F32) for l in range(self.nl)]
        self.modA = [k.sb(f"modA{l}", [128, 3, 8, 2], F32) for l in range(self.nl)]
        self.modH = [k.sb(f"modH{l}", [128, 3, 8, 2], F32) for l in range(self.nl)]

    def convert_weights(self, l):
        k = self.k
        for w in range(2):
            src = self.w1_d[w][l].rearrange("(c p) (two j n) -> j p c two n", p=128, two=2, j=22, n=128)
            sg = f"cv1_{l}_{w}"
            for j in range(22):
                b = k.dbuf(("W1s", l, w, j), sg)
                for two in range(2):
                    k.dma("pool", DV(self.W1s[l][w][j][:, :, two * 128:(two + 1) * 128], b),
                          DV(src[j][:, :, two, :], k.dbuf("win")), sembuf=b)
            src = self.w2_d[w][l].rearrange("(j p) (f n) -> f p j n", p=128, n=128)
            sg = f"cv2_{l}_{w}"
            for f in range(8):
                b = k.dbuf(("W2s", l, w, f), sg)
                k.dma("pool", DV(self.W2s[l][w][f], b), DV(src[f], k.dbuf("win")), sembuf=b)
        sg = f"cva_{l}"
        colsA = [j * 128 for j in range(8)] + [1536 + j * 128 for j in range(12)]
        srcw = self.win_d[l].rearrange("(c p) n -> p c n", p=128)
        for j in range(20):
            b = k.dbuf(("WA", l, j), sg)
            k.dma("pool", DV(self.WA[l][j], b), DV(srcw[:, :, colsA[j]:colsA[j] + 128], k.dbuf("win")), sembuf=b)
        b = k.dbuf(("WB", l), sg)
        k.dma("pool", DV(self.WB[l][:, :, 0:512], b), DV(srcw[:, :, 1024:1536], k.dbuf("win")), sembuf=b)
        k.dma("pool", DV(self.WB[l][:, :, 512:1040], b), DV(srcw[:, :, 3072:3600], k.dbuf("win")), sembuf=b)
        src = self.wout_d[l].rearrange("(c p) (f n) -> f p c n", p=128, n=128)
        for f in range(8):
            b = k.dbuf(("WO", l, f), sg)
            k.dma("pool", DV(self.WO[l][f], b), DV(src[f], k.dbuf("win")), sembuf=b)

    def compute_mods(self, layers):
        k = self.k
        sv = self.sv
        k.push_scope()
        cs = k.sb("mod_cs", [128, 8, 2], F32)
        k.act(cs[:, :, :], sv.v(sv.h[:, SV_CC:SV_CC + 16].rearrange("p (c s) -> p c s", s=2)), AF.Silu)
        wt = [k.sb(f"mod_w{i}", [128, 1024], F32) for i in range(3)]
        n = 0
        for l in layers:
            for cb in range(9):
                for kc in range(8):
                    t = wt[n % 3]
                    n += 1
                    k.dma("sp", t[:, :], DV(self.wmod_d[l][kc * 128:(kc + 1) * 128, cb * 1024:(cb + 1) * 1024],
                                            k.dbuf("win")))
                    for jj in range(8):
                        k.mm(self.PS[jj][:, 0:2], t[:, jj * 128:(jj + 1) * 128], cs[:, kc, :],
                             start=(kc == 0), stop=(kc == 7), sig=True)
                for jj in range(8):
                    j = cb * 8 + jj
                    k.ts(self.modT[l][:, j, :], self.PS[jj][:, 0:2],
                         sv[:, SV_BMOD + l * 72 + j:SV_BMOD + l * 72 + j + 1], None, ALU.add)
            for nn in range(3):
                g = sv.v(sv.h[:, SV_NORM + (l * 3 + nn) * 8:SV_NORM + (l * 3 + nn + 1) * 8]
                         .unsqueeze(2).broadcast_to([128, 8, 2]))
                sc = self.modT[l][:, (nn * 3 + 1) * 8:(nn * 3 + 2) * 8, :]
                k.stt(self.modA[l][:, nn, :, :], sc, 1.0, g, ALU.add, ALU.mult)
                gt = self.modT[l][:, (nn * 3 + 2) * 8:(nn * 3 + 3) * 8, :]
                k.ts(self.modH[l][:, nn, :, :], gt, 0.5 if nn != 1 else 1.0, None, ALU.mult)
        k.pop_scope()

    def init_xt(self):
        k = self.k
        k.push_scope()
        xin = [k.sb(f"ix{i}", [128, D], F32) for i in range(2)]
        xo = [k.sb(f"ixo{i}", [128, 8, 128], F32) for i in range(2)]
        for ti in range(NTILE):
            t = xin[ti % 2]
            if ti < 32:
                src = DV(self.x_d[ti * 128:(ti + 1) * 128, :], k.dbuf("x_d"))
            else:
                src = DV(self.ctx_d[(ti - 32) * 128:(ti - 31) * 128, :], k.dbuf("x_d"))
            k.dma("sp", t[:, :], src)
            o = xo[ti % 2]
            for half in range(2):
                ps = self.PS[(ti % 2) * 2 + half]
                for c4 in range(4):
                    c = half * 4 + c4
                    k.tr(ps[:, c4 * 128:(c4 + 1) * 128], t[:, c * 128:(c + 1) * 128], self.identf[:, :])
                eng_copy = k.copy if half == 0 else (lambda o_, i_: k.act(o_, i_, AF.Copy))
                eng_copy(o.v(o.h[:, half * 4:(half + 1) * 4, :]),
                         ps.v(ps.h[:, :].rearrange("p (c n) -> p c n", n=128)))
            blk = min(ti // 4, 8)
            dst = self.XT[:, ti * 128:(ti + 1) * 128].rearrange("(c p) n -> p c n", p=128)
            k.dma("sp", DV(dst, *[k.dbuf(("XT", blk, c, ti % 4)) for c in range(8)]), o[:, :, :])
        k.pop_scope()

    def xbufs(self, blk, c):
        return [self.k.dbuf(("XT", blk, c, q)) for q in range(4)]

    def row_pass(self, l):
        k = self.k
        nl = self.nl
        k.push_scope()
        P = self
        P.xT = [[k.sb(f"xT{s}_{c}", [128, 512], F32, sg=f"xT{s}") for c in range(8)] for s in range(2)]
        P.hT = [k.sb(f"hT{c}", [128, 512], BF16) for c in range(8)]
        P.actT = [k.sb(f"actT{j}", [128, 512], BF16) for j in range(22)]
        P.sq = [k.sb(f"sq{i}", [128, 512], BF16) for i in range(2)]
        P.rs = k.sb("rs", [128, 512], F32)
        P.tmp = [k.sb(f"tmp{i}", [128, 512], F32) for i in range(2)]
        P.sgt = [k.sb(f"sgt{i}", [128, 512], F32) for i in range(2)]
        P.w1t = [k.sb(f"w1t{i}", [128, 8, 256], BF16) for i in range(5)]
        P.w2t = [k.sb(f"w2t{i}", [128, 22, 128], BF16) for i in range(3)]
        P.w1n = 0
        P.w2n = 0
        if l > 0:
            P.yT = [k.sb(f"yT{s}", [128, 8, 512], BF16) for s in range(2)]
            P.wot = [k.sb(f"wot{i}", [128, 8, 128], BF16) for i in range(2)]
        if l < nl:
            P.wat = [k.sb(f"wat{i}", [128, 8, 128], BF16) for i in range(4)]
            P.wbt = k.sb("wbt", [128, 8, 1040], BF16)
            k.dma("sp", P.wbt[:, :, :], DV(self.WB[l][:, :, :], k.dbuf(("WB", l))))
            P.blk64 = k.sb("blk64", [128, 128], BF16)
            k.memset(P.blk64[:, :], 0.0)
            k.memset(P.blk64[0:64, 0:64], 1.0)
            k.memset(P.blk64[64:128, 64:128], 1.0)
            P.qkg = k.sb("qkg", [128, 2], F32)
            k.ts(P.qkg[:, 0:1], self.sv[:, SV_QKG + 2 * l:SV_QKG + 2 * l + 1], 0.125, None, ALU.mult)
            k.copy(P.qkg[:, 1:2], self.sv[:, SV_QKG + 2 * l + 1:SV_QKG + 2 * l + 2])
            P.rq = [k.sb(f"rq{i}", [128, 512], F32) for i in range(2)]
            P.qo = [k.sb(f"qo{i}", [128, 512], BF16) for i in range(2)]
            P.go = [k.sb(f"go{i}", [128, 512], F32) for i in range(2)]
            P.vo = [k.sb(f"vo{i}", [128, 512], BF16) for i in range(2)]
            P.gto = [k.sb(f"gto{i}", [128, 512], F32) for i in range(2)]
            P.bdo = [k.sb(f"bdo{i}", [128, 16], F32) for i in range(2)]
        else:
            P.oo = [k.sb(f"oo{i}", [128, D], F32) for i in range(2)]
        blocks = list(range(9)) if l < nl else list(range(8))

        def load(bi):
            blk = blocks[bi]
            t0, N, s = BLOCKS[blk]
            slot = bi % 2
            for c in range(8):
                k.dma("sp", P.xT[slot][c][:, :N], DV(self.XT[c * 128:(c + 1) * 128, t0:t0 + N], *self.xbufs(blk, c)))
            if l > 0:
                src = self.YT[:, :, t0:t0 + N].rearrange("c p n -> p c n")
                k.dma("sp", P.yT[slot][:, :, :N], DV(src, *[k.dbuf(("YT", c, blk, q)) for c in range(8) for q in range(4)]))

        load(0)
        for bi, blk in enumerate(blocks):
            if bi + 1 < len(blocks):
                load(bi + 1)
            t0, N, s = BLOCKS[blk]
            slot = bi % 2
            xT = P.xT[slot]
            if l > 0:
                lp = l - 1
                for f in range(8):
                    wo = P.wot[f % 2]
                    k.dma("sp", wo[:, :, :], DV(self.WO[lp][f], k.dbuf(("WO", lp, f))))
                    po = self.PS[5 + f % 2]
                    for c in range(8):
                        k.mm(po[:, :N], wo[:, c, :], P.yT[slot][:, c, :N], start=(c == 0), stop=(c == 7))
                    k.stt(xT[f][:, :N], po[:, :N], self.modH[lp][:, 1, f, s:s + 1], xT[f][:, :N], ALU.mult, ALU.add)
                self.norm(lp, 2, xT, N, s)
                self.ffn(lp, 1, xT, N, s, 2)
            if l < nl:
                self.norm(l, 0, xT, N, s)
                self.ffn(l, 0, xT, N, s, 0)
                for c in range(8):
                    k.dma("sp", DV(self.XT[c * 128:(c + 1) * 128, t0:t0 + N], *self.xbufs(blk, c)), xT[c][:, :N])
                if "x_ffn1" in self.dbg and l == 0:
                    pass
                self.norm(l, 1, xT, N, s)
                self.inproj(l, blk)
            else:
                for tt in range(N // 128):
                    o = P.oo[tt % 2]
                    for half in range(2):
                        ps = self.PS[1 + (tt % 2) * 2 + half]
                        for c4 in range(4):
                            c = half * 4 + c4
                            k.tr(ps[:, c4 * 128:(c4 + 1) * 128], xT[c][:, tt * 128:(tt + 1) * 128], self.identf[:, :])
                        if half == 0:
                            k.copy(o[:, 0:512], ps[:, :])
                        else:
                            k.act(o[:, 512:1024], ps[:, :], AF.Copy)
                    k.dma("sp", DV(self.out_d[t0 + tt * 128:t0 + (tt + 1) * 128, :], k.dbuf(("out", blk, tt))), o[:, :])
        k.pop_scope()

    def norm(self, l, nn, xT, N, s):
        k = self.k
        P = self
        ps = self.PS[0]
        for c in range(8):
            sq = P.sq[c % 2]
            k.act(sq[:, :N], xT[c][:, :N], AF.Square)
            k.mm(ps[:, :N], self.onesb[:, :], sq[:, :N], start=(c == 0), stop=(c == 7), sig=True)
        k.act(P.rs[:, :N], ps[:, :N], AF.Sqrt, scale=1.0 / D, bias=self.epsc[:, 0:1])
        k.recip(P.rs[:, :N], P.rs[:, :N])
        for c in range(8):
            tmp = P.tmp[c % 2]
            k.stt(tmp[:, :N], xT[c][:, :N], self.modA[l][:, nn, c, s:s + 1], P.rs[:, :N], ALU.mult, ALU.mult)
            k.act(P.hT[c][:, :N], tmp[:, :N], AF.Identity, bias=self.modT[l][:, nn * 24 + c, s:s + 1])

    def ffn(self, l, w, xT, N, s, nn):
        k = self.k
        P = self
        for j in range(22):
            wt = P.w1t[P.w1n % 5]
            P.w1n += 1
            k.dma("sp", wt[:, :, :], DV(self.W1s[l][w][j], k.dbuf(("W1s", l, w, j))))
            pg = self.PS[1 + 2 * (j % 2)]
            pu = self.PS[2 + 2 * (j % 2)]
            for c in range(8):
                k.mm(pg[:, :N], wt[:, c, 0:128], P.hT[c][:, :N], start=(c == 0), stop=(c == 7))
            for c in range(8):
                k.mm(pu[:, :N], wt[:, c, 128:256], P.hT[c][:, :N], start=(c == 0), stop=(c == 7))
            sg = P.sgt[j % 2]
            k.act(sg[:, :N], pg[:, :N], AF.Silu)
            k.tt(P.actT[j][:, :N], sg[:, :N], pu[:, :N], ALU.mult)
        for f in range(8):
            w2 = P.w2t[P.w2n % 3]
            P.w2n += 1
            k.dma("sp", w2[:, :, :], DV(self.W2s[l][w][f], k.dbuf(("W2s", l, w, f))))
            po = self.PS[5 + f % 2]
            for j in range(22):
                k.mm(po[:, :N], w2[:, j, :], P.actT[j][:, :N], start=(j == 0), stop=(j == 21))
            k.stt(xT[f][:, :N], po[:, :N], self.modH[l][:, nn, f, s:s + 1], xT[f][:, :N], ALU.mult, ALU.add)

    def inproj(self, l, blk):
        k = self.k
        P = self
        t0, N, s = BLOCKS[blk]
        n = 0
        for j in range(20):
            wa = P.wat[j % 4]
            k.dma("sp", wa[:, :, :], DV(self.WA[l][j], k.dbuf(("WA", l, j))))
            ps = self.PS[1 + 2 * (j % 2)]
            for c in range(8):
                k.mm(ps[:, :N], wa[:, c, :], P.hT[c][:, :N], start=(c == 0), stop=(c == 7))
            if j < 8:
                ps2 = self.PS[2 + 2 * (j % 2)]
                sq = P.sq[j % 2]
                k.act(sq[:, :N], ps[:, :N], AF.Square)
                k.mm(ps2[:, :N], P.blk64[:, :], sq[:, :N])
                rq = P.rq[j % 2]
                k.act(rq[:, :N], ps2[:, :N], AF.Sqrt, scale=1.0 / 64, bias=self.epsc[:, 0:1])
                k.recip(rq[:, :N], rq[:, :N])
                qo = P.qo[j % 2]
                gcol = P.qkg[:, 0:1] if j < 4 else P.qkg[:, 1:2]
                k.stt(qo[:, :N], ps[:, :N], gcol, rq[:, :N], ALU.mult, ALU.mult)
                k.dma("sp", DV(self.QKT[j][:, t0:t0 + N], k.dbuf(("QKT", j, blk))), qo[:, :N])
            else:
                go = P.go[j % 2]
                if j % 2 == 0:
                    k.copy(go[:, :N], ps[:, :N])
                else:
                    k.act(go[:, :N], ps[:, :N], AF.Copy)
                k.dma("sp", DV(self.GQ[j - 8][:, t0:t0 + N], k.dbuf(("GQ", j - 8, blk))), go[:, :N])
        for tt in range(N // 128):
            r0 = t0 + tt * 128
            ti = r0 // 128
            pv = self.PS[5]
            pgt = self.PS[6]
            pbd = self.PS[7]
            for c in range(8):
                k.mm(pv[:, :], P.hT[c][:, tt * 128:(tt + 1) * 128], P.wbt[:, c, 0:512], start=(c == 0), stop=(c == 7))
            for c in range(8):
                k.mm(pgt[:, :], P.hT[c][:, tt * 128:(tt + 1) * 128], P.wbt[:, c, 512:1024], start=(c == 0), stop=(c == 7))
            for c in range(8):
                k.mm(pbd[:, 0:16], P.hT[c][:, tt * 128:(tt + 1) * 128], P.wbt[:, c, 1024:1040], start=(c == 0), stop=(c == 7))
            vo = P.vo[tt % 2]
            k.copy(vo[:, :], pv[:, :])
            k.dma("sp", DV(self.VA[r0:r0 + 128, :], k.dbuf(("VA", ti))), vo[:, :])
            gto = P.gto[tt % 2]
            k.act(gto[:, :], pgt[:, :], AF.Silu)
            k.dma("sp", DV(self.GG[r0:r0 + 128, :], k.dbuf(("GG", ti))), gto[:, :])
            bdo = P.bdo[tt % 2]
            k.copy(bdo[:, :], pbd[:, 0:16])
            k.dma("sp", DV(self.BD[r0:r0 + 128, :], k.dbuf(("BD", ti))), bdo[:, :])

    def mixer(self, l):
        if l + 1 < self.nl:
            k = self.k
            E = k.engs["pool"]
            for key in ("Epe", "Eact", "Edve"):
                sem, tot = k.sems[key]
                if tot > 0 and E.seen.get(key, 0) < tot:
                    E.h.wait_ge(sem, tot)
                    E.seen[key] = tot
            self.convert_weights(l + 1)
        self.na_phase(l)
        if l + 1 < self.nl:
            self.compute_mods([l + 1])
        if "stop_na" in self.dbg:
            return
        self.gdn_phase(l)

    def na_phase(self, l):
        k = self.k
        ctx_out = l < self.nl - 1 or ("force_ctx" in self.dbg)
        k.push_scope()
        KQ = k.sb("naKQ", [128, 8, TT], BF16)
        for j in range(8):
            k.dma("sp", KQ[:, j, :], DV(self.QKT[j], *[k.dbuf(("QKT", j, blk)) for blk in range(9)]))
        Vt = k.sb("naV", [128, NTILE, 8, 65], BF16)
        k.memset(Vt[:, :, :, 64:65], 1.0)
        for ti in range(NTILE):
            k.dma("sp", Vt[:, ti, :, 0:64],
                  DV(self.VA[ti * 128:(ti + 1) * 128, :].rearrange("p (h d) -> p h d", d=64), k.dbuf(("VA", ti))))
        EB = k.sb("naEB", [128, 8, 12 * 128], BF16)
        st = [k.sb(f"naST{i}", [128, 12 * 128], F32) for i in range(2)]
        for h in range(8):
            t = st[h % 2]
            k.dma("sp", t[:, :], DV(self.rpb_d[l][h], k.dbuf("rpb_d")))
            k.act(t[:, :], t[:, :], AF.Exp)
            k.tt(EB[:, h, :], t[:, :], self.cst[:, C_CM:C_CM + 12 * 128], ALU.mult)
        E32 = [k.sb(f"naE{i}", [128, 5 * 128], F32) for i in range(2)]
        PT = [k.sb(f"naPT{i}", [128, 7 * 128], BF16) for i in range(2)]
        rden = [k.sb(f"naRD{i}", [128, 8], F32) for i in range(2)]
        yna = [k.sb(f"naY{i}", [128, 512], BF16) for i in range(2)]
        ynaT = [k.sb(f"naYT{i}", [128, 4, 128], BF16) for i in range(2)]
        psT = self.PS[0].v(self.PS[0].h[:, :].bitcast(BF16))
        groups = []
        for rp in range(32):
            if rp <= 1:
                groups.append((rp, [0, 1, 2, 3], 3 - rp))
            elif rp >= 30:
                groups.append((rp, [28, 29, 30, 31], 31 - rp))
            else:
                groups.append((rp, [rp - 2, rp - 1, rp, rp + 1, rp + 2], 7))
        if ctx_out:
            groups.append((32, [], None))
            groups.append((33, [], None))
        hn = 0
        for gi, (qt, lat, s0) in enumerate(groups):
            nlat = len(lat)
            po = [self.PS[4 + 2 * (gi % 2)], self.PS[5 + 2 * (gi % 2)]]
            slots = [32, 33] + lat
            ns = len(slots)
            for h in range(8):
                hc, pb = h // 2, (h % 2) * 64
                pA = self.PS[2 * (hn % 2)]
                pB = self.PS[2 * (hn % 2) + 1]
                e32 = E32[hn % 2]
                pt = PT[hn % 2]
                hn += 1
                q = KQ[pb:pb + 64, hc, qt * 128:(qt + 1) * 128]
                for si, kt in enumerate(slots):
                    dst = pA[:, si * 128:(si + 1) * 128] if si < 4 else pB[:, (si - 4) * 128:(si - 3) * 128]
                    k.mm(dst, KQ[pb:pb + 64, 4 + hc, kt * 128:(kt + 1) * 128], q)
                k.act(pt[:, 0:256], pA[:, 0:256], AF.Exp)
                if nlat > 0:
                    k.act(e32[:, 0:256], pA[:, 256:512], AF.Exp)
                    k.act(e32[:, 256:nlat * 128], pB[:, 0:(nlat - 2) * 128], AF.Exp)
                    k.tt(pt[:, 256:256 + nlat * 128], e32[:, 0:nlat * 128],
                         EB[:, h, s0 * 128:(s0 + nlat) * 128], ALU.mult)
                for si, kt in enumerate(slots):
                    k.mm(po[h // 4][:, (h % 4) * 65:(h % 4) * 65 + 65], pt[:, si * 128:(si + 1) * 128],
                         Vt[:, kt, h, :], start=(si == 0), stop=(si == ns - 1))
            rd = rden[gi % 2]
            y = yna[gi % 2]
            for half in range(2):
                pv = po[half].v(po[half].h[:, 0:260].rearrange("p (h d) -> p h d", d=65))
                k.recip(rd.v(rd.h[:, half * 4:half * 4 + 4].unsqueeze(2)), DV(pv.ap[:, :, 64:65], *pv.bufs))
                k.tt(y.v(y.h[:, half * 256:(half + 1) * 256].rearrange("p (h d) -> p h d", d=64)),
                     DV(pv.ap[:, :, 0:64], *pv.bufs),
                     rd.v(rd.h[:, half * 4:half * 4 + 4].unsqueeze(2).broadcast_to([128, 4, 64])), ALU.mult)
            yt = ynaT[gi % 2]
            for c in range(4):
                k.tr(DV(psT.ap[:, c * 128:(c + 1) * 128], *psT.bufs), y[:, c * 128:(c + 1) * 128], self.identb[:, :])
            k.copy(yt[:, :, :], DV(psT.ap[:, 0:512].rearrange("p (c n) -> p c n", n=128), *psT.bufs))
            blk = min(qt // 4, 8)
            dst = self.YT[0:4, :, qt * 128:(qt + 1) * 128].rearrange("c p n -> p c n")
            k.dma("sp", DV(dst, *[k.dbuf(("YT", c, blk, qt % 4)) for c in range(4)]), yt[:, :, :])
        k.pop_scope()

    def gdn_phase(self, l):
        k = self.k
        sv, cst = self.sv, self.cst
        ctx_out = l < self.nl - 1 or ("force_ctx" in self.dbg)
        k.push_scope()
        BDt = k.sb("gBD", [128, NTILE, 16], F32)
        k.dma("sp", BDt[:, :, :], DV(self.BD.rearrange("(t p) c -> p t c", p=128), *[k.dbuf(("BD", ti)) for ti in range(NTILE)]))
        BETA = k.sb("gBETA", [128, NTILE, 8], F32)
        k.act(BETA[:, :, :], BDt[:, :, 0:8], AF.Sigmoid)
        xg = k.sb("gXG", [128, NTILE, 8], F32)
        k.tt(xg[:, :, :], BDt[:, :, 8:16],
             sv.v(sv.h[:, SV_DTB + l * 8:SV_DTB + l * 8 + 8].unsqueeze(1).broadcast_to([128, NTILE, 8])), ALU.add)
        ax = k.sb("gAX", [128, NTILE, 8], F32)
        k.act(ax[:, :, :], xg[:, :, :], AF.Abs)
        k.act(ax[:, :, :], ax[:, :, :], AF.Exp, scale=-1.0)
        k.act(ax[:, :, :], ax[:, :, :], AF.Ln, bias=1.0)
        k.ts(xg[:, :, :], xg[:, :, :], 0.0, None, ALU.max)
        k.tt(xg[:, :, :], xg[:, :, :], ax[:, :, :], ALU.add)
        nea = k.sb("gNEA", [128, 8], F32)
        k.act(nea[:, :], sv[:, SV_ALOG + l * 8:SV_ALOG + l * 8 + 8], AF.Exp)
        k.ts(nea[:, :], nea[:, :], -1.0, None, ALU.mult)
        GL = k.sb("gGL", [128, NTILE, 8], F32)
        k.tt(GL[:, :, :], xg[:, :, :], nea.v(nea.h[:, :].unsqueeze(1).broadcast_to([128, NTILE, 8])), ALU.mult)
        Ur = [k.sb(f"gUr{d}", [128, 128], F32R) for d in range(2)]
        NEGr = [k.sb(f"gNEGr{d}", [128, 128], F32R) for d in range(2)]
        for d in range(2):
            k.copy(Ur[d][:, :], cst[:, C_U + d * 128:C_U + (d + 1) * 128])
            k.copy(NEGr[d][:, :], cst[:, C_NEG + d * 128:C_NEG + (d + 1) * 128])
        identr = k.sb("gIr", [128, 128], F32R)
        k.copy(identr[:, :], self.identf[:, :])
        onesr = k.sb("gOr", [128, 128], F32R)
        k.copy(onesr[:, :], self.onesf[:, :])
        rmb = k.sb("gRm", [128, 128], BF16)
        k.copy(rmb[:, :], cst[:, C_RM:C_RM + 128])
        OG = sv[:, SV_OG + l * 128:SV_OG + (l + 1) * 128]
        QT = k.sb("gQT", [128, TT], BF16)
        KT = k.sb("gKT", [128, TT], BF16)
        Ktok = k.sb("gKtok", [128, NTILE, 128], BF16)
        Vtok = k.sb("gVtok", [128, NTILE, 128], F32)

        for h in range(1 if "gdbg" in self.dbg else 4):
            k.push_scope()
            raw = k.sb("gRaw", [128, TT], F32)
            cv = k.sb("gCv", [128, TT], F32)
            sqb = [k.sb(f"gSq{i}", [128, 512], BF16) for i in range(2)]
            rs_ = [k.sb(f"gRs{i}", [128, 512], F32) for i in range(2)]
            un = [k.sb(f"gUn{i}", [128, 512], F32) for i in range(2)]
            unb = [k.sb(f"gUnb{i}", [128, 512], BF16) for i in range(2)]
            t1 = [k.sb(f"gT1{i}", [128, 512], F32) for i in range(2)]
            t2 = [k.sb(f"gT2{i}", [128, 512], F32) for i in range(2)]
            rp_ = [k.sb(f"gRope{i}", [128, 2, 512], F32) for i in range(2)]
            for ui, ch in enumerate((h, 4 + h, 8 + h)):
                k.dma("sp", raw[:, :], DV(self.GQ[ch], *[k.dbuf(("GQ", ch, blk)) for blk in range(9)]))
                wc = lambda tap: sv[:, SV_CONV + l * 60 + tap * 12 + ch:SV_CONV + l * 60 + tap * 12 + ch + 1]
                for (a, b) in ((0, TLAT), (TLAT, TT)):
                    k.ts(cv[:, a:b], raw[:, a:b], wc(2), None, ALU.mult)
                    for tap in (0, 1, 3, 4):
                        o = tap - 2
                        lo, hi = max(a, a - o), min(b, b - o)
                        k.stt(cv[:, lo:hi], raw[:, lo + o:hi + o], wc(tap), cv[:, lo:hi], ALU.mult, ALU.add)
                k.act(cv[:, :], cv[:, :], AF.Silu)
                if ui == 2:
                    for g4 in range(0, NTILE, 4):
                        ps = self.PS[(g4 // 4) % 2]
                        nt = min(4, NTILE - g4)
                        for q in range(nt):
                            k.tr(ps[:, q * 128:(q + 1) * 128], cv[:, (g4 + q) * 128:(g4 + q + 1) * 128], self.identf[:, :])
                        src = ps.v(ps.h[:, 0:nt * 128].rearrange("p (t n) -> p t n", n=128))
                        if (g4 // 4) % 2 == 0:
                            k.copy(Vtok[:, g4:g4 + nt, :], src)
                        else:
                            k.act(Vtok[:, g4:g4 + nt, :], src, AF.Copy)
                    continue
                dstT = QT if ui == 0 else KT
                for bi, (t0, N, s) in enumerate(BLOCKS):
                    i2 = bi % 2
                    if s == 0:
                        k.dma("sp", rp_[i2][:, :, :], DV(self.rope_d[:, :, t0:t0 + N], k.dbuf("rope_d")))
                    k.act(sqb[i2][:, :N], cv[:, t0:t0 + N], AF.Square)
                    pss = self.PS[2 + i2]
                    k.mm(pss[:, :N], self.onesb[:, :], sqb[i2][:, :N])
                    k.act(rs_[i2][:, :N], pss[:, :N], AF.Sqrt, bias=self.epsc[:, 0:1])
                    k.recip(rs_[i2][:, :N], rs_[i2][:, :N])
                    if s == 1:
                        if ui == 0:
                            k.stt(dstT[:, t0:t0 + N], cv[:, t0:t0 + N], 128.0 ** -0.5, rs_[i2][:, :N], ALU.mult, ALU.mult)
                        else:
                            k.tt(dstT[:, t0:t0 + N], cv[:, t0:t0 + N], rs_[i2][:, :N], ALU.mult)
                        continue
                    if ui == 0:
                        k.stt(un[i2][:, :N], cv[:, t0:t0 + N], 128.0 ** -0.5, rs_[i2][:, :N], ALU.mult, ALU.mult)
                    else:
                        k.tt(un[i2][:, :N], cv[:, t0:t0 + N], rs_[i2][:, :N], ALU.mult)
                    k.act(unb[i2][:, :N], un[i2][:, :N], AF.Copy)
                    psr = self.PS[4 + i2]
                    k.mm(psr[:, :N], rmb[:, :], unb[i2][:, :N])
                    k.tt(t1[i2][:, :N], un[i2][:, :N], rp_[i2][:, 0, :N], ALU.mult)
                    k.tt(t2[i2][:, :N], psr[:, :N], rp_[i2][:, 1, :N], ALU.mult)
                    k.tt(dstT[:, t0:t0 + N], t1[i2][:, :N], t2[i2][:, :N], ALU.add)
            for g8 in range(0, NTILE, 8):
                ps = self.PS[5 + (g8 // 8) % 2]
                psb = ps.v(ps.h[:, :].bitcast(BF16))
                nt = min(8, NTILE - g8)
                for q in range(nt):
                    k.tr(DV(psb.ap[:, q * 128:(q + 1) * 128], *psb.bufs), KT[:, (g8 + q) * 128:(g8 + q + 1) * 128], self.identb[:, :])
                k.copy(Ktok[:, g8:g8 + nt, :], DV(psb.ap[:, 0:nt * 128].rearrange("p (t n) -> p t n", n=128), *psb.bufs))
            k.pop_scope()
            if "gdbg" in self.dbg:
                k.dma("sp", DV(self.dbg_t["dq"][:, :], k.dbuf("dq")), QT[:, :])
                k.dma("sp", DV(self.dbg_t["dk"][:, :], k.dbuf("dk")), KT[:, :])
                k.dma("sp", DV(self.dbg_t["dv"][:, :, :], k.dbuf("dv")), Vtok[:, :, :])
                k.dma("sp", DV(self.dbg_t["dGL"][:, :, :], k.dbuf("dGL")), GL[:, :, :])
                k.dma("sp", DV(self.dbg_t["dBETA"][:, :, :], k.dbuf("dBETA")), BETA[:, :, :])
                if "g_pre_only" in self.dbg:
                    continue

            k.push_scope()
            QKM = k.sb("gQKM", [128, 2, NTILE, 128], BF16)
            KTL = k.sb("gKTL", [128, 2, NTILE, 128], BF16)
            TTb = k.sb("gTTb", [128, 2, NTILE, 128], BF16)
            OA = k.sb("gOA", [128, NTILE, 128], F32)
            k.memset(OA[:, :, :], 0.0)
            ECUM = k.sb("gECUM", [128, 2, NTILE], F32)
            NECUM = k.sb("gNECUM", [128, 2, NTILE], F32)
            ETOT = k.sb("gETOT", [128, 2, NTILE], F32)
            EK = k.sb("gEK", [128, 2, NTILE], F32)
            Gd = k.sb("gGd", [128, 2, NTILE], F32R)
            for d in range(2):
                col = d * 4 + h
                k.copy(Gd[:, d, :], GL[:, :, col])
                pc = self.PS[d]
                k.mm(pc[:, 0:NTILE], Ur[d][:, :], Gd[:, d, :])
                k.mm(pc[:, 64:64 + NTILE], onesr[:, :], Gd[:, d, :])
                k.act(ECUM[:, d, :], pc[:, 0:NTILE], AF.Exp)
                k.ts(NECUM[:, d, :], ECUM[:, d, :], -1.0, None, ALU.mult)
                k.act(ETOT[:, d, :], pc[:, 64:64 + NTILE], AF.Exp)
                k.copy(EK[:, d, :], pc[:, 0:NTILE])
                k.tt(EK[:, d, :], pc[:, 64:64 + NTILE], EK[:, d, :], ALU.subtract)
                k.act(EK[:, d, :], EK[:, d, :], AF.Exp)
            k.barrier_all()
            NP = 8
            RG = []
            for p in range(NP):
                b0 = self.PS[p]
                RG.append(dict(D=(b0, 0), G=(b0, 128), KQ=(b0, 256), TR=(b0, 384), Y=(b0, 0), Z=(b0, 128)))
            k.push_scope()
            WK = []
            for p in range(NP):
                WK.append(dict(
                    gB=k.sb(f"gB{p}", [128, 128], F32R), ngB=k.sb(f"gnB{p}", [128, 128], F32R),
                    GT=k.sb(f"gGT{p}", [128, 128], F32), AT=k.sb(f"gAT{p}", [128, 128], F32),
                    E=[k.sb(f"gE{p}_{i}", [128, 128], F32R) for i in range(2)],
                    Ysb=k.sb(f"gYs{p}", [128, 128], F32R),
                    Vm=k.sb(f"gVm{p}", [128, 128], F32R), Wm=k.sb(f"gWm{p}", [128, 128], F32R)))

            def rv(p, nm):
                t, c0 = RG[p][nm]
                return t[:, c0:c0 + 128]

            probs = [(n, d) for n in range(NTILE) for d in range(2)]
            g1cut = 9
            for f_ in self.dbg:
                if f_.startswith("g1cut="):
                    g1cut = int(f_[6:])
            for g0 in range(0, len(probs), NP):
                grp = probs[g0:g0 + NP]
                if g1cut == 0 or ("g1one" in self.dbg and g0 > 0):
                    break
                for p, (n, d) in enumerate(grp):
                    w = WK[p]
                    gcol = GL[:, n, d * 4 + h:d * 4 + h + 1]
                    k.ts(w["gB"][:, :], self.onesf[:, :], gcol, None, ALU.mult)
                    k.ts(w["ngB"][:, :], self.onesf[:, :], gcol, -1.0, ALU.mult, ALU.mult)
                for p, (n, d) in enumerate(grp):
                    w = WK[p]
                    k.mm(rv(p, "D"), w["gB"][:, :], Ur[d][:, :], start=True, stop=False)
                    k.mm(rv(p, "D"), Ur[d][:, :], w["ngB"][:, :], start=False, stop=False)
                    k.mm(rv(p, "D"), identr[:, :], NEGr[d][:, :], start=False, stop=True)
                    kc = KT[:, n * 128:(n + 1) * 128]
                    k.mm(rv(p, "G"), kc, kc)
                    k.mm(rv(p, "KQ"), kc, QT[:, n * 128:(n + 1) * 128])
                for p, (n, d) in enumerate(grp):
                    w = WK[p]
                    k.act(w["GT"][:, :], rv(p, "D"), AF.Exp)
                if g1cut <= 1:
                    continue
                for p, (n, d) in enumerate(grp):
                    w = WK[p]
                    k.stt(w["AT"][:, :], rv(p, "G"), BETA[:, n, d * 4 + h:d * 4 + h + 1], w["GT"][:, :], ALU.mult, ALU.mult)
                    k.tt(QKM[:, d, n, :], rv(p, "KQ"), w["GT"][:, :], ALU.mult)
                    k.act(KTL[:, d, n, :], Ktok[:, n, :], AF.Copy, scale=EK[:, d, n:n + 1])
                lm = lambda d, lev: cst[:, C_LM + (d * 7 + lev) * 128:C_LM + (d * 7 + lev + 1) * 128]
                if g1cut <= 2:
                    continue
                for p, (n, d) in enumerate(grp):
                    w = WK[p]
                    k.tt(w["E"][0][:, :], w["AT"][:, :], lm(d, 0), ALU.mult)
                for p, (n, d) in enumerate(grp):
                    w = WK[p]
                    e0 = w["E"][0]
                    k.tr(rv(p, "TR"), e0.v(e0.h[:, :].bitcast(F32)), self.identf[:, :])
                for p, (n, d) in enumerate(grp):
                    w = WK[p]
                    e0 = w["E"][0]
                    k.tt(w["Wm"][:, :], self.identf[:, :], e0.v(e0.h[:, :].bitcast(F32)), ALU.subtract)
                    k.tt(w["Vm"][:, :], self.identf[:, :], rv(p, "TR"), ALU.subtract)
                for lev in range(1, 7 if g1cut > 3 else 1):
                    for p, (n, d) in enumerate(grp):
                        w = WK[p]
                        k.tt(w["E"][lev % 2][:, :], w["AT"][:, :], lm(d, lev), ALU.mult, eng=self.e_eng)
                    for p, (n, d) in enumerate(grp):
                        w = WK[p]
                        k.mm(rv(p, "Y"), w["E"][lev % 2][:, :], w["Vm"][:, :])
                    for p, (n, d) in enumerate(grp):
                        w = WK[p]
                        k.act(w["Ysb"][:, :], rv(p, "Y"), AF.Copy)
                    for p, (n, d) in enumerate(grp):
                        w = WK[p]
                        k.mm(rv(p, "Z"), w["Wm"][:, :], w["Ysb"][:, :])
                    for p, (n, d) in enumerate(grp):
                        w = WK[p]
                        vm = w["Vm"]
                        k.tt(vm[:, :], vm.v(vm.h[:, :].bitcast(F32)), rv(p, "Z"), ALU.subtract)
                    for p, (n, d) in enumerate(grp):
                        w = WK[p]
                        vm = w["Vm"]
                        k.tr(rv(p, "TR"), vm.v(vm.h[:, :].bitcast(F32)), self.identf[:, :])
                    for p, (n, d) in enumerate(grp):
                        w = WK[p]
                        if lev < 6:
                            k.act(w["Wm"][:, :], rv(p, "TR"), AF.Copy)
                        else:
                            k.act(TTb[:, d, n, :], rv(p, "TR"), AF.Copy)
            k.pop_scope()
            if "gdbg" in self.dbg:
                k.dma("sp", DV(self.dbg_t["dT"][:, :, :, :], k.dbuf("dT")), TTb[:, :, :, :])
                k.dma("sp", DV(self.dbg_t["dQKM"][:, :, :, :], k.dbuf("dQKM")), QKM[:, :, :, :])
                k.dma("sp", DV(self.dbg_t["dKTL"][:, :, :, :], k.dbuf("dKTL")), KTL[:, :, :, :])
                for ii, tt_ in enumerate((ECUM, NECUM, ETOT, EK)):
                    k.dma("sp", DV(self.dbg_t["dE"][:, ii, :, :], k.dbuf("dE")), tt_[:, :, :])
                if "g_g1_only" in self.dbg:
                    k.pop_scope()
                    continue
            SR = []
            for d in range(2):
                b = [self.PS[4 * d + i] for i in range(4)]
                SR.append(dict(KS=(k.region(b[0], f"sKS{d}"), 0), QS=(k.region(b[0], f"sQS{d}"), 128),
                               VN=(k.region(b[1], f"sVN{d}"), 0), O=(k.region(b[2], f"sO{d}"), 0),
                               SD=(k.region(b[3], f"sSD{d}"), 0)))

            def sv_(d, nm):
                t, c0 = SR[d][nm]
                return t[:, c0:c0 + 128]

            S = [k.sb(f"gS{d}", [128, 128], F32) for d in range(2)]
            Sb = [k.sb(f"gSb{d}", [128, 128], BF16) for d in range(2)]
            Rb = [k.sb(f"gRb{d}", [128, 128], BF16) for d in range(2)]
            VNb = [k.sb(f"gVNb{d}", [128, 128], BF16) for d in range(2)]
            for d in range(2):
                k.memset(S[d][:, :], 0.0)
                k.memset(Sb[d][:, :], 0.0)
            order = [[32, 33] + list(range(32)), [33, 32] + list(range(31, -1, -1))]
            for step in range(NTILE):
                for d in range(2):
                    n = order[d][step]
                    col = d * 4 + h
                    need_o = (n < 32) or ctx_out
                    k.mm(sv_(d, "KS"), KT[:, n * 128:(n + 1) * 128], Sb[d][:, :])
                    if need_o:
                        k.mm(sv_(d, "QS"), QT[:, n * 128:(n + 1) * 128], Sb[d][:, :])
                    k.stt(Rb[d][:, :], sv_(d, "KS"), NECUM[:, d, n:n + 1], Vtok[:, n, :], ALU.mult, ALU.add)
                    k.mm(sv_(d, "VN"), TTb[:, d, n, :], Rb[d][:, :])
                    k.act(VNb[d][:, :], sv_(d, "VN"), AF.Copy, scale=BETA[:, n, col:col + 1])
                    k.mm(sv_(d, "SD"), KTL[:, d, n, :], VNb[d][:, :])
                    if need_o:
                        k.mm(sv_(d, "O"), QKM[:, d, n, :], VNb[d][:, :])
                    k.stt(S[d][:, :], S[d][:, :], ETOT[:, d, n:n + 1], sv_(d, "SD"), ALU.mult, ALU.add)
                    k.act(Sb[d][:, :], S[d][:, :], AF.Copy)
                    if need_o:
                        k.stt(OA[:, n, :], sv_(d, "QS"), ECUM[:, d, n:n + 1], OA[:, n, :], ALU.mult, ALU.add, eng="pool") \
                            if False else k.stt(OA[:, n, :], sv_(d, "QS"), ECUM[:, d, n:n + 1], OA[:, n, :], ALU.mult, ALU.add)
                        k.tt(OA[:, n, :], OA[:, n, :], sv_(d, "O"), ALU.add)
            k.barrier_all()
            if "gdbg" in self.dbg:
                k.dma("sp", DV(self.dbg_t["dOA"][:, :, :], k.dbuf("dOA")), OA[:, :, :])
            ntl = NTILE if ctx_out else 32
            o2 = k.sb("gO2", [128, 4, 128], F32)
            junk = k.sb("gJunk", [128, 128], F32)
            ssq = k.sb("gSSQ", [128, NTILE], F32)
            k.memset(ssq[:, :], 0.0)
            for n in range(ntl):
                k.act(junk[:, :], OA[:, n, :], AF.Square, accum=ssq[:, n:n + 1])
            k.act(ssq[:, 0:ntl], ssq[:, 0:ntl], AF.Sqrt, scale=1.0 / 128, bias=self.epsc[:, 0:1])
            k.recip(ssq[:, 0:ntl], ssq[:, 0:ntl])
            ggt = [k.sb(f"gGG{i}", [128, 128], F32) for i in range(2)]
            yb = [k.sb(f"gYb{i}", [128, 4, 128], BF16) for i in range(2)]
            ybT = [k.sb(f"gYbT{i}", [128, 512], BF16) for i in range(2)]
            for g4 in range(0, ntl, 4):
                gi = (g4 // 4) % 2
                nt = min(4, ntl - g4)
                for q in range(nt):
                    n = g4 + q
                    gg = ggt[n % 2]
                    k.dma("sp", gg[:, :], DV(self.GG[n * 128:(n + 1) * 128, h * 128:(h + 1) * 128], k.dbuf(("GG", n))))
                    k.stt(o2[:, n % 4, :], OA[:, n, :], ssq[:, n:n + 1], OG, ALU.mult, ALU.mult)
                    k.tt(yb[gi][:, q, :], o2[:, n % 4, :], gg[:, :], ALU.mult)
                ps = self.PS[gi]
                psb = ps.v(ps.h[:, :].bitcast(BF16))
                for q in range(nt):
                    k.tr(DV(psb.ap[:, q * 128:(q + 1) * 128], *psb.bufs), yb[gi][:, q, :], self.identb[:, :])
                k.act(ybT[gi][:, 0:nt * 128], DV(psb.ap[:, 0:nt * 128], *psb.bufs), AF.Copy)
                blk = min(g4 // 4, 8)
                k.dma("sp", DV(self.YT[4 + h][:, g4 * 128:(g4 + nt) * 128],
                               *[k.dbuf(("YT", 4 + h, blk, q)) for q in range(4)]), ybT[gi][:, 0:nt * 128])
            k.pop_scope()
        k.pop_scope()


_CACHE = {}


def _prep_inputs(inp, ncores=8):
    cst, rope = host_consts()
    rpbg = host_rpb(np.asarray(inp["na_rpb"], np.float32))
    maps = []
    shared = {n: np.ascontiguousarray(inp[n], dtype=np.float32) for n in
              ("w_mod", "w_ffn1_in", "w_ffn2_in", "w_ffn1_out", "w_ffn2_out", "w_in", "w_out")}
    for b in range(ncores):
        m = {"x": np.ascontiguousarray(inp["x"][b], dtype=np.float32),
             "ctx": np.ascontiguousarray(inp["ctx"][b], dtype=np.float32),
             "sv": host_smallvec(inp, b), "cst": cst, "rope": rope, "rpbg": rpbg}
        m.update(shared)
        maps.append(m)
    return maps


def kernel(**inputs):
    inp = {k_: np.asarray(v) for k_, v in inputs.items()}
    if "nc" not in _CACHE:
        _CACHE["nc"] = Prog().build()
    nc = _CACHE["nc"]
    maps = _prep_inputs(inp)
    res = run_bass_kernel_spmd(nc, maps, core_ids=list(range(8)))
    out = np.stack([np.asarray(r["out"], dtype=np.float32) for r in res.results], axis=0)
    return out
```

```python
import numpy as np
import ml_dtypes
import concourse.bass as bass
import concourse.mybir as mybir
from concourse.bass_utils import run_bass_kernel_spmd
from contextlib import ExitStack

F32 = mybir.dt.float32
F32R = mybir.dt.float32r
BF16 = mybir.dt.bfloat16
AF = mybir.ActivationFunctionType
ALU = mybir.AluOpType
AX = mybir.AxisListType

NLAYERS = 4
D = 1024
FF = 2816
TLAT = 4096
TCTX = 256
TT = TLAT + TCTX
NTILE = TT // 128
INC = 3600
EPS = 1e-6


class Buf:
    __slots__ = ("name", "wev", "rev", "sg")

    def __init__(self, name, sg=None):
        self.name = name
        self.wev = {}
        self.rev = {}
        self.sg = sg if sg is not None else name


class V:
    __slots__ = ("ap", "bufs")

    def __init__(self, ap, bufs):
        self.ap = ap
        self.bufs = bufs


def DV(ap, *bufs):
    return V(ap, tuple(bufs))


class Tile:
    def __init__(self, h, name, sg=None):
        self.h = h
        self.buf = Buf(name, sg)

    def __getitem__(self, idx):
        return V(self.h[idx], (self.buf,))

    def v(self, ap):
        return V(ap, (self.buf,))


class Eng:
    def __init__(self, name, h, sem, key):
        self.name = name
        self.h = h
        self.sem = sem
        self.key = key
        self.cnt = 0
        self.seen = {}


class KB:
    def __init__(self, nc, es):
        self.nc = nc
        self.ges = es
        self.es = es
        self.sems = {}
        self.engs = {}
        for name, h in (("pe", nc.tensor), ("act", nc.scalar), ("dve", nc.vector),
                        ("pool", nc.gpsimd), ("sp", nc.sync)):
            sem = es.enter_context(nc.semaphore("s_" + name))
            key = "E" + name
            self.sems[key] = [sem, 0]
            self.engs[name] = Eng(name, h, sem, key)
        self.ninstr = 0
        self.uid = 0
        self.dbufs = {}

    def push_scope(self):
        if not hasattr(self, "stack"):
            self.stack = []
        self.stack.append(self.es)
        self.es = ExitStack()
        self.es.__enter__()

    def pop_scope(self):
        self.barrier_all()
        self.es.__exit__(None, None, None)
        self.es = self.stack.pop()

    def region(self, tile, name):
        return tile

    def sb(self, name, shape, dt, sg=None):
        self.uid += 1
        h = self.es.enter_context(self.nc.sbuf_tensor(f"{name}_{self.uid}", list(shape), dt))
        return Tile(h, name, sg)

    def ps(self, name, shape, dt=F32):
        h = self.es.enter_context(self.nc.psum_tensor(name, list(shape), dt))
        return Tile(h, name)

    def dbuf(self, key, sg=None):
        b = self.dbufs.get(key)
        if b is None:
            b = Buf("@" + str(key), sg)
            self.dbufs[key] = b
        return b

    def dsem_key(self, buf):
        key = "D" + buf.sg
        if key not in self.sems:
            sem = self.ges.enter_context(self.nc.semaphore("d%d" % len(self.sems)))
            self.sems[key] = [sem, 0]
        return key

    def _waits(self, E, rb, wb):
        need = {}
        for b in rb:
            for k_, v in b.wev.items():
                if need.get(k_, 0) < v:
                    need[k_] = v
        for b in wb:
            for k_, v in b.wev.items():
                if need.get(k_, 0) < v:
                    need[k_] = v
            for k_, v in b.rev.items():
                if need.get(k_, 0) < v:
                    need[k_] = v
        for k_, v in need.items():
            if E.name == "pe" and k_ == E.key:
                continue
            if k_[0] == "D":
                v = self.sems[k_][1]
            if E.seen.get(k_, 0) >= v:
                continue
            E.h.wait_ge(self.sems[k_][0], v)
            E.seen[k_] = v

    def op(self, eng, fn, reads, writes, sig=True):
        E = self.engs[eng]
        rb = [b for v in reads for b in v.bufs]
        wb = [b for v in writes for b in v.bufs]
        self._waits(E, rb, wb)
        ins = fn(E.h)
        if sig:
            E.cnt += 1
            ins.then_inc(E.sem, 1)
            self.sems[E.key][1] = E.cnt
            ev = E.cnt
        else:
            ev = E.cnt + 1
        for b in rb:
            b.rev[E.key] = ev
        for b in wb:
            b.wev = {E.key: ev}
            b.rev = {}
        self.ninstr += 1
        return ins

    def dma(self, eng, out, in_, sembuf=None, **kw):
        E = self.engs[eng]
        rb = list(in_.bufs)
        wb = list(out.bufs)
        self._waits(E, rb, wb)
        if sembuf is None:
            sembuf = wb[0] if wb[0].name[0] != "@" else rb[0]
        key = self.dsem_key(sembuf)
        ins = E.h.dma_start(out=out.ap, in_=in_.ap, **kw)
        self.sems[key][1] += 16
        ins.then_inc(self.sems[key][0], 16)
        val = self.sems[key][1]
        for b in rb:
            b.rev[key] = val
        for b in wb:
            b.wev = {key: val}
            b.rev = {}
        self.ninstr += 1
        return ins

    def barrier_all(self):
        for E in self.engs.values():
            for k_, (sem, tot) in self.sems.items():
                if tot == 0 or (k_ == E.key and E.name == "pe"):
                    continue
                if E.seen.get(k_, 0) >= tot:
                    continue
                E.h.wait_ge(sem, tot)
                E.seen[k_] = tot

    def finish(self):
        E = self.engs["sp"]
        for k_, (sem, tot) in self.sems.items():
            if tot == 0 or E.seen.get(k_, 0) >= tot:
                continue
            E.h.wait_ge(sem, tot)
            E.seen[k_] = tot

    def _pe_guard(self, is_r):
        if getattr(self, "pe_last_r", False) and not is_r and getattr(self, "dummy", None) is not None:
            o, l, r = self.dummy
            self.op("pe", lambda h: h.matmul(o.ap, l.ap, r.ap, start=True, stop=True), [l, r], [o])
        self.pe_last_r = is_r

    def mm(self, out, lhsT, rhs, start=True, stop=True, sig=None, **kw):
        self._pe_guard(lhsT.ap.dtype == F32R)
        if sig is None:
            sig = bool(stop)
        return self.op("pe", lambda h: h.matmul(out.ap, lhsT.ap, rhs.ap, start=start, stop=stop, **kw),
                       [lhsT, rhs], [out], sig=sig)

    def tr(self, out, in_, ident):
        self._pe_guard(False)
        return self.op("pe", lambda h: h.transpose(out.ap, in_.ap, ident.ap), [in_, ident], [out])

    def act(self, out, in_, func, bias=None, scale=None, accum=None):
        kw = {}
        reads = [in_]
        writes = [out]
        if bias is not None:
            if isinstance(bias, V):
                kw["bias"] = bias.ap
                reads.append(bias)
            else:
                kw["bias"] = bias
        if scale is not None:
            if isinstance(scale, V):
                kw["scale"] = scale.ap
                reads.append(scale)
            else:
                kw["scale"] = scale
        if accum is not None:
            kw["accum_out"] = accum.ap
            writes.append(accum)
        return self.op("act", lambda h: h.activation(out.ap, in_.ap, func, **kw), reads, writes)

    def tt(self, out, a, b, op, eng="dve"):
        return self.op(eng, lambda h: h.tensor_tensor(out.ap, a.ap, b.ap, op), [a, b], [out])

    def ts(self, out, a, s1, s2, op0, op1=None, eng="dve"):
        reads = [a]
        x1 = s1.ap if isinstance(s1, V) else s1
        x2 = s2.ap if isinstance(s2, V) else s2
        if isinstance(s1, V):
            reads.append(s1)
        if isinstance(s2, V):
            reads.append(s2)
        kw = {}
        if op1 is not None:
            kw["op1"] = op1
        return self.op(eng, lambda h: h.tensor_scalar(out.ap, a.ap, x1, x2, op0, **kw), reads, [out])

    def stt(self, out, a, s, b, op0, op1, eng="dve"):
        reads = [a, b]
        x = s.ap if isinstance(s, V) else s
        if isinstance(s, V):
            reads.append(s)
        return self.op(eng, lambda h: h.scalar_tensor_tensor(out.ap, a.ap, x, b.ap, op0, op1), reads, [out])

    def copy(self, out, in_, eng="dve"):
        return self.op(eng, lambda h: h.tensor_copy(out.ap, in_.ap), [in_], [out])

    def memset(self, out, val, eng="dve"):
        return self.op(eng, lambda h: h.memset(out.ap, val), [], [out])

    def recip(self, out, in_, eng="dve"):
        return self.op(eng, lambda h: h.reciprocal(out.ap, in_.ap), [in_], [out])

    def reduce(self, out, in_, op, axis=AX.X, eng="dve"):
        return self.op(eng, lambda h: h.tensor_reduce(out.ap, in_.ap, axis, op), [in_], [out])


C_U = 0
C_NEG = 256
C_LM = 512
C_RM = 512 + 14 * 128
C_CM = C_RM + 128
CST_COLS = C_CM + 12 * 128

NA_DR0 = [1, 3, 5, 7, 9, 11, 13, 3, 5, 7, 9, 11]


def host_consts():
    i = np.arange(128)
    cst = np.zeros((128, CST_COLS), np.float32)
    cst[:, C_U:C_U + 128] = (i[:, None] <= i[None, :])
    cst[:, C_U + 128:C_U + 256] = (i[:, None] >= i[None, :])
    jj, ii = i[:, None], i[None, :]
    cst[:, C_NEG:C_NEG + 128] = np.where(ii >= jj, 0.0, -30000.0)
    cst[:, C_NEG + 128:C_NEG + 256] = np.where(ii <= jj, 0.0, -30000.0)
    for k in range(7):
        b = 2 ** k
        same = (ii // (2 * b)) == (jj // (2 * b))
        cst[:, C_LM + k * 128:C_LM + (k + 1) * 128] = same & ((ii % (2 * b)) >= b) & ((jj % (2 * b)) < b)
        cst[:, C_LM + (7 + k) * 128:C_LM + (8 + k) * 128] = same & ((ii % (2 * b)) < b) & ((jj % (2 * b)) >= b)
    rm = np.zeros((128, 128), np.float32)
    for d in range(128):
        q = (d % 64) // 32
        if q == 0:
            rm[d + 32, d] = -1.0
        else:
            rm[d - 32, d] = 1.0
    cst[:, C_RM:C_RM + 128] = rm
    cols = np.arange(64)
    cs = np.clip(cols - 8, 0, 48)
    col_ok = (cols[None, :] >= cs[:, None]) & (cols[None, :] < cs[:, None] + 16)
    nm = np.zeros((2, 64, 12, 2, 64), np.float32)
    for s, dr0 in enumerate(NA_DR0):
        for kl in range(2):
            for ql in range(2):
                dr = dr0 + kl - ql
                ok = 1.0 if (s < 7 or (3 <= dr <= 10)) else 0.0
                nm[kl, :, s, ql, :] = col_ok.T * ok
    cst[:, C_CM:C_CM + 12 * 128] = nm.reshape(128, 12 * 128)
    inv = 10000.0 ** (-np.arange(32, dtype=np.float64) / 32.0)
    t = np.arange(TLAT)
    row = (t // 64).astype(np.float64)[:, None] * inv
    col = (t % 64).astype(np.float64)[:, None] * inv
    ang = np.concatenate([row, row, col, col], axis=-1)
    rope = np.stack([np.cos(ang).T, np.sin(ang).T], axis=1).astype(np.float32)
    return cst, np.ascontiguousarray(rope)


SV_NORM = 0
SV_BMOD = SV_NORM + NLAYERS * 24
SV_CC = SV_BMOD + NLAYERS * 72
SV_QKG = SV_CC + 16
SV_CONV = SV_QKG + NLAYERS * 2
SV_ALOG = SV_CONV + NLAYERS * 60
SV_DTB = SV_ALOG + NLAYERS * 8
SV_OG = SV_DTB + NLAYERS * 8
SV_COLS = SV_OG + NLAYERS * 128


def host_smallvec(inp, b):
    sv = np.zeros((128, SV_COLS), np.float32)
    for l in range(NLAYERS):
        for n, nm in enumerate(("norm_ffn1", "norm_mix", "norm_ffn2")):
            sv[:, SV_NORM + (l * 3 + n) * 8:SV_NORM + (l * 3 + n + 1) * 8] = inp[nm][l].reshape(8, 128).T
        sv[:, SV_BMOD + l * 72:SV_BMOD + (l + 1) * 72] = inp["b_mod"][l].reshape(72, 128).T
        sv[:, SV_QKG + l * 2] = np.tile(inp["na_q_gain"][l], 2)
        sv[:, SV_QKG + l * 2 + 1] = np.tile(inp["na_k_gain"][l], 2)
        sv[:, SV_CONV + l * 60:SV_CONV + (l + 1) * 60] = \
            inp["dn_conv"][l].reshape(5, 12, 128).transpose(2, 0, 1).reshape(128, 60)
        sv[:, SV_ALOG + l * 8:SV_ALOG + (l + 1) * 8] = inp["dn_a_log"][l].reshape(1, 8)
        sv[:, SV_DTB + l * 8:SV_DTB + (l + 1) * 8] = inp["dn_dt_bias"][l].reshape(1, 8)
        sv[:, SV_OG + l * 128:SV_OG + (l + 1) * 128] = inp["dn_out_gain"][l].reshape(1, 128)
    cc = np.stack([inp["c"][b], inp["c_ctx"]], axis=-1)
    sv[:, SV_CC:SV_CC + 16] = cc.reshape(8, 128, 2).transpose(1, 0, 2).reshape(128, 16)
    return sv


def host_rpb(rpb):
    kc = np.arange(64)[:, None]
    qc = np.arange(64)[None, :]
    dc = np.clip(kc - qc + 15, 0, 30)
    out = np.zeros((rpb.shape[0], 8, 2, 64, 12, 2, 64), np.float32)
    for s, dr0 in enumerate(NA_DR0):
        for kl in range(2):
            for ql in range(2):
                dr = int(np.clip(dr0 + kl - ql, 0, 14))
                out[:, :, kl, :, s, ql, :] = rpb[:, :, dr][:, :, dc]
    return np.ascontiguousarray(out.reshape(rpb.shape[0], 8, 128, 12 * 128))


BLOCKS = [(i * 512, 512, 0) for i in range(8)] + [(TLAT, TCTX, 1)]


class Prog:
    def __init__(self, nl=NLAYERS, dbg=(), dump=()):
        self.nl = nl
        self.dbg = set(dbg)
        dump = set(dump)
        nc = bass.Bass("TRN2", target_bir_lowering=False)
        self.nc = nc
        di = lambda n, s, dt=F32: nc.dram_tensor(n, list(s), dt, kind="ExternalInput")
        self.x_d = di("x", [TLAT, D])
        self.ctx_d = di("ctx", [TCTX, D])
        self.sv_d = di("sv", [128, SV_COLS])
        self.cst_d = di("cst", [128, CST_COLS])
        self.rope_d = di("rope", [128, 2, TLAT])
        self.rpb_d = di("rpbg", [NLAYERS, 8, 128, 12 * 128])
        self.wmod_d = di("w_mod", [NLAYERS, D, 9 * D])
        self.w1_d = [di("w_ffn1_in", [NLAYERS, D, 2 * FF]), di("w_ffn2_in", [NLAYERS, D, 2 * FF])]
        self.w2_d = [di("w_ffn1_out", [NLAYERS, FF, D]), di("w_ffn2_out", [NLAYERS, FF, D])]
        self.win_d = di("w_in", [NLAYERS, D, INC])
        self.wout_d = di("w_out", [NLAYERS, D, D])
        self.out_d = nc.dram_tensor("out", [TLAT, D], F32, kind="ExternalOutput")
        ds = lambda n, s, dt: nc.dram_tensor(n, list(s), dt, kind=("ExternalOutput" if n in dump else "Internal"))
        self.XT = ds("XT", [D, TT], F32)
        self.QKT = ds("QKT", [8, 128, TT], BF16)
        self.VA = ds("VA", [TT, 512], BF16)
        self.GQ = ds("GQ", [12, 128, TT], F32)
        self.GG = ds("GG", [TT, 512], F32)
        self.BD = ds("BD", [TT, 16], F32)
        self.YT = ds("YT", [8, 128, TT], BF16)
        self.W1s = [[ds(f"W1s_{l}_{w}", [22, 128, 8, 256], BF16) for w in range(2)] for l in range(nl)]
        self.W2s = [[ds(f"W2s_{l}_{w}", [8, 128, 22, 128], BF16) for w in range(2)] for l in range(nl)]
        self.WA = [ds(f"WA_{l}", [20, 128, 8, 128], BF16) for l in range(nl)]
        self.WB = [ds(f"WB_{l}", [128, 8, 1040], BF16) for l in range(nl)]
        self.WO = [ds(f"WO_{l}", [8, 128, 8, 128], BF16) for l in range(nl)]
        self.dbg_out = {}
        self.e_eng = "pool" if "epool" in self.dbg else "dve"
        self.dbg_t = {}
        if "gdbg" in self.dbg:
            do = lambda n, sh, dt: nc.dram_tensor("dbg_" + n, list(sh), dt, kind="ExternalOutput")
            self.dbg_t = dict(dq=do("dq", [128, TT], BF16), dk=do("dk", [128, TT], BF16), dv=do("dv", [128, NTILE, 128], F32),
                              dT=do("dT", [128, 2, NTILE, 128], BF16), dQKM=do("dQKM", [128, 2, NTILE, 128], BF16),
                              dKTL=do("dKTL", [128, 2, NTILE, 128], BF16), dOA=do("dOA", [128, NTILE, 128], F32),
                              dE=do("dE", [128, 4, 2, NTILE], F32), dGL=do("dGL", [128, NTILE, 8], F32),
                              dBETA=do("dBETA", [128, NTILE, 8], F32))

    def dbg_tensor(self, name, shape, dt=F32):
        t = self.nc.dram_tensor("dbg_" + name, list(shape), dt, kind="ExternalOutput")
        self.dbg_out[name] = t
        return t

    def build(self):
        nc = self.nc
        with ExitStack() as es:
            k = KB(nc, es)
            self.k = k
            self.PS = [k.ps(f"ps{i}", [128, 512]) for i in range(8)]
            self.psd = k.region(self.PS[7], "psd")
            self.setup_consts()
            k.dummy = None
            self.convert_weights(0)
            self.compute_mods([0])
            self.init_xt()
            for l in range(self.nl + 1):
                self.row_pass(l)
                if l < self.nl:
                    if "stop_inproj" in self.dbg and l == 0:
                        break
                    self.mixer(l)
                    if ("stop_na" in self.dbg or "stop_gdn" in self.dbg) and l == 0:
                        break
            k.finish()
        return nc

    def setup_consts(self):
        k = self.k
        self.cst = k.sb("cst", [128, CST_COLS], F32)
        k.dma("sp", self.cst[:, :], DV(self.cst_d[:, :], k.dbuf("cst_d")))
        self.sv = k.sb("sv", [128, SV_COLS], F32)
        k.dma("sp", self.sv[:, :], DV(self.sv_d[:, :], k.dbuf("sv_d")))
        self.identf = k.sb("identf", [128, 128], F32)
        k.memset(self.identf[:, :], 0.0)
        k.op("pool", lambda h: h.affine_select(out=self.identf.h[:, :], in_=self.identf.h[:, :],
                                               pattern=[[-1, 128]], compare_op=ALU.not_equal, fill=1.0,
                                               base=0, channel_multiplier=1),
             [self.identf[:, :]], [self.identf[:, :]])
        self.identb = k.sb("identb", [128, 128], BF16)
        k.copy(self.identb[:, :], self.identf[:, :])
        self.onesb = k.sb("onesb", [128, 128], BF16)
        k.memset(self.onesb[:, :], 1.0)
        self.onesf = k.sb("onesf", [128, 128], F32)
        k.memset(self.onesf[:, :], 1.0)
        self.epsc = k.sb("epsc", [128, 1], F32)
        k.memset(self.epsc[:, :], EPS)
        self.modT = [k.sb(f"modT{l}", [128, 72, 2], F32) for l in range(self.nl)]
        self.modA = [k.sb(f"modA{l}", [128, 3, 8, 2], F32) for l in range(self.nl)]
        self.modH = [k.sb(f"modH{l}", [128, 3, 8, 2], F32) for l in range(self.nl)]

    def convert_weights(self, l):
        k = self.k
        for w in range(2):
            src = self.w1_d[w][l].rearrange("(c p) (two j n) -> j p c two n", p=128, two=2, j=22, n=128)
            sg = f"cv1_{l}_{w}"
            for j in range(22):
                b = k.dbuf(("W1s", l, w, j), sg)
                for two in range(2):
                    k.dma("pool", DV(self.W1s[l][w][j][:, :, two * 128:(two + 1) * 128], b),
                          DV(src[j][:, :, two, :], k.dbuf("win")), sembuf=b)
            src = self.w2_d[w][l].rearrange("(j p) (f n) -> f p j n", p=128, n=128)
            sg = f"cv2_{l}_{w}"
            for f in range(8):
                b = k.dbuf(("W2s", l, w, f), sg)
                k.dma("pool", DV(self.W2s[l][w][f], b), DV(src[f], k.dbuf("win")), sembuf=b)
        sg = f"cva_{l}"
        colsA = [j * 128 for j in range(8)] + [1536 + j * 128 for j in range(12)]
        srcw = self.win_d[l].rearrange("(c p) n -> p c n", p=128)
        for j in range(20):
            b = k.dbuf(("WA", l, j), sg)
            k.dma("pool", DV(self.WA[l][j], b), DV(srcw[:, :, colsA[j]:colsA[j] + 128], k.dbuf("win")), sembuf=b)
        b = k.dbuf(("WB", l), sg)
        k.dma("pool", DV(self.WB[l][:, :, 0:512], b), DV(srcw[:, :, 1024:1536], k.dbuf("win")), sembuf=b)
        k.dma("pool", DV(self.WB[l][:, :, 512:1040], b), DV(srcw[:, :, 3072:3600], k.dbuf("win")), sembuf=b)
        src = self.wout_d[l].rearrange("(c p) (f n) -> f p c n", p=128, n=128)
        for f in range(8):
            b = k.dbuf(("WO", l, f), sg)
            k.dma("pool", DV(self.WO[l][f], b), DV(src[f], k.dbuf("win")), sembuf=b)

    def compute_mods(self, layers):
        k = self.k
        sv = self.sv
        k.push_scope()
        cs = k.sb("mod_cs", [128, 8, 2], F32)
        k.act(cs[:, :, :], sv.v(sv.h[:, SV_CC:SV_CC + 16].rearrange("p (c s) -> p c s", s=2)), AF.Silu)
        wt = [k.sb(f"mod_w{i}", [128, 1024], F32) for i in range(3)]
        n = 0
        for l in layers:
            for cb in range(9):
                for kc in range(8):
                    t = wt[n % 3]
                    n += 1
                    k.dma("sp", t[:, :], DV(self.wmod_d[l][kc * 128:(kc + 1) * 128, cb * 1024:(cb + 1) * 1024],
                                            k.dbuf("win")))
                    for jj in range(8):
                        k.mm(self.PS[jj][:, 0:2], t[:, jj * 128:(jj + 1) * 128], cs[:, kc, :],
                             start=(kc == 0), stop=(kc == 7), sig=True)
                for jj in range(8):
                    j = cb * 8 + jj
                    k.ts(self.modT[l][:, j, :], self.PS[jj][:, 0:2],
                         sv[:, SV_BMOD + l * 72 + j:SV_BMOD + l * 72 + j + 1], None, ALU.add)
            for nn in range(3):
                g = sv.v(sv.h[:, SV_NORM + (l * 3 + nn) * 8:SV_NORM + (l * 3 + nn + 1) * 8]
                         .unsqueeze(2).broadcast_to([128, 8, 2]))
                sc = self.modT[l][:, (nn * 3 + 1) * 8:(nn * 3 + 2) * 8, :]
                k.stt(self.modA[l][:, nn, :, :], sc, 1.0, g, ALU.add, ALU.mult)
                gt = self.modT[l][:, (nn * 3 + 2) * 8:(nn * 3 + 3) * 8, :]
                k.ts(self.modH[l][:, nn, :, :], gt, 0.5 if nn != 1 else 1.0, None, ALU.mult)
        k.pop_scope()

    def init_xt(self):
        k = self.k
        k.push_scope()
        xin = [k.sb(f"ix{i}", [128, D], F32) for i in range(2)]
        xo = [k.sb(f"ixo{i}", [128, 8, 128], F32) for i in range(2)]
        for ti in range(NTILE):
            t = xin[ti % 2]
            if ti < 32:
                src = DV(self.x_d[ti * 128:(ti + 1) * 128, :], k.dbuf("x_d"))
            else:
                src = DV(self.ctx_d[(ti - 32) * 128:(ti - 31) * 128, :], k.dbuf("x_d"))
            k.dma("sp", t[:, :], src)
            o = xo[ti % 2]
            for half in range(2):
                ps = self.PS[(ti % 2) * 2 + half]
                for c4 in range(4):
                    c = half * 4 + c4
                    k.tr(ps[:, c4 * 128:(c4 + 1) * 128], t[:, c * 128:(c + 1) * 128], self.identf[:, :])
                eng_copy = k.copy if half == 0 else (lambda o_, i_: k.act(o_, i_, AF.Copy))
                eng_copy(o.v(o.h[:, half * 4:(half + 1) * 4, :]),
                         ps.v(ps.h[:, :].rearrange("p (c n) -> p c n", n=128)))
            blk = min(ti // 4, 8)
            dst = self.XT[:, ti * 128:(ti + 1) * 128].rearrange("(c p) n -> p c n", p=128)
            k.dma("sp", DV(dst, *[k.dbuf(("XT", blk, c, ti % 4)) for c in range(8)]), o[:, :, :])
        k.pop_scope()

    def xbufs(self, blk, c):
        return [self.k.dbuf(("XT", blk, c, q)) for q in range(4)]

    def row_pass(self, l):
        k = self.k
        nl = self.nl
        k.push_scope()
        P = self
        P.xT = [[k.sb(f"xT{s}_{c}", [128, 512], F32, sg=f"xT{s}") for c in range(8)] for s in range(2)]
        P.hT = [k.sb(f"hT{c}", [128, 512], BF16) for c in range(8)]
        P.actT = [k.sb(f"actT{j}", [128, 512], BF16) for j in range(22)]
        P.sq = [k.sb(f"sq{i}", [128, 512], BF16) for i in range(2)]
        P.rs = k.sb("rs", [128, 512], F32)
        P.tmp = [k.sb(f"tmp{i}", [128, 512], F32) for i in range(2)]
        P.sgt = [k.sb(f"sgt{i}", [128, 512], F32) for i in range(2)]
        P.w1t = [k.sb(f"w1t{i}", [128, 8, 256], BF16) for i in range(5)]
        P.w2t = [k.sb(f"w2t{i}", [128, 22, 128], BF16) for i in range(3)]
        P.w1n = 0
        P.w2n = 0
        if l > 0:
            P.yT = [k.sb(f"yT{s}", [128, 8, 512], BF16) for s in range(2)]
            P.wot = [k.sb(f"wot{i}", [128, 8, 128], BF16) for i in range(2)]
        if l < nl:
            P.wat = [k.sb(f"wat{i}", [128, 8, 128], BF16) for i in range(4)]
            P.wbt = k.sb("wbt", [128, 8, 1040], BF16)
            k.dma("sp", P.wbt[:, :, :], DV(self.WB[l][:, :, :], k.dbuf(("WB", l))))
            P.blk64 = k.sb("blk64", [128, 128], BF16)
            k.memset(P.blk64[:, :], 0.0)
            k.memset(P.blk64[0:64, 0:64], 1.0)
            k.memset(P.blk64[64:128, 64:128], 1.0)
            P.qkg = k.sb("qkg", [128, 2], F32)
            k.ts(P.qkg[:, 0:1], self.sv[:, SV_QKG + 2 * l:SV_QKG + 2 * l + 1], 0.125, None, ALU.mult)
            k.copy(P.qkg[:, 1:2], self.sv[:, SV_QKG + 2 * l + 1:SV_QKG + 2 * l + 2])
            P.rq = [k.sb(f"rq{i}", [128, 512], F32) for i in range(2)]
            P.qo = [k.sb(f"qo{i}", [128, 512], BF16) for i in range(2)]
            P.go = [k.sb(f"go{i}", [128, 512], F32) for i in range(2)]
            P.vo = [k.sb(f"vo{i}", [128, 512], BF16) for i in range(2)]
            P.gto = [k.sb(f"gto{i}", [128, 512], F32) for i in range(2)]
            P.bdo = [k.sb(f"bdo{i}", [128, 16], F32) for i in range(2)]
        else:
            P.oo = [k.sb(f"oo{i}", [128, D], F32) for i in range(2)]
        blocks = list(range(9)) if l < nl else list(range(8))

        def load(bi):
            blk = blocks[bi]
            t0, N, s = BLOCKS[blk]
            slot = bi % 2
            for c in range(8):
                k.dma("sp", P.xT[slot][c][:, :N], DV(self.XT[c * 128:(c + 1) * 128, t0:t0 + N], *self.xbufs(blk, c)))
            if l > 0:
                src = self.YT[:, :, t0:t0 + N].rearrange("c p n -> p c n")
                k.dma("sp", P.yT[slot][:, :, :N], DV(src, *[k.dbuf(("YT", c, blk, q)) for c in range(8) for q in range(4)]))

        load(0)
        for bi, blk in enumerate(blocks):
            if bi + 1 < len(blocks):
                load(bi + 1)
            t0, N, s = BLOCKS[blk]
            slot = bi % 2
            xT = P.xT[slot]
            if l > 0:
                lp = l - 1
                for f in range(8):
                    wo = P.wot[f % 2]
                    k.dma("sp", wo[:, :, :], DV(self.WO[lp][f], k.dbuf(("WO", lp, f))))
                    po = self.PS[5 + f % 2]
                    for c in range(8):
                        k.mm(po[:, :N], wo[:, c, :], P.yT[slot][:, c, :N], start=(c == 0), stop=(c == 7))
                    k.stt(xT[f][:, :N], po[:, :N], self.modH[lp][:, 1, f, s:s + 1], xT[f][:, :N], ALU.mult, ALU.add)
                self.norm(lp, 2, xT, N, s)
                self.ffn(lp, 1, xT, N, s, 2)
            if l < nl:
                self.norm(l, 0, xT, N, s)
                self.ffn(l, 0, xT, N, s, 0)
                for c in range(8):
                    k.dma("pool", DV(self.XT[c * 128:(c + 1) * 128, t0:t0 + N], *self.xbufs(blk, c)), xT[c][:, :N])
                if "x_ffn1" in self.dbg and l == 0:
                    pass
                self.norm(l, 1, xT, N, s)
                self.inproj(l, blk)
            else:
                for tt in range(N // 128):
                    o = P.oo[tt % 2]
                    for half in range(2):
                        ps = self.PS[1 + (tt % 2) * 2 + half]
                        for c4 in range(4):
                            c = half * 4 + c4
                            k.tr(ps[:, c4 * 128:(c4 + 1) * 128], xT[c][:, tt * 128:(tt + 1) * 128], self.identf[:, :])
                        if half == 0:
                            k.copy(o[:, 0:512], ps[:, :])
                        else:
                            k.act(o[:, 512:1024], ps[:, :], AF.Copy)
                    k.dma("pool", DV(self.out_d[t0 + tt * 128:t0 + (tt + 1) * 128, :], k.dbuf(("out", blk, tt))), o[:, :])
        k.pop_scope()

    def norm(self, l, nn, xT, N, s):
        k = self.k
        P = self
        ps = self.PS[0]
        for c in range(8):
            sq = P.sq[c % 2]
            k.act(sq[:, :N], xT[c][:, :N], AF.Square)
            k.mm(ps[:, :N], self.onesb[:, :], sq[:, :N], start=(c == 0), stop=(c == 7), sig=True)
        k.act(P.rs[:, :N], ps[:, :N], AF.Sqrt, scale=1.0 / D, bias=self.epsc[:, 0:1])
        k.recip(P.rs[:, :N], P.rs[:, :N])
        for c in range(8):
            tmp = P.tmp[c % 2]
            k.stt(tmp[:, :N], xT[c][:, :N], self.modA[l][:, nn, c, s:s + 1], P.rs[:, :N], ALU.mult, ALU.mult)
            k.act(P.hT[c][:, :N], tmp[:, :N], AF.Identity, bias=self.modT[l][:, nn * 24 + c, s:s + 1])

    def ffn(self, l, w, xT, N, s, nn):
        k = self.k
        P = self
        for j in range(22):
            wt = P.w1t[P.w1n % 5]
            P.w1n += 1
            k.dma("sp", wt[:, :, :], DV(self.W1s[l][w][j], k.dbuf(("W1s", l, w, j))))
            pg = self.PS[1 + 2 * (j % 2)]
            pu = self.PS[2 + 2 * (j % 2)]
            for c in range(8):
                k.mm(pg[:, :N], wt[:, c, 0:128], P.hT[c][:, :N], start=(c == 0), stop=(c == 7))
            for c in range(8):
                k.mm(pu[:, :N], wt[:, c, 128:256], P.hT[c][:, :N], start=(c == 0), stop=(c == 7))
            sg = P.sgt[j % 2]
            k.act(sg[:, :N], pg[:, :N], AF.Silu)
            k.tt(P.actT[j][:, :N], sg[:, :N], pu[:, :N], ALU.mult)
        for f in range(8):
            w2 = P.w2t[P.w2n % 3]
            P.w2n += 1
            k.dma("sp", w2[:, :, :], DV(self.W2s[l][w][f], k.dbuf(("W2s", l, w, f))))
            po = self.PS[5 + f % 2]
            for j in range(22):
                k.mm(po[:, :N], w2[:, j, :], P.actT[j][:, :N], start=(j == 0), stop=(j == 21))
            k.stt(xT[f][:, :N], po[:, :N], self.modH[l][:, nn, f, s:s + 1], xT[f][:, :N], ALU.mult, ALU.add)

    def inproj(self, l, blk):
        k = self.k
        P = self
        t0, N, s = BLOCKS[blk]
        n = 0
        for j in range(20):
            wa = P.wat[j % 4]
            k.dma("sp", wa[:, :, :], DV(self.WA[l][j], k.dbuf(("WA", l, j))))
            ps = self.PS[1 + 2 * (j % 2)]
            for c in range(8):
                k.mm(ps[:, :N], wa[:, c, :], P.hT[c][:, :N], start=(c == 0), stop=(c == 7))
            if j < 8:
                ps2 = self.PS[2 + 2 * (j % 2)]
                sq = P.sq[j % 2]
                k.act(sq[:, :N], ps[:, :N], AF.Square)
                k.mm(ps2[:, :N], P.blk64[:, :], sq[:, :N])
                rq = P.rq[j % 2]
                k.act(rq[:, :N], ps2[:, :N], AF.Sqrt, scale=1.0 / 64, bias=self.epsc[:, 0:1])
                k.recip(rq[:, :N], rq[:, :N])
                qo = P.qo[j % 2]
                gcol = P.qkg[:, 0:1] if j < 4 else P.qkg[:, 1:2]
                k.stt(qo[:, :N], ps[:, :N], gcol, rq[:, :N], ALU.mult, ALU.mult)
                k.dma("pool", DV(self.QKT[j][:, t0:t0 + N], k.dbuf(("QKT", j, blk))), qo[:, :N])
            else:
                go = P.go[j % 2]
                if j % 2 == 0:
                    k.copy(go[:, :N], ps[:, :N])
                else:
                    k.act(go[:, :N], ps[:, :N], AF.Copy)
                k.dma("pool", DV(self.GQ[j - 8][:, t0:t0 + N], k.dbuf(("GQ", j - 8, blk))), go[:, :N])
        for tt in range(N // 128):
            r0 = t0 + tt * 128
            ti = r0 // 128
            pv = self.PS[5]
            pgt = self.PS[6]
            pbd = self.PS[7]
            for c in range(8):
                k.mm(pv[:, :], P.hT[c][:, tt * 128:(tt + 1) * 128], P.wbt[:, c, 0:512], start=(c == 0), stop=(c == 7))
            for c in range(8):
                k.mm(pgt[:, :], P.hT[c][:, tt * 128:(tt + 1) * 128], P.wbt[:, c, 512:1024], start=(c == 0), stop=(c == 7))
            for c in range(8):
                k.mm(pbd[:, 0:16], P.hT[c][:, tt * 128:(tt + 1) * 128], P.wbt[:, c, 1024:1040], start=(c == 0), stop=(c == 7))
            vo = P.vo[tt % 2]
            k.copy(vo[:, :], pv[:, :])
            k.dma("pool", DV(self.VA[r0:r0 + 128, :], k.dbuf(("VA", ti))), vo[:, :])
            gto = P.gto[tt % 2]
            k.act(gto[:, :], pgt[:, :], AF.Silu)
            k.dma("pool", DV(self.GG[r0:r0 + 128, :], k.dbuf(("GG", ti))), gto[:, :])
            bdo = P.bdo[tt % 2]
            k.copy(bdo[:, :], pbd[:, 0:16])
            k.dma("pool", DV(self.BD[r0:r0 + 128, :], k.dbuf(("BD", ti))), bdo[:, :])

    def mixer(self, l):
        if l + 1 < self.nl:
            k = self.k
            E = k.engs["pool"]
            for key in ("Epe", "Eact", "Edve"):
                sem, tot = k.sems[key]
                if tot > 0 and E.seen.get(key, 0) < tot:
                    E.h.wait_ge(sem, tot)
                    E.seen[key] = tot
            self.convert_weights(l + 1)
        self.na_phase(l)
        if l + 1 < self.nl:
            self.compute_mods([l + 1])
        if "stop_na" in self.dbg:
            return
        self.gdn_phase(l)

    def na_phase(self, l):
        k = self.k
        ctx_out = l < self.nl - 1 or ("force_ctx" in self.dbg)
        k.push_scope()
        KQ = k.sb("naKQ", [128, 8, TT], BF16)
        for j in range(8):
            k.dma("sp", KQ[:, j, :], DV(self.QKT[j], *[k.dbuf(("QKT", j, blk)) for blk in range(9)]))
        Vt = k.sb("naV", [128, NTILE, 8, 65], BF16)
        k.memset(Vt[:, :, :, 64:65], 1.0)
        for ti in range(NTILE):
            k.dma("sp", Vt[:, ti, :, 0:64],
                  DV(self.VA[ti * 128:(ti + 1) * 128, :].rearrange("p (h d) -> p h d", d=64), k.dbuf(("VA", ti))))
        EB = k.sb("naEB", [128, 8, 12 * 128], BF16)
        st = [k.sb(f"naST{i}", [128, 12 * 128], F32) for i in range(2)]
        for h in range(8):
            t = st[h % 2]
            k.dma("sp", t[:, :], DV(self.rpb_d[l][h], k.dbuf("rpb_d")))
            k.act(t[:, :], t[:, :], AF.Exp)
            k.tt(EB[:, h, :], t[:, :], self.cst[:, C_CM:C_CM + 12 * 128], ALU.mult)
        E32 = [k.sb(f"naE{i}", [128, 5 * 128], F32) for i in range(2)]
        PT = [k.sb(f"naPT{i}", [128, 7 * 128], BF16) for i in range(2)]
        rden = [k.sb(f"naRD{i}", [128, 8], F32) for i in range(2)]
        yna = [k.sb(f"naY{i}", [128, 512], BF16) for i in range(2)]
        ynaT = [k.sb(f"naYT{i}", [128, 4, 128], BF16) for i in range(2)]
        psT = self.PS[0].v(self.PS[0].h[:, :].bitcast(BF16))
        groups = []
        for rp in range(32):
            if rp <= 1:
                groups.append((rp, [0, 1, 2, 3], 3 - rp))
            elif rp >= 30:
                groups.append((rp, [28, 29, 30, 31], 31 - rp))
            else:
                groups.append((rp, [rp - 2, rp - 1, rp, rp + 1, rp + 2], 7))
        if ctx_out:
            groups.append((32, [], None))
            groups.append((33, [], None))
        hn = 0
        for gi, (qt, lat, s0) in enumerate(groups):
            nlat = len(lat)
            po = [self.PS[4 + 2 * (gi % 2)], self.PS[5 + 2 * (gi % 2)]]
            slots = [32, 33] + lat
            ns = len(slots)
            for h in range(8):
                hc, pb = h // 2, (h % 2) * 64
                pA = self.PS[2 * (hn % 2)]
                pB = self.PS[2 * (hn % 2) + 1]
                e32 = E32[hn % 2]
                pt = PT[hn % 2]
                hn += 1
                q = KQ[pb:pb + 64, hc, qt * 128:(qt + 1) * 128]
                for si, kt in enumerate(slots):
                    dst = pA[:, si * 128:(si + 1) * 128] if si < 4 else pB[:, (si - 4) * 128:(si - 3) * 128]
                    k.mm(dst, KQ[pb:pb + 64, 4 + hc, kt * 128:(kt + 1) * 128], q)
                k.act(pt[:, 0:256], pA[:, 0:256], AF.Exp)
                if nlat > 0:
                    k.act(e32[:, 0:256], pA[:, 256:512], AF.Exp)
                    k.act(e32[:, 256:nlat * 128], pB[:, 0:(nlat - 2) * 128], AF.Exp)
                    k.tt(pt[:, 256:256 + nlat * 128], e32[:, 0:nlat * 128],
                         EB[:, h, s0 * 128:(s0 + nlat) * 128], ALU.mult)
                for si, kt in enumerate(slots):
                    k.mm(po[h // 4][:, (h % 4) * 65:(h % 4) * 65 + 65], pt[:, si * 128:(si + 1) * 128],
                         Vt[:, kt, h, :], start=(si == 0), stop=(si == ns - 1))
            rd = rden[gi % 2]
            y = yna[gi % 2]
            for half in range(2):
                pv = po[half].v(po[half].h[:, 0:260].rearrange("p (h d) -> p h d", d=65))
                k.recip(rd.v(rd.h[:, half * 4:half * 4 + 4].unsqueeze(2)), DV(pv.ap[:, :, 64:65], *pv.bufs))
                k.tt(y.v(y.h[:, half * 256:(half + 1) * 256].rearrange("p (h d) -> p h d", d=64)),
                     DV(pv.ap[:, :, 0:64], *pv.bufs),
                     rd.v(rd.h[:, half * 4:half * 4 + 4].unsqueeze(2).broadcast_to([128, 4, 64])), ALU.mult)
            yt = ynaT[gi % 2]
            for c in range(4):
                k.tr(DV(psT.ap[:, c * 128:(c + 1) * 128], *psT.bufs), y[:, c * 128:(c + 1) * 128], self.identb[:, :])
            k.copy(yt[:, :, :], DV(psT.ap[:, 0:512].rearrange("p (c n) -> p c n", n=128), *psT.bufs))
            blk = min(qt // 4, 8)
            dst = self.YT[0:4, :, qt * 128:(qt + 1) * 128].rearrange("c p n -> p c n")
            k.dma("sp", DV(dst, *[k.dbuf(("YT", c, blk, qt % 4)) for c in range(4)]), yt[:, :, :])
        k.pop_scope()

    def gdn_phase(self, l):
        k = self.k
        sv, cst = self.sv, self.cst
        ctx_out = l < self.nl - 1 or ("force_ctx" in self.dbg)
        k.push_scope()
        BDt = k.sb("gBD", [128, NTILE, 16], F32)
        k.dma("sp", BDt[:, :, :], DV(self.BD.rearrange("(t p) c -> p t c", p=128), *[k.dbuf(("BD", ti)) for ti in range(NTILE)]))
        BETA = k.sb("gBETA", [128, NTILE, 8], F32)
        k.act(BETA[:, :, :], BDt[:, :, 0:8], AF.Sigmoid)
        xg = k.sb("gXG", [128, NTILE, 8], F32)
        k.tt(xg[:, :, :], BDt[:, :, 8:16],
             sv.v(sv.h[:, SV_DTB + l * 8:SV_DTB + l * 8 + 8].unsqueeze(1).broadcast_to([128, NTILE, 8])), ALU.add)
        ax = k.sb("gAX", [128, NTILE, 8], F32)
        k.act(ax[:, :, :], xg[:, :, :], AF.Abs)
        k.act(ax[:, :, :], ax[:, :, :], AF.Exp, scale=-1.0)
        k.act(ax[:, :, :], ax[:, :, :], AF.Ln, bias=1.0)
        k.ts(xg[:, :, :], xg[:, :, :], 0.0, None, ALU.max)
        k.tt(xg[:, :, :], xg[:, :, :], ax[:, :, :], ALU.add)
        nea = k.sb("gNEA", [128, 8], F32)
        k.act(nea[:, :], sv[:, SV_ALOG + l * 8:SV_ALOG + l * 8 + 8], AF.Exp)
        k.ts(nea[:, :], nea[:, :], -1.0, None, ALU.mult)
        GL = k.sb("gGL", [128, NTILE, 8], F32)
        k.tt(GL[:, :, :], xg[:, :, :], nea.v(nea.h[:, :].unsqueeze(1).broadcast_to([128, NTILE, 8])), ALU.mult)
        Ur = [k.sb(f"gUr{d}", [128, 128], F32R) for d in range(2)]
        NEGr = [k.sb(f"gNEGr{d}", [128, 128], F32R) for d in range(2)]
        for d in range(2):
            k.copy(Ur[d][:, :], cst[:, C_U + d * 128:C_U + (d + 1) * 128])
            k.copy(NEGr[d][:, :], cst[:, C_NEG + d * 128:C_NEG + (d + 1) * 128])
        identr = k.sb("gIr", [128, 128], F32R)
        k.copy(identr[:, :], self.identf[:, :])
        onesr = k.sb("gOr", [128, 128], F32R)
        k.copy(onesr[:, :], self.onesf[:, :])
        rmb = k.sb("gRm", [128, 128], BF16)
        k.copy(rmb[:, :], cst[:, C_RM:C_RM + 128])
        OG = sv[:, SV_OG + l * 128:SV_OG + (l + 1) * 128]
        QT = k.sb("gQT", [128, TT], BF16)
        KT = k.sb("gKT", [128, TT], BF16)
        Ktok = k.sb("gKtok", [128, NTILE, 128], BF16)
        Vtok = k.sb("gVtok", [128, NTILE, 128], F32)

        for h in range(1 if "gdbg" in self.dbg else 4):
            k.push_scope()
            raw = k.sb("gRaw", [128, TT], F32)
            cv = k.sb("gCv", [128, TT], F32)
            sqb = [k.sb(f"gSq{i}", [128, 512], BF16) for i in range(2)]
            rs_ = [k.sb(f"gRs{i}", [128, 512], F32) for i in range(2)]
            un = [k.sb(f"gUn{i}", [128, 512], F32) for i in range(2)]
            unb = [k.sb(f"gUnb{i}", [128, 512], BF16) for i in range(2)]
            t1 = [k.sb(f"gT1{i}", [128, 512], F32) for i in range(2)]
            t2 = [k.sb(f"gT2{i}", [128, 512], F32) for i in range(2)]
            rp_ = [k.sb(f"gRope{i}", [128, 2, 512], F32) for i in range(2)]
            for ui, ch in enumerate((h, 4 + h, 8 + h)):
                k.dma("sp", raw[:, :], DV(self.GQ[ch], *[k.dbuf(("GQ", ch, blk)) for blk in range(9)]))
                wc = lambda tap: sv[:, SV_CONV + l * 60 + tap * 12 + ch:SV_CONV + l * 60 + tap * 12 + ch + 1]
                for (a, b) in ((0, TLAT), (TLAT, TT)):
                    k.ts(cv[:, a:b], raw[:, a:b], wc(2), None, ALU.mult)
                    for tap in (0, 1, 3, 4):
                        o = tap - 2
                        lo, hi = max(a, a - o), min(b, b - o)
                        k.stt(cv[:, lo:hi], raw[:, lo + o:hi + o], wc(tap), cv[:, lo:hi], ALU.mult, ALU.add)
                k.act(cv[:, :], cv[:, :], AF.Silu)
                if ui == 2:
                    for g4 in range(0, NTILE, 4):
                        ps = self.PS[(g4 // 4) % 2]
                        nt = min(4, NTILE - g4)
                        for q in range(nt):
                            k.tr(ps[:, q * 128:(q + 1) * 128], cv[:, (g4 + q) * 128:(g4 + q + 1) * 128], self.identf[:, :])
                        src = ps.v(ps.h[:, 0:nt * 128].rearrange("p (t n) -> p t n", n=128))
                        if (g4 // 4) % 2 == 0:
                            k.copy(Vtok[:, g4:g4 + nt, :], src)
                        else:
                            k.act(Vtok[:, g4:g4 + nt, :], src, AF.Copy)
                    continue
                dstT = QT if ui == 0 else KT
                for bi, (t0, N, s) in enumerate(BLOCKS):
                    i2 = bi % 2
                    if s == 0:
                        k.dma("sp", rp_[i2][:, :, :], DV(self.rope_d[:, :, t0:t0 + N], k.dbuf("rope_d")))
                    k.act(sqb[i2][:, :N], cv[:, t0:t0 + N], AF.Square)
                    pss = self.PS[2 + i2]
                    k.mm(pss[:, :N], self.onesb[:, :], sqb[i2][:, :N])
                    k.act(rs_[i2][:, :N], pss[:, :N], AF.Sqrt, bias=self.epsc[:, 0:1])
                    k.recip(rs_[i2][:, :N], rs_[i2][:, :N])
                    if s == 1:
                        if ui == 0:
                            k.stt(dstT[:, t0:t0 + N], cv[:, t0:t0 + N], 128.0 ** -0.5, rs_[i2][:, :N], ALU.mult, ALU.mult)
                        else:
                            k.tt(dstT[:, t0:t0 + N], cv[:, t0:t0 + N], rs_[i2][:, :N], ALU.mult)
                        continue
                    if ui == 0:
                        k.stt(un[i2][:, :N], cv[:, t0:t0 + N], 128.0 ** -0.5, rs_[i2][:, :N], ALU.mult, ALU.mult)
                    else:
                        k.tt(un[i2][:, :N], cv[:, t0:t0 + N], rs_[i2][:, :N], ALU.mult)
                    k.act(unb[i2][:, :N], un[i2][:, :N], AF.Copy)
                    psr = self.PS[4 + i2]
                    k.mm(psr[:, :N], rmb[:, :], unb[i2][:, :N])
                    k.tt(t1[i2][:, :N], un[i2][:, :N], rp_[i2][:, 0, :N], ALU.mult)
                    k.tt(t2[i2][:, :N], psr[:, :N], rp_[i2][:, 1, :N], ALU.mult)
                    k.tt(dstT[:, t0:t0 + N], t1[i2][:, :N], t2[i2][:, :N], ALU.add)
            for g8 in range(0, NTILE, 8):
                ps = self.PS[5 + (g8 // 8) % 2]
                psb = ps.v(ps.h[:, :].bitcast(BF16))
                nt = min(8, NTILE - g8)
                for q in range(nt):
                    k.tr(DV(psb.ap[:, q * 128:(q + 1) * 128], *psb.bufs), KT[:, (g8 + q) * 128:(g8 + q + 1) * 128], self.identb[:, :])
                k.copy(Ktok[:, g8:g8 + nt, :], DV(psb.ap[:, 0:nt * 128].rearrange("p (t n) -> p t n", n=128), *psb.bufs))
            k.pop_scope()
            if "gdbg" in self.dbg:
                k.dma("sp", DV(self.dbg_t["dq"][:, :], k.dbuf("dq")), QT[:, :])
                k.dma("sp", DV(self.dbg_t["dk"][:, :], k.dbuf("dk")), KT[:, :])
                k.dma("sp", DV(self.dbg_t["dv"][:, :, :], k.dbuf("dv")), Vtok[:, :, :])
                k.dma("sp", DV(self.dbg_t["dGL"][:, :, :], k.dbuf("dGL")), GL[:, :, :])
                k.dma("sp", DV(self.dbg_t["dBETA"][:, :, :], k.dbuf("dBETA")), BETA[:, :, :])
                if "g_pre_only" in self.dbg:
                    continue

            k.push_scope()
            QKM = k.sb("gQKM", [128, 2, NTILE, 128], BF16)
            KTL = k.sb("gKTL", [128, 2, NTILE, 128], BF16)
            TTb = k.sb("gTTb", [128, 2, NTILE, 128], BF16)
            OA = k.sb("gOA", [128, NTILE, 128], F32)
            k.memset(OA[:, :, :], 0.0)
            ECUM = k.sb("gECUM", [128, 2, NTILE], F32)
            NECUM = k.sb("gNECUM", [128, 2, NTILE], F32)
            ETOT = k.sb("gETOT", [128, 2, NTILE], F32)
            EK = k.sb("gEK", [128, 2, NTILE], F32)
            Gd = k.sb("gGd", [128, 2, NTILE], F32R)
            for d in range(2):
                col = d * 4 + h
                k.copy(Gd[:, d, :], GL[:, :, col])
                pc = self.PS[d]
                k.mm(pc[:, 0:NTILE], Ur[d][:, :], Gd[:, d, :])
                k.mm(pc[:, 64:64 + NTILE], onesr[:, :], Gd[:, d, :])
                k.act(ECUM[:, d, :], pc[:, 0:NTILE], AF.Exp)
                k.ts(NECUM[:, d, :], ECUM[:, d, :], -1.0, None, ALU.mult)
                k.act(ETOT[:, d, :], pc[:, 64:64 + NTILE], AF.Exp)
                k.copy(EK[:, d, :], pc[:, 0:NTILE])
                k.tt(EK[:, d, :], pc[:, 64:64 + NTILE], EK[:, d, :], ALU.subtract)
                k.act(EK[:, d, :], EK[:, d, :], AF.Exp)
            k.barrier_all()
            NP = 8
            RG = []
            for p in range(NP):
                b0 = self.PS[p]
                RG.append(dict(D=(b0, 0), G=(b0, 128), KQ=(b0, 256), TR=(b0, 384), Y=(b0, 0), Z=(b0, 128)))
            k.push_scope()
            WK = []
            for p in range(NP):
                WK.append(dict(
                    gB=k.sb(f"gB{p}", [128, 128], F32R), ngB=k.sb(f"gnB{p}", [128, 128], F32R),
                    GT=k.sb(f"gGT{p}", [128, 128], F32), AT=k.sb(f"gAT{p}", [128, 128], F32),
                    E=[k.sb(f"gE{p}_{i}", [128, 128], F32R) for i in range(2)],
                    Ysb=k.sb(f"gYs{p}", [128, 128], F32R),
                    Vm=k.sb(f"gVm{p}", [128, 128], F32R), Wm=k.sb(f"gWm{p}", [128, 128], F32R)))

            def rv(p, nm):
                t, c0 = RG[p][nm]
                return t[:, c0:c0 + 128]

            probs = [(n, d) for n in range(NTILE) for d in range(2)]
            g1cut = 9
            for f_ in self.dbg:
                if f_.startswith("g1cut="):
                    g1cut = int(f_[6:])
            for g0 in range(0, len(probs), NP):
                grp = probs[g0:g0 + NP]
                if g1cut == 0 or ("g1one" in self.dbg and g0 > 0):
                    break
                for p, (n, d) in enumerate(grp):
                    w = WK[p]
                    gcol = GL[:, n, d * 4 + h:d * 4 + h + 1]
                    k.ts(w["gB"][:, :], self.onesf[:, :], gcol, None, ALU.mult)
                    k.ts(w["ngB"][:, :], self.onesf[:, :], gcol, -1.0, ALU.mult, ALU.mult)
                for p, (n, d) in enumerate(grp):
                    w = WK[p]
                    k.mm(rv(p, "D"), w["gB"][:, :], Ur[d][:, :], start=True, stop=False)
                    k.mm(rv(p, "D"), Ur[d][:, :], w["ngB"][:, :], start=False, stop=False)
                    k.mm(rv(p, "D"), identr[:, :], NEGr[d][:, :], start=False, stop=True)
                    kc = KT[:, n * 128:(n + 1) * 128]
                    k.mm(rv(p, "G"), kc, kc)
                    k.mm(rv(p, "KQ"), kc, QT[:, n * 128:(n + 1) * 128])
                for p, (n, d) in enumerate(grp):
                    w = WK[p]
                    k.act(w["GT"][:, :], rv(p, "D"), AF.Exp)
                if g1cut <= 1:
                    continue
                for p, (n, d) in enumerate(grp):
                    w = WK[p]
                    k.stt(w["AT"][:, :], rv(p, "G"), BETA[:, n, d * 4 + h:d * 4 + h + 1], w["GT"][:, :], ALU.mult, ALU.mult)
                    k.tt(QKM[:, d, n, :], rv(p, "KQ"), w["GT"][:, :], ALU.mult)
                    k.act(KTL[:, d, n, :], Ktok[:, n, :], AF.Copy, scale=EK[:, d, n:n + 1])
                lm = lambda d, lev: cst[:, C_LM + (d * 7 + lev) * 128:C_LM + (d * 7 + lev + 1) * 128]
                if g1cut <= 2:
                    continue
                for p, (n, d) in enumerate(grp):
                    w = WK[p]
                    k.tt(w["E"][0][:, :], w["AT"][:, :], lm(d, 0), ALU.mult)
                for p, (n, d) in enumerate(grp):
                    w = WK[p]
                    e0 = w["E"][0]
                    k.tr(rv(p, "TR"), e0.v(e0.h[:, :].bitcast(F32)), self.identf[:, :])
                for p, (n, d) in enumerate(grp):
                    w = WK[p]
                    e0 = w["E"][0]
                    k.tt(w["Wm"][:, :], self.identf[:, :], e0.v(e0.h[:, :].bitcast(F32)), ALU.subtract)
                    k.tt(w["Vm"][:, :], self.identf[:, :], rv(p, "TR"), ALU.subtract)
                for lev in range(1, 7 if g1cut > 3 else 1):
                    for p, (n, d) in enumerate(grp):
                        w = WK[p]
                        k.tt(w["E"][lev % 2][:, :], w["AT"][:, :], lm(d, lev), ALU.mult, eng=self.e_eng)
                    for p, (n, d) in enumerate(grp):
                        w = WK[p]
                        k.mm(rv(p, "Y"), w["E"][lev % 2][:, :], w["Vm"][:, :])
                    for p, (n, d) in enumerate(grp):
                        w = WK[p]
                        k.act(w["Ysb"][:, :], rv(p, "Y"), AF.Copy)
                    for p, (n, d) in enumerate(grp):
                        w = WK[p]
                        k.mm(rv(p, "Z"), w["Wm"][:, :], w["Ysb"][:, :])
                    for p, (n, d) in enumerate(grp):
                        w = WK[p]
                        vm = w["Vm"]
                        k.tt(vm[:, :], vm.v(vm.h[:, :].bitcast(F32)), rv(p, "Z"), ALU.subtract)
                    for p, (n, d) in enumerate(grp):
                        w = WK[p]
                        vm = w["Vm"]
                        k.tr(rv(p, "TR"), vm.v(vm.h[:, :].bitcast(F32)), self.identf[:, :])
                    for p, (n, d) in enumerate(grp):
                        w = WK[p]
                        if lev < 6:
                            k.act(w["Wm"][:, :], rv(p, "TR"), AF.Copy)
                        else:
                            k.act(TTb[:, d, n, :], rv(p, "TR"), AF.Copy)
            k.pop_scope()
            if "gdbg" in self.dbg:
                k.dma("sp", DV(self.dbg_t["dT"][:, :, :, :], k.dbuf("dT")), TTb[:, :, :, :])
                k.dma("sp", DV(self.dbg_t["dQKM"][:, :, :, :], k.dbuf("dQKM")), QKM[:, :, :, :])
                k.dma("sp", DV(self.dbg_t["dKTL"][:, :, :, :], k.dbuf("dKTL")), KTL[:, :, :, :])
                for ii, tt_ in enumerate((ECUM, NECUM, ETOT, EK)):
                    k.dma("sp", DV(self.dbg_t["dE"][:, ii, :, :], k.dbuf("dE")), tt_[:, :, :])
                if "g_g1_only" in self.dbg:
                    k.pop_scope()
                    continue
            SR = []
            for d in range(2):
                b = [self.PS[4 * d + i] for i in range(4)]
                SR.append(dict(KS=(k.region(b[0], f"sKS{d}"), 0), QS=(k.region(b[0], f"sQS{d}"), 128),
                               VN=(k.region(b[1], f"sVN{d}"), 0), O=(k.region(b[2], f"sO{d}"), 0),
                               SD=(k.region(b[3], f"sSD{d}"), 0)))

            def sv_(d, nm):
                t, c0 = SR[d][nm]
                return t[:, c0:c0 + 128]

            S = [k.sb(f"gS{d}", [128, 128], F32) for d in range(2)]
            Sb = [k.sb(f"gSb{d}", [128, 128], BF16) for d in range(2)]
            Rb = [k.sb(f"gRb{d}", [128, 128], BF16) for d in range(2)]
            VNb = [k.sb(f"gVNb{d}", [128, 128], BF16) for d in range(2)]
            for d in range(2):
                k.memset(S[d][:, :], 0.0)
                k.memset(Sb[d][:, :], 0.0)
            order = [[32, 33] + list(range(32)), [33, 32] + list(range(31, -1, -1))]
            for step in range(NTILE):
                for d in range(2):
                    n = order[d][step]
                    col = d * 4 + h
                    need_o = (n < 32) or ctx_out
                    k.mm(sv_(d, "KS"), KT[:, n * 128:(n + 1) * 128], Sb[d][:, :])
                    if need_o:
                        k.mm(sv_(d, "QS"), QT[:, n * 128:(n + 1) * 128], Sb[d][:, :])
                    k.stt(Rb[d][:, :], sv_(d, "KS"), NECUM[:, d, n:n + 1], Vtok[:, n, :], ALU.mult, ALU.add)
                    k.mm(sv_(d, "VN"), TTb[:, d, n, :], Rb[d][:, :])
                    k.act(VNb[d][:, :], sv_(d, "VN"), AF.Copy, scale=BETA[:, n, col:col + 1])
                    k.mm(sv_(d, "SD"), KTL[:, d, n, :], VNb[d][:, :])
                    if need_o:
                        k.mm(sv_(d, "O"), QKM[:, d, n, :], VNb[d][:, :])
                    k.stt(S[d][:, :], S[d][:, :], ETOT[:, d, n:n + 1], sv_(d, "SD"), ALU.mult, ALU.add)
                    k.act(Sb[d][:, :], S[d][:, :], AF.Copy)
                    if need_o:
                        k.stt(OA[:, n, :], sv_(d, "QS"), ECUM[:, d, n:n + 1], OA[:, n, :], ALU.mult, ALU.add, eng="pool") \
                            if False else k.stt(OA[:, n, :], sv_(d, "QS"), ECUM[:, d, n:n + 1], OA[:, n, :], ALU.mult, ALU.add)
                        k.tt(OA[:, n, :], OA[:, n, :], sv_(d, "O"), ALU.add)
            k.barrier_all()
            if "gdbg" in self.dbg:
                k.dma("sp", DV(self.dbg_t["dOA"][:, :, :], k.dbuf("dOA")), OA[:, :, :])
            ntl = NTILE if ctx_out else 32
            o2 = k.sb("gO2", [128, 4, 128], F32)
            junk = k.sb("gJunk", [128, 128], F32)
            ssq = k.sb("gSSQ", [128, NTILE], F32)
            k.memset(ssq[:, :], 0.0)
            for n in range(ntl):
                k.act(junk[:, :], OA[:, n, :], AF.Square, accum=ssq[:, n:n + 1])
            k.act(ssq[:, 0:ntl], ssq[:, 0:ntl], AF.Sqrt, scale=1.0 / 128, bias=self.epsc[:, 0:1])
            k.recip(ssq[:, 0:ntl], ssq[:, 0:ntl])
            ggt = [k.sb(f"gGG{i}", [128, 128], F32) for i in range(2)]
            yb = [k.sb(f"gYb{i}", [128, 4, 128], BF16) for i in range(2)]
            ybT = [k.sb(f"gYbT{i}", [128, 512], BF16) for i in range(2)]
            for g4 in range(0, ntl, 4):
                gi = (g4 // 4) % 2
                nt = min(4, ntl - g4)
                for q in range(nt):
                    n = g4 + q
                    gg = ggt[n % 2]
                    k.dma("sp", gg[:, :], DV(self.GG[n * 128:(n + 1) * 128, h * 128:(h + 1) * 128], k.dbuf(("GG", n))))
                    k.stt(o2[:, n % 4, :], OA[:, n, :], ssq[:, n:n + 1], OG, ALU.mult, ALU.mult)
                    k.tt(yb[gi][:, q, :], o2[:, n % 4, :], gg[:, :], ALU.mult)
                ps = self.PS[gi]
                psb = ps.v(ps.h[:, :].bitcast(BF16))
                for q in range(nt):
                    k.tr(DV(psb.ap[:, q * 128:(q + 1) * 128], *psb.bufs), yb[gi][:, q, :], self.identb[:, :])
                k.act(ybT[gi][:, 0:nt * 128], DV(psb.ap[:, 0:nt * 128], *psb.bufs), AF.Copy)
                blk = min(g4 // 4, 8)
                k.dma("sp", DV(self.YT[4 + h][:, g4 * 128:(g4 + nt) * 128],
                               *[k.dbuf(("YT", 4 + h, blk, q)) for q in range(4)]), ybT[gi][:, 0:nt * 128])
            k.pop_scope()
        k.pop_scope()


_CACHE = {}


def _prep_inputs(inp, ncores=8):
    cst, rope = host_consts()
    rpbg = host_rpb(np.asarray(inp["na_rpb"], np.float32))
    maps = []
    shared = {n: np.ascontiguousarray(inp[n], dtype=np.float32) for n in
              ("w_mod", "w_ffn1_in", "w_ffn2_in", "w_ffn1_out", "w_ffn2_out", "w_in", "w_out")}
    for b in range(ncores):
        m = {"x": np.ascontiguousarray(inp["x"][b], dtype=np.float32),
             "ctx": np.ascontiguousarray(inp["ctx"][b], dtype=np.float32),
             "sv": host_smallvec(inp, b), "cst": cst, "rope": rope, "rpbg": rpbg}
        m.update(shared)
        maps.append(m)
    return maps


def kernel(**inputs):
    inp = {k_: np.asarray(v) for k_, v in inputs.items()}
    if "nc" not in _CACHE:
        _CACHE["nc"] = Prog().build()
    nc = _CACHE["nc"]
    maps = _prep_inputs(inp)
    res = run_bass_kernel_spmd(nc, maps, core_ids=list(range(8)))
    out = np.stack([np.asarray(r["out"], dtype=np.float32) for r in res.results], axis=0)
    return out
```

```python
import numpy as np
import ml_dtypes
import concourse.bass as bass
import concourse.mybir as mybir
from concourse.bass_utils import run_bass_kernel_spmd
from contextlib import ExitStack

F32 = mybir.dt.float32
F32R = mybir.dt.float32r
BF16 = mybir.dt.bfloat16
AF = mybir.ActivationFunctionType
ALU = mybir.AluOpType
AX = mybir.AxisListType

NLAYERS = 4
D = 1024
FF = 2816
TLAT = 4096
TCTX = 256
TT = TLAT + TCTX
NTILE = TT // 128
INC = 3600
EPS = 1e-6


class Buf:
    __slots__ = ("name", "wev", "rev", "sg")

    def __init__(self, name, sg=None):
        self.name = name
        self.wev = {}
        self.rev = {}
        self.sg = sg if sg is not None else name


class V:
    __slots__ = ("ap", "bufs")

    def __init__(self, ap, bufs):
        self.ap = ap
        self.bufs = bufs


def DV(ap, *bufs):
    return V(ap, tuple(bufs))


class Tile:
    def __init__(self, h, name, sg=None):
        self.h = h
        self.buf = Buf(name, sg)

    def __getitem__(self, idx):
        return V(self.h[idx], (self.buf,))

    def v(self, ap):
        return V(ap, (self.buf,))


class Eng:
    def __init__(self, name, h, sem, key):
        self.name = name
        self.h = h
        self.sem = sem
        self.key = key
        self.cnt = 0
        self.seen = {}


class KB:
    def __init__(self, nc, es):
        self.nc = nc
        self.ges = es
        self.es = es
        self.sems = {}
        self.engs = {}
        for name, h in (("pe", nc.tensor), ("act", nc.scalar), ("dve", nc.vector),
                        ("pool", nc.gpsimd), ("sp", nc.sync)):
            sem = es.enter_context(nc.semaphore("s_" + name))
            key = "E" + name
            self.sems[key] = [sem, 0]
            self.engs[name] = Eng(name, h, sem, key)
        self.ninstr = 0
        self.uid = 0
        self.dbufs = {}

    def push_scope(self):
        if not hasattr(self, "stack"):
            self.stack = []
        self.stack.append(self.es)
        self.es = ExitStack()
        self.es.__enter__()

    def pop_scope(self):
        self.barrier_all()
        self.es.__exit__(None, None, None)
        self.es = self.stack.pop()

    def region(self, tile, name):
        return tile

    def sb(self, name, shape, dt, sg=None):
        self.uid += 1
        h = self.es.enter_context(self.nc.sbuf_tensor(f"{name}_{self.uid}", list(shape), dt))
        return Tile(h, name, sg)

    def ps(self, name, shape, dt=F32):
        h = self.es.enter_context(self.nc.psum_tensor(name, list(shape), dt))
        return Tile(h, name)

    def dbuf(self, key, sg=None):
        b = self.dbufs.get(key)
        if b is None:
            b = Buf("@" + str(key), sg)
            self.dbufs[key] = b
        return b

    def dsem_key(self, buf):
        key = "D" + buf.sg
        if key not in self.sems:
            sem = self.ges.enter_context(self.nc.semaphore("d%d" % len(self.sems)))
            self.sems[key] = [sem, 0]
        return key

    def _waits(self, E, rb, wb):
        need = {}
        for b in rb:
            for k_, v in b.wev.items():
                if need.get(k_, 0) < v:
                    need[k_] = v
        for b in wb:
            for k_, v in b.wev.items():
                if need.get(k_, 0) < v:
                    need[k_] = v
            for k_, v in b.rev.items():
                if need.get(k_, 0) < v:
                    need[k_] = v
        for k_, v in need.items():
            if E.name == "pe" and k_ == E.key:
                continue
            if k_[0] == "D":
                v = self.sems[k_][1]
            if E.seen.get(k_, 0) >= v:
                continue
            E.h.wait_ge(self.sems[k_][0], v)
            E.seen[k_] = v

    def op(self, eng, fn, reads, writes, sig=True):
        E = self.engs[eng]
        rb = [b for v in reads for b in v.bufs]
        wb = [b for v in writes for b in v.bufs]
        self._waits(E, rb, wb)
        ins = fn(E.h)
        if sig:
            E.cnt += 1
            ins.then_inc(E.sem, 1)
            self.sems[E.key][1] = E.cnt
            ev = E.cnt
        else:
            ev = E.cnt + 1
        for b in rb:
            b.rev[E.key] = ev
        for b in wb:
            b.wev = {E.key: ev}
            b.rev = {}
        self.ninstr += 1
        return ins

    def dma(self, eng, out, in_, sembuf=None, **kw):
        E = self.engs[eng]
        rb = list(in_.bufs)
        wb = list(out.bufs)
        self._waits(E, rb, wb)
        if sembuf is None:
            sembuf = wb[0] if wb[0].name[0] != "@" else rb[0]
        key = self.dsem_key(sembuf)
        ins = E.h.dma_start(out=out.ap, in_=in_.ap, **kw)
        self.sems[key][1] += 16
        ins.then_inc(self.sems[key][0], 16)
        val = self.sems[key][1]
        for b in rb:
            b.rev[key] = val
        for b in wb:
            b.wev = {key: val}
            b.rev = {}
        self.ninstr += 1
        return ins

    def barrier_all(self):
        for E in self.engs.values():
            for k_, (sem, tot) in self.sems.items():
                if tot == 0 or (k_ == E.key and E.name == "pe"):
                    continue
                if E.seen.get(k_, 0) >= tot:
                    continue
                E.h.wait_ge(sem, tot)
                E.seen[k_] = tot

    def finish(self):
        E = self.engs["sp"]
        for k_, (sem, tot) in self.sems.items():
            if tot == 0 or E.seen.get(k_, 0) >= tot:
                continue
            E.h.wait_ge(sem, tot)
            E.seen[k_] = tot

    def _pe_guard(self, is_r):
        if getattr(self, "pe_last_r", False) and not is_r and getattr(self, "dummy", None) is not None:
            o, l, r = self.dummy
            self.op("pe", lambda h: h.matmul(o.ap, l.ap, r.ap, start=True, stop=True), [l, r], [o])
        self.pe_last_r = is_r

    def mm(self, out, lhsT, rhs, start=True, stop=True, sig=None, **kw):
        self._pe_guard(lhsT.ap.dtype == F32R)
        if sig is None:
            sig = bool(stop)
        return self.op("pe", lambda h: h.matmul(out.ap, lhsT.ap, rhs.ap, start=start, stop=stop, **kw),
                       [lhsT, rhs], [out], sig=sig)

    def tr(self, out, in_, ident):
        self._pe_guard(False)
        return self.op("pe", lambda h: h.transpose(out.ap, in_.ap, ident.ap), [in_, ident], [out])

    def act(self, out, in_, func, bias=None, scale=None, accum=None):
        kw = {}
        reads = [in_]
        writes = [out]
        if bias is not None:
            if isinstance(bias, V):
                kw["bias"] = bias.ap
                reads.append(bias)
            else:
                kw["bias"] = bias
        if scale is not None:
            if isinstance(scale, V):
                kw["scale"] = scale.ap
                reads.append(scale)
            else:
                kw["scale"] = scale
        if accum is not None:
            kw["accum_out"] = accum.ap
            writes.append(accum)
        return self.op("act", lambda h: h.activation(out.ap, in_.ap, func, **kw), reads, writes)

    def tt(self, out, a, b, op, eng="dve"):
        return self.op(eng, lambda h: h.tensor_tensor(out.ap, a.ap, b.ap, op), [a, b], [out])

    def ts(self, out, a, s1, s2, op0, op1=None, eng="dve"):
        reads = [a]
        x1 = s1.ap if isinstance(s1, V) else s1
        x2 = s2.ap if isinstance(s2, V) else s2
        if isinstance(s1, V):
            reads.append(s1)
        if isinstance(s2, V):
            reads.append(s2)
        kw = {}
        if op1 is not None:
            kw["op1"] = op1
        return self.op(eng, lambda h: h.tensor_scalar(out.ap, a.ap, x1, x2, op0, **kw), reads, [out])

    def stt(self, out, a, s, b, op0, op1, eng="dve"):
        reads = [a, b]
        x = s.ap if isinstance(s, V) else s
        if isinstance(s, V):
            reads.append(s)
        return self.op(eng, lambda h: h.scalar_tensor_tensor(out.ap, a.ap, x, b.ap, op0, op1), reads, [out])

    def copy(self, out, in_, eng="dve"):
        return self.op(eng, lambda h: h.tensor_copy(out.ap, in_.ap), [in_], [out])

    def memset(self, out, val, eng="dve"):
        return self.op(eng, lambda h: h.memset(out.ap, val), [], [out])

    def recip(self, out, in_, eng="dve"):
        return self.op(eng, lambda h: h.reciprocal(out.ap, in_.ap), [in_], [out])

    def reduce(self, out, in_, op, axis=AX.X, eng="dve"):
        return self.op(eng, lambda h: h.tensor_reduce(out.ap, in_.ap, axis, op), [in_], [out])


C_U = 0
C_NEG = 256
C_LM = 512
C_RM = 512 + 14 * 128
C_CM = C_RM + 128
CST_COLS = C_CM + 12 * 128

NA_DR0 = [1, 3, 5, 7, 9, 11, 13, 3, 5, 7, 9, 11]


def host_consts():
    i = np.arange(128)
    cst = np.zeros((128, CST_COLS), np.float32)
    cst[:, C_U:C_U + 128] = (i[:, None] <= i[None, :])
    cst[:, C_U + 128:C_U + 256] = (i[:, None] >= i[None, :])
    jj, ii = i[:, None], i[None, :]
    cst[:, C_NEG:C_NEG + 128] = np.where(ii >= jj, 0.0, -30000.0)
    cst[:, C_NEG + 128:C_NEG + 256] = np.where(ii <= jj, 0.0, -30000.0)
    for k in range(7):
        b = 2 ** k
        same = (ii // (2 * b)) == (jj // (2 * b))
        cst[:, C_LM + k * 128:C_LM + (k + 1) * 128] = same & ((ii % (2 * b)) >= b) & ((jj % (2 * b)) < b)
        cst[:, C_LM + (7 + k) * 128:C_LM + (8 + k) * 128] = same & ((ii % (2 * b)) < b) & ((jj % (2 * b)) >= b)
    rm = np.zeros((128, 128), np.float32)
    for d in range(128):
        q = (d % 64) // 32
        if q == 0:
            rm[d + 32, d] = -1.0
        else:
            rm[d - 32, d] = 1.0
    cst[:, C_RM:C_RM + 128] = rm
    cols = np.arange(64)
    cs = np.clip(cols - 8, 0, 48)
    col_ok = (cols[None, :] >= cs[:, None]) & (cols[None, :] < cs[:, None] + 16)
    nm = np.zeros((2, 64, 12, 2, 64), np.float32)
    for s, dr0 in enumerate(NA_DR0):
        for kl in range(2):
            for ql in range(2):
                dr = dr0 + kl - ql
                ok = 1.0 if (s < 7 or (3 <= dr <= 10)) else 0.0
                nm[kl, :, s, ql, :] = col_ok.T * ok
    cst[:, C_CM:C_CM + 12 * 128] = nm.reshape(128, 12 * 128)
    inv = 10000.0 ** (-np.arange(32, dtype=np.float64) / 32.0)
    t = np.arange(TLAT)
    row = (t // 64).astype(np.float64)[:, None] * inv
    col = (t % 64).astype(np.float64)[:, None] * inv
    ang = np.concatenate([row, row, col, col], axis=-1)
    rope = np.stack([np.cos(ang).T, np.sin(ang).T], axis=1).astype(np.float32)
    return cst, np.ascontiguousarray(rope)


SV_NORM = 0
SV_BMOD = SV_NORM + NLAYERS * 24
SV_CC = SV_BMOD + NLAYERS * 72
SV_QKG = SV_CC + 16
SV_CONV = SV_QKG + NLAYERS * 2
SV_ALOG = SV_CONV + NLAYERS * 60
SV_DTB = SV_ALOG + NLAYERS * 8
SV_OG = SV_DTB + NLAYERS * 8
SV_COLS = SV_OG + NLAYERS * 128


def host_smallvec(inp, b):
    sv = np.zeros((128, SV_COLS), np.float32)
    for l in range(NLAYERS):
        for n, nm in enumerate(("norm_ffn1", "norm_mix", "norm_ffn2")):
            sv[:, SV_NORM + (l * 3 + n) * 8:SV_NORM + (l * 3 + n + 1) * 8] = inp[nm][l].reshape(8, 128).T
        sv[:, SV_BMOD + l * 72:SV_BMOD + (l + 1) * 72] = inp["b_mod"][l].reshape(72, 128).T
        sv[:, SV_QKG + l * 2] = np.tile(inp["na_q_gain"][l], 2)
        sv[:, SV_QKG + l * 2 + 1] = np.tile(inp["na_k_gain"][l], 2)
        sv[:, SV_CONV + l * 60:SV_CONV + (l + 1) * 60] = \
            inp["dn_conv"][l].reshape(5, 12, 128).transpose(2, 0, 1).reshape(128, 60)
        sv[:, SV_ALOG + l * 8:SV_ALOG + (l + 1) * 8] = inp["dn_a_log"][l].reshape(1, 8)
        sv[:, SV_DTB + l * 8:SV_DTB + (l + 1) * 8] = inp["dn_dt_bias"][l].reshape(1, 8)
        sv[:, SV_OG + l * 128:SV_OG + (l + 1) * 128] = inp["dn_out_gain"][l].reshape(1, 128)
    cc = np.stack([inp["c"][b], inp["c_ctx"]], axis=-1)
    sv[:, SV_CC:SV_CC + 16] = cc.reshape(8, 128, 2).transpose(1, 0, 2).reshape(128, 16)
    return sv


def host_rpb(rpb):
    kc = np.arange(64)[:, None]
    qc = np.arange(64)[None, :]
    dc = np.clip(kc - qc + 15, 0, 30)
    out = np.zeros((rpb.shape[0], 8, 2, 64, 12, 2, 64), np.float32)
    for s, dr0 in enumerate(NA_DR0):
        for kl in range(2):
            for ql in range(2):
                dr = int(np.clip(dr0 + kl - ql, 0, 14))
                out[:, :, kl, :, s, ql, :] = rpb[:, :, dr][:, :, dc]
    return np.ascontiguousarray(out.reshape(rpb.shape[0], 8, 128, 12 * 128))


BLOCKS = [(i * 512, 512, 0) for i in range(8)] + [(TLAT, TCTX, 1)]


class Prog:
    def __init__(self, nl=NLAYERS, dbg=(), dump=()):
        self.nl = nl
        self.dbg = set(dbg)
        dump = set(dump)
        nc = bass.Bass("TRN2", target_bir_lowering=False)
        self.nc = nc
        di = lambda n, s, dt=F32: nc.dram_tensor(n, list(s), dt, kind="ExternalInput")
        self.x_d = di("x", [TLAT, D])
        self.ctx_d = di("ctx", [TCTX, D])
        self.sv_d = di("sv", [128, SV_COLS])
        self.cst_d = di("cst", [128, CST_COLS])
        self.rope_d = di("rope", [128, 2, TLAT])
        self.rpb_d = di("rpbg", [NLAYERS, 8, 128, 12 * 128])
        self.wmod_d = di("w_mod", [NLAYERS, D, 9 * D])
        self.w1_d = [di("w_ffn1_in", [NLAYERS, D, 2 * FF]), di("w_ffn2_in", [NLAYERS, D, 2 * FF])]
        self.w2_d = [di("w_ffn1_out", [NLAYERS, FF, D]), di("w_ffn2_out", [NLAYERS, FF, D])]
        self.win_d = di("w_in", [NLAYERS, D, INC])
        self.wout_d = di("w_out", [NLAYERS, D, D])
        self.out_d = nc.dram_tensor("out", [TLAT, D], F32, kind="ExternalOutput")
        ds = lambda n, s, dt: nc.dram_tensor(n, list(s), dt, kind=("ExternalOutput" if n in dump else "Internal"))
        self.XT = ds("XT", [D, TT], F32)
        self.QKT = ds("QKT", [8, 128, TT], BF16)
        self.VA = ds("VA", [TT, 512], BF16)
        self.GQ = ds("GQ", [12, 128, TT], F32)
        self.GG = ds("GG", [TT, 512], F32)
        self.BD = ds("BD", [TT, 16], F32)
        self.YT = ds("YT", [8, 128, TT], BF16)
        self.W1s = [[ds(f"W1s_{l}_{w}", [22, 128, 8, 256], BF16) for w in range(2)] for l in range(nl)]
        self.W2s = [[ds(f"W2s_{l}_{w}", [8, 128, 22, 128], BF16) for w in range(2)] for l in range(nl)]
        self.WA = [ds(f"WA_{l}", [20, 128, 8, 128], BF16) for l in range(nl)]
        self.WB = [ds(f"WB_{l}", [128, 8, 1040], BF16) for l in range(nl)]
        self.WO = [ds(f"WO_{l}", [8, 128, 8, 128], BF16) for l in range(nl)]
        self.dbg_out = {}
        self.e_eng = "pool" if "epool" in self.dbg else "dve"
        self.dbg_t = {}
        if "gdbg" in self.dbg:
            do = lambda n, sh, dt: nc.dram_tensor("dbg_" + n, list(sh), dt, kind="ExternalOutput")
            self.dbg_t = dict(dq=do("dq", [128, TT], BF16), dk=do("dk", [128, TT], BF16), dv=do("dv", [128, NTILE, 128], F32),
                              dT=do("dT", [128, 2, NTILE, 128], BF16), dQKM=do("dQKM", [128, 2, NTILE, 128], BF16),
                              dKTL=do("dKTL", [128, 2, NTILE, 128], BF16), dOA=do("dOA", [128, NTILE, 128], F32),
                              dE=do("dE", [128, 4, 2, NTILE], F32), dGL=do("dGL", [128, NTILE, 8], F32),
                              dBETA=do("dBETA", [128, NTILE, 8], F32))

    def dbg_tensor(self, name, shape, dt=F32):
        t = self.nc.dram_tensor("dbg_" + name, list(shape), dt, kind="ExternalOutput")
        self.dbg_out[name] = t
        return t

    def build(self):
        nc = self.nc
        with ExitStack() as es:
            k = KB(nc, es)
            self.k = k
            self.PS = [k.ps(f"ps{i}", [128, 512]) for i in range(8)]
            self.psd = k.region(self.PS[7], "psd")
            self.setup_consts()
            k.dummy = None
            self.convert_weights(0)
            self.compute_mods([0])
            self.init_xt()
            for l in range(self.nl + 1):
                self.row_pass(l)
                if l < self.nl:
                    if "stop_inproj" in self.dbg and l == 0:
                        break
                    self.mixer(l)
                    if ("stop_na" in self.dbg or "stop_gdn" in self.dbg) and l == 0:
                        break
            k.finish()
        return nc

    def setup_consts(self):
        k = self.k
        self.cst = k.sb("cst", [128, CST_COLS], F32)
        k.dma("sp", self.cst[:, :], DV(self.cst_d[:, :], k.dbuf("cst_d")))
        self.sv = k.sb("sv", [128, SV_COLS], F32)
        k.dma("sp", self.sv[:, :], DV(self.sv_d[:, :], k.dbuf("sv_d")))
        self.identf = k.sb("identf", [128, 128], F32)
        k.memset(self.identf[:, :], 0.0)
        k.op("pool", lambda h: h.affine_select(out=self.identf.h[:, :], in_=self.identf.h[:, :],
                                               pattern=[[-1, 128]], compare_op=ALU.not_equal, fill=1.0,
                                               base=0, channel_multiplier=1),
             [self.identf[:, :]], [self.identf[:, :]])
        self.identb = k.sb("identb", [128, 128], BF16)
        k.copy(self.identb[:, :], self.identf[:, :])
        self.onesb = k.sb("onesb", [128, 128], BF16)
        k.memset(self.onesb[:, :], 1.0)
        self.onesf = k.sb("onesf", [128, 128], F32)
        k.memset(self.onesf[:, :], 1.0)
        self.epsc = k.sb("epsc", [128, 1], F32)
        k.memset(self.epsc[:, :], EPS)
        self.modT = [k.sb(f"modT{l}", [128, 72, 2], F32) for l in range(self.nl)]
        self.modA = [k.sb(f"modA{l}", [128, 3, 8, 2], F32) for l in range(self.nl)]
        self.modH = [k.sb(f"modH{l}", [128, 3, 8, 2], F32) for l in range(self.nl)]

    def convert_weights(self, l):
        k = self.k
        for w in range(2):
            src = self.w1_d[w][l].rearrange("(c p) (two j n) -> j p c two n", p=128, two=2, j=22, n=128)
            sg = f"cv1_{l}_{w}"
            for j in range(22):
                b = k.dbuf(("W1s", l, w, j), sg)
                for two in range(2):
                    k.dma("pool", DV(self.W1s[l][w][j][:, :, two * 128:(two + 1) * 128], b),
                          DV(src[j][:, :, two, :], k.dbuf("win")), sembuf=b)
            src = self.w2_d[w][l].rearrange("(j p) (f n) -> f p j n", p=128, n=128)
            sg = f"cv2_{l}_{w}"
            for f in range(8):
                b = k.dbuf(("W2s", l, w, f), sg)
                k.dma("pool", DV(self.W2s[l][w][f], b), DV(src[f], k.dbuf("win")), sembuf=b)
        sg = f"cva_{l}"
        colsA = [j * 128 for j in range(8)] + [1536 + j * 128 for j in range(12)]
        srcw = self.win_d[l].rearrange("(c p) n -> p c n", p=128)
        for j in range(20):
            b = k.dbuf(("WA", l, j), sg)
            k.dma("pool", DV(self.WA[l][j], b), DV(srcw[:, :, colsA[j]:colsA[j] + 128], k.dbuf("win")), sembuf=b)
        b = k.dbuf(("WB", l), sg)
        k.dma("pool", DV(self.WB[l][:, :, 0:512], b), DV(srcw[:, :, 1024:1536], k.dbuf("win")), sembuf=b)
        k.dma("pool", DV(self.WB[l][:, :, 512:1040], b), DV(srcw[:, :, 3072:3600], k.dbuf("win")), sembuf=b)
        src = self.wout_d[l].rearrange("(c p) (f n) -> f p c n", p=128, n=128)
        for f in range(8):
            b = k.dbuf(("WO", l, f), sg)
            k.dma("pool", DV(self.WO[l][f], b), DV(src[f], k.dbuf("win")), sembuf=b)

    def compute_mods(self, layers):
        k = self.k
        sv = self.sv
        k.push_scope()
        cs = k.sb("mod_cs", [128, 8, 2], F32)
        k.act(cs[:, :, :], sv.v(sv.h[:, SV_CC:SV_CC + 16].rearrange("p (c s) -> p c s", s=2)), AF.Silu)
        wt = [k.sb(f"mod_w{i}", [128, 1024], F32) for i in range(3)]
        n = 0
        for l in layers:
            for cb in range(9):
                for kc in range(8):
                    t = wt[n % 3]
                    n += 1
                    k.dma("sp", t[:, :], DV(self.wmod_d[l][kc * 128:(kc + 1) * 128, cb * 1024:(cb + 1) * 1024],
                                            k.dbuf("win")))
                    for jj in range(8):
                        k.mm(self.PS[jj][:, 0:2], t[:, jj * 128:(jj + 1) * 128], cs[:, kc, :],
                             start=(kc == 0), stop=(kc == 7), sig=True)
                for jj in range(8):
                    j = cb * 8 + jj
                    k.ts(self.modT[l][:, j, :], self.PS[jj][:, 0:2],
                         sv[:, SV_BMOD + l * 72 + j:SV_BMOD + l * 72 + j + 1], None, ALU.add)
            for nn in range(3):
                g = sv.v(sv.h[:, SV_NORM + (l * 3 + nn) * 8:SV_NORM + (l * 3 + nn + 1) * 8]
                         .unsqueeze(2).broadcast_to([128, 8, 2]))
                sc = self.modT[l][:, (nn * 3 + 1) * 8:(nn * 3 + 2) * 8, :]
                k.stt(self.modA[l][:, nn, :, :], sc, 1.0, g, ALU.add, ALU.mult)
                gt = self.modT[l][:, (nn * 3 + 2) * 8:(nn * 3 + 3) * 8, :]
                k.ts(self.modH[l][:, nn, :, :], gt, 0.5 if nn != 1 else 1.0, None, ALU.mult)
        k.pop_scope()

    def init_xt(self):
        k = self.k
        k.push_scope()
        xin = [k.sb(f"ix{i}", [128, D], F32) for i in range(2)]
        xo = [k.sb(f"ixo{i}", [128, 8, 128], F32) for i in range(2)]
        for ti in range(NTILE):
            t = xin[ti % 2]
            if ti < 32:
                src = DV(self.x_d[ti * 128:(ti + 1) * 128, :], k.dbuf("x_d"))
            else:
                src = DV(self.ctx_d[(ti - 32) * 128:(ti - 31) * 128, :], k.dbuf("x_d"))
            k.dma("sp", t[:, :], src)
            o = xo[ti % 2]
            for half in range(2):
                ps = self.PS[(ti % 2) * 2 + half]
                for c4 in range(4):
                    c = half * 4 + c4
                    k.tr(ps[:, c4 * 128:(c4 + 1) * 128], t[:, c * 128:(c + 1) * 128], self.identf[:, :])
                eng_copy = k.copy if half == 0 else (lambda o_, i_: k.act(o_, i_, AF.Copy))
                eng_copy(o.v(o.h[:, half * 4:(half + 1) * 4, :]),
                         ps.v(ps.h[:, :].rearrange("p (c n) -> p c n", n=128)))
            blk = min(ti // 4, 8)
            dst = self.XT[:, ti * 128:(ti + 1) * 128].rearrange("(c p) n -> p c n", p=128)
            k.dma("sp", DV(dst, *[k.dbuf(("XT", blk, c, ti % 4)) for c in range(8)]), o[:, :, :])
        k.pop_scope()

    def xbufs(self, blk, c):
        return [self.k.dbuf(("XT", blk, c, q)) for q in range(4)]

    def row_pass(self, l):
        k = self.k
        nl = self.nl
        k.push_scope()
        P = self
        P.xT = [[k.sb(f"xT{s}_{c}", [128, 512], F32, sg=f"xT{s}") for c in range(8)] for s in range(2)]
        P.hT = [k.sb(f"hT{c}", [128, 512], BF16) for c in range(8)]
        P.actT = [k.sb(f"actT{j}", [128, 512], BF16) for j in range(22)]
        P.sq = [k.sb(f"sq{i}", [128, 512], BF16) for i in range(2)]
        P.rs = k.sb("rs", [128, 512], F32)
        P.tmp = [k.sb(f"tmp{i}", [128, 512], F32) for i in range(2)]
        P.sgt = [k.sb(f"sgt{i}", [128, 512], F32) for i in range(2)]
        P.w1t = [k.sb(f"w1t{i}", [128, 8, 256], BF16) for i in range(5)]
        P.w2t = [k.sb(f"w2t{i}", [128, 22, 128], BF16) for i in range(3)]
        P.w1n = 0
        P.w2n = 0
        if l > 0:
            P.yT = [k.sb(f"yT{s}", [128, 8, 512], BF16) for s in range(2)]
            P.wot = [k.sb(f"wot{i}", [128, 8, 128], BF16) for i in range(2)]
        if l < nl:
            P.wat = [k.sb(f"wat{i}", [128, 8, 128], BF16) for i in range(4)]
            P.wbt = k.sb("wbt", [128, 8, 1040], BF16)
            k.dma("sp", P.wbt[:, :, :], DV(self.WB[l][:, :, :], k.dbuf(("WB", l))))
            P.blk64 = k.sb("blk64", [128, 128], BF16)
            k.memset(P.blk64[:, :], 0.0)
            k.memset(P.blk64[0:64, 0:64], 1.0)
            k.memset(P.blk64[64:128, 64:128], 1.0)
            P.qkg = k.sb("qkg", [128, 2], F32)
            k.ts(P.qkg[:, 0:1], self.sv[:, SV_QKG + 2 * l:SV_QKG + 2 * l + 1], 0.125, None, ALU.mult)
            k.copy(P.qkg[:, 1:2], self.sv[:, SV_QKG + 2 * l + 1:SV_QKG + 2 * l + 2])
            P.rq = [k.sb(f"rq{i}", [128, 512], F32) for i in range(2)]
            P.qo = [k.sb(f"qo{i}", [128, 512], BF16) for i in range(2)]
            P.go = [k.sb(f"go{i}", [128, 512], F32) for i in range(2)]
            P.vo = [k.sb(f"vo{i}", [128, 512], BF16) for i in range(2)]
            P.gto = [k.sb(f"gto{i}", [128, 512], F32) for i in range(2)]
            P.bdo = [k.sb(f"bdo{i}", [128, 16], F32) for i in range(2)]
        else:
            P.oo = [k.sb(f"oo{i}", [128, D], F32) for i in range(2)]
        blocks = list(range(9)) if l < nl else list(range(8))

        def load(bi):
            blk = blocks[bi]
            t0, N, s = BLOCKS[blk]
            slot = bi % 2
            for c in range(8):
                k.dma("sp", P.xT[slot][c][:, :N], DV(self.XT[c * 128:(c + 1) * 128, t0:t0 + N], *self.xbufs(blk, c)))
            if l > 0:
                src = self.YT[:, :, t0:t0 + N].rearrange("c p n -> p c n")
                k.dma("sp", P.yT[slot][:, :, :N], DV(src, *[k.dbuf(("YT", c, blk, q)) for c in range(8) for q in range(4)]))

        load(0)
        for bi, blk in enumerate(blocks):
            if bi + 1 < len(blocks):
                load(bi + 1)
            t0, N, s = BLOCKS[blk]
            slot = bi % 2
            xT = P.xT[slot]
            if l > 0:
                lp = l - 1
                for f in range(8):
                    wo = P.wot[f % 2]
                    k.dma("sp", wo[:, :, :], DV(self.WO[lp][f], k.dbuf(("WO", lp, f))))
                    po = self.PS[5 + f % 2]
                    for c in range(8):
                        k.mm(po[:, :N], wo[:, c, :], P.yT[slot][:, c, :N], start=(c == 0), stop=(c == 7))
                    k.stt(xT[f][:, :N], po[:, :N], self.modH[lp][:, 1, f, s:s + 1], xT[f][:, :N], ALU.mult, ALU.add)
                self.norm(lp, 2, xT, N, s)
                self.ffn(lp, 1, xT, N, s, 2)
            if l < nl:
                self.norm(l, 0, xT, N, s)
                self.ffn(l, 0, xT, N, s, 0)
                for c in range(8):
                    k.dma("pool", DV(self.XT[c * 128:(c + 1) * 128, t0:t0 + N], *self.xbufs(blk, c)), xT[c][:, :N])
                if "x_ffn1" in self.dbg and l == 0:
                    pass
                self.norm(l, 1, xT, N, s)
                self.inproj(l, blk)
            else:
                for tt in range(N // 128):
                    o = P.oo[tt % 2]
                    for half in range(2):
                        ps = self.PS[1 + (tt % 2) * 2 + half]
                        for c4 in range(4):
                            c = half * 4 + c4
                            k.tr(ps[:, c4 * 128:(c4 + 1) * 128], xT[c][:, tt * 128:(tt + 1) * 128], self.identf[:, :])
                        if half == 0:
                            k.copy(o[:, 0:512], ps[:, :])
                        else:
                            k.act(o[:, 512:1024], ps[:, :], AF.Copy)
                    k.dma("pool", DV(self.out_d[t0 + tt * 128:t0 + (tt + 1) * 128, :], k.dbuf(("out", blk, tt))), o[:, :])
        k.pop_scope()

    def norm(self, l, nn, xT, N, s):
        k = self.k
        P = self
        ps = self.PS[0]
        for c in range(8):
            sq = P.sq[c % 2]
            k.act(sq[:, :N], xT[c][:, :N], AF.Square)
            k.mm(ps[:, :N], self.onesb[:, :], sq[:, :N], start=(c == 0), stop=(c == 7), sig=True)
        k.act(P.rs[:, :N], ps[:, :N], AF.Sqrt, scale=1.0 / D, bias=self.epsc[:, 0:1])
        k.recip(P.rs[:, :N], P.rs[:, :N])
        for c in range(8):
            tmp = P.tmp[c % 2]
            k.stt(tmp[:, :N], xT[c][:, :N], self.modA[l][:, nn, c, s:s + 1], P.rs[:, :N], ALU.mult, ALU.mult)
            k.act(P.hT[c][:, :N], tmp[:, :N], AF.Identity, bias=self.modT[l][:, nn * 24 + c, s:s + 1])

    def ffn(self, l, w, xT, N, s, nn):
        k = self.k
        P = self
        for j in range(22):
            wt = P.w1t[P.w1n % 5]
            P.w1n += 1
            k.dma("sp", wt[:, :, :], DV(self.W1s[l][w][j], k.dbuf(("W1s", l, w, j))))
            pg = self.PS[1 + 2 * (j % 2)]
            pu = self.PS[2 + 2 * (j % 2)]
            for c in range(8):
                k.mm(pg[:, :N], wt[:, c, 0:128], P.hT[c][:, :N], start=(c == 0), stop=(c == 7))
            for c in range(8):
                k.mm(pu[:, :N], wt[:, c, 128:256], P.hT[c][:, :N], start=(c == 0), stop=(c == 7))
            sg = P.sgt[j % 2]
            k.act(sg[:, :N], pg[:, :N], AF.Silu)
            k.tt(P.actT[j][:, :N], sg[:, :N], pu[:, :N], ALU.mult)
        for f in range(8):
            w2 = P.w2t[P.w2n % 3]
            P.w2n += 1
            k.dma("sp", w2[:, :, :], DV(self.W2s[l][w][f], k.dbuf(("W2s", l, w, f))))
            po = self.PS[5 + f % 2]
            for j in range(22):
                k.mm(po[:, :N], w2[:, j, :], P.actT[j][:, :N], start=(j == 0), stop=(j == 21))
            k.stt(xT[f][:, :N], po[:, :N], self.modH[l][:, nn, f, s:s + 1], xT[f][:, :N], ALU.mult, ALU.add)

    def inproj(self, l, blk):
        k = self.k
        P = self
        t0, N, s = BLOCKS[blk]
        n = 0
        for j in range(20):
            wa = P.wat[j % 4]
            k.dma("sp", wa[:, :, :], DV(self.WA[l][j], k.dbuf(("WA", l, j))))
            ps = self.PS[1 + 2 * (j % 2)]
            for c in range(8):
                k.mm(ps[:, :N], wa[:, c, :], P.hT[c][:, :N], start=(c == 0), stop=(c == 7))
            if j < 8:
                ps2 = self.PS[2 + 2 * (j % 2)]
                sq = P.sq[j % 2]
                k.act(sq[:, :N], ps[:, :N], AF.Square)
                k.mm(ps2[:, :N], P.blk64[:, :], sq[:, :N])
                rq = P.rq[j % 2]
                k.act(rq[:, :N], ps2[:, :N], AF.Sqrt, scale=1.0 / 64, bias=self.epsc[:, 0:1])
                k.recip(rq[:, :N], rq[:, :N])
                qo = P.qo[j % 2]
                gcol = P.qkg[:, 0:1] if j < 4 else P.qkg[:, 1:2]
                k.stt(qo[:, :N], ps[:, :N], gcol, rq[:, :N], ALU.mult, ALU.mult)
                k.dma("pool", DV(self.QKT[j][:, t0:t0 + N], k.dbuf(("QKT", j, blk))), qo[:, :N])
            else:
                go = P.go[j % 2]
                if j % 2 == 0:
                    k.copy(go[:, :N], ps[:, :N])
                else:
                    k.act(go[:, :N], ps[:, :N], AF.Copy)
                k.dma("pool", DV(self.GQ[j - 8][:, t0:t0 + N], k.dbuf(("GQ", j - 8, blk))), go[:, :N])
        for tt in range(N // 128):
            r0 = t0 + tt * 128
            ti = r0 // 128
            pv = self.PS[5]
            pgt = self.PS[6]
            pbd = self.PS[7]
            for c in range(8):
                k.mm(pv[:, :], P.hT[c][:, tt * 128:(tt + 1) * 128], P.wbt[:, c, 0:512], start=(c == 0), stop=(c == 7))
            for c in range(8):
                k.mm(pgt[:, :], P.hT[c][:, tt * 128:(tt + 1) * 128], P.wbt[:, c, 512:1024], start=(c == 0), stop=(c == 7))
            for c in range(8):
                k.mm(pbd[:, 0:16], P.hT[c][:, tt * 128:(tt + 1) * 128], P.wbt[:, c, 1024:1040], start=(c == 0), stop=(c == 7))
            vo = P.vo[tt % 2]
            k.copy(vo[:, :], pv[:, :])
            k.dma("pool", DV(self.VA[r0:r0 + 128, :], k.dbuf(("VA", ti))), vo[:, :])
            gto = P.gto[tt % 2]
            k.act(gto[:, :], pgt[:, :], AF.Silu)
            k.dma("pool", DV(self.GG[r0:r0 + 128, :], k.dbuf(("GG", ti))), gto[:, :])
            bdo = P.bdo[tt % 2]
            k.copy(bdo[:, :], pbd[:, 0:16])
            k.dma("pool", DV(self.BD[r0:r0 + 128, :], k.dbuf(("BD", ti))), bdo[:, :])

    def mixer(self, l):
        if l + 1 < self.nl:
            k = self.k
            E = k.engs["pool"]
            for key in ("Epe", "Eact", "Edve"):
                sem, tot = k.sems[key]
                if tot > 0 and E.seen.get(key, 0) < tot:
                    E.h.wait_ge(sem, tot)
                    E.seen[key] = tot
            self.convert_weights(l + 1)
        self.na_phase(l)
        if l + 1 < self.nl:
            self.compute_mods([l + 1])
        if "stop_na" in self.dbg:
            return
        self.gdn_phase(l)

    def na_phase(self, l):
        k = self.k
        ctx_out = l < self.nl - 1 or ("force_ctx" in self.dbg)
        k.push_scope()
        KQ = k.sb("naKQ", [128, 8, TT], BF16)
        for j in range(8):
            k.dma("sp", KQ[:, j, :], DV(self.QKT[j], *[k.dbuf(("QKT", j, blk)) for blk in range(9)]))
        Vt = k.sb("naV", [128, NTILE, 8, 65], BF16)
        k.memset(Vt[:, :, :, 64:65], 1.0)
        for ti in range(NTILE):
            k.dma("sp", Vt[:, ti, :, 0:64],
                  DV(self.VA[ti * 128:(ti + 1) * 128, :].rearrange("p (h d) -> p h d", d=64), k.dbuf(("VA", ti))))
        EB = k.sb("naEB", [128, 8, 12 * 128], BF16)
        st = [k.sb(f"naST{i}", [128, 12 * 128], F32) for i in range(2)]
        for h in range(8):
            t = st[h % 2]
            k.dma("sp", t[:, :], DV(self.rpb_d[l][h], k.dbuf("rpb_d")))
            k.act(t[:, :], t[:, :], AF.Exp)
            k.tt(EB[:, h, :], t[:, :], self.cst[:, C_CM:C_CM + 12 * 128], ALU.mult)
        E32 = [k.sb(f"naE{i}", [128, 5 * 128], F32) for i in range(2)]
        PT = [k.sb(f"naPT{i}", [128, 7 * 128], BF16) for i in range(2)]
        rden = [k.sb(f"naRD{i}", [128, 8], F32) for i in range(2)]
        yna = [k.sb(f"naY{i}", [128, 512], BF16) for i in range(2)]
        ynaT = [k.sb(f"naYT{i}", [128, 4, 128], BF16) for i in range(2)]
        psT = self.PS[0].v(self.PS[0].h[:, :].bitcast(BF16))
        groups = []
        for rp in range(32):
            if rp <= 1:
                groups.append((rp, [0, 1, 2, 3], 3 - rp))
            elif rp >= 30:
                groups.append((rp, [28, 29, 30, 31], 31 - rp))
            else:
                groups.append((rp, [rp - 2, rp - 1, rp, rp + 1, rp + 2], 7))
        if ctx_out:
            groups.append((32, [], None))
            groups.append((33, [], None))
        hn = 0
        for gi, (qt, lat, s0) in enumerate(groups):
            nlat = len(lat)
            po = [self.PS[4 + 2 * (gi % 2)], self.PS[5 + 2 * (gi % 2)]]
            slots = [32, 33] + lat
            ns = len(slots)
            for h in range(8):
                hc, pb = h // 2, (h % 2) * 64
                pA = self.PS[2 * (hn % 2)]
                pB = self.PS[2 * (hn % 2) + 1]
                e32 = E32[hn % 2]
                pt = PT[hn % 2]
                hn += 1
                q = KQ[pb:pb + 64, hc, qt * 128:(qt + 1) * 128]
                for si, kt in enumerate(slots):
                    dst = pA[:, si * 128:(si + 1) * 128] if si < 4 else pB[:, (si - 4) * 128:(si - 3) * 128]
                    k.mm(dst, KQ[pb:pb + 64, 4 + hc, kt * 128:(kt + 1) * 128], q)
                k.act(pt[:, 0:256], pA[:, 0:256], AF.Exp)
                if nlat > 0:
                    k.act(e32[:, 0:256], pA[:, 256:512], AF.Exp)
                    k.act(e32[:, 256:nlat * 128], pB[:, 0:(nlat - 2) * 128], AF.Exp)
                    k.tt(pt[:, 256:256 + nlat * 128], e32[:, 0:nlat * 128],
                         EB[:, h, s0 * 128:(s0 + nlat) * 128], ALU.mult)
                for si, kt in enumerate(slots):
                    k.mm(po[h // 4][:, (h % 4) * 65:(h % 4) * 65 + 65], pt[:, si * 128:(si + 1) * 128],
                         Vt[:, kt, h, :], start=(si == 0), stop=(si == ns - 1))
            rd = rden[gi % 2]
            y = yna[gi % 2]
            for half in range(2):
                pv = po[half].v(po[half].h[:, 0:260].rearrange("p (h d) -> p h d", d=65))
                k.recip(rd.v(rd.h[:, half * 4:half * 4 + 4].unsqueeze(2)), DV(pv.ap[:, :, 64:65], *pv.bufs))
                k.tt(y.v(y.h[:, half * 256:(half + 1) * 256].rearrange("p (h d) -> p h d", d=64)),
                     DV(pv.ap[:, :, 0:64], *pv.bufs),
                     rd.v(rd.h[:, half * 4:half * 4 + 4].unsqueeze(2).broadcast_to([128, 4, 64])), ALU.mult)
            yt = ynaT[gi % 2]
            for c in range(4):
                k.tr(DV(psT.ap[:, c * 128:(c + 1) * 128], *psT.bufs), y[:, c * 128:(c + 1) * 128], self.identb[:, :])
            k.copy(yt[:, :, :], DV(psT.ap[:, 0:512].rearrange("p (c n) -> p c n", n=128), *psT.bufs))
            blk = min(qt // 4, 8)
            dst = self.YT[0:4, :, qt * 128:(qt + 1) * 128].rearrange("c p n -> p c n")
            k.dma("sp", DV(dst, *[k.dbuf(("YT", c, blk, qt % 4)) for c in range(4)]), yt[:, :, :])
        k.pop_scope()

    def gdn_phase(self, l):
        k = self.k
        sv, cst = self.sv, self.cst
        ctx_out = l < self.nl - 1 or ("force_ctx" in self.dbg)
        k.push_scope()
        BDt = k.sb("gBD", [128, NTILE, 16], F32)
        k.dma("sp", BDt[:, :, :], DV(self.BD.rearrange("(t p) c -> p t c", p=128), *[k.dbuf(("BD", ti)) for ti in range(NTILE)]))
        BETA = k.sb("gBETA", [128, NTILE, 8], F32)
        k.act(BETA[:, :, :], BDt[:, :, 0:8], AF.Sigmoid)
        xg = k.sb("gXG", [128, NTILE, 8], F32)
        k.tt(xg[:, :, :], BDt[:, :, 8:16],
             sv.v(sv.h[:, SV_DTB + l * 8:SV_DTB + l * 8 + 8].unsqueeze(1).broadcast_to([128, NTILE, 8])), ALU.add)
        ax = k.sb("gAX", [128, NTILE, 8], F32)
        k.act(ax[:, :, :], xg[:, :, :], AF.Abs)
        k.act(ax[:, :, :], ax[:, :, :], AF.Exp, scale=-1.0)
        k.act(ax[:, :, :], ax[:, :, :], AF.Ln, bias=1.0)
        k.ts(xg[:, :, :], xg[:, :, :], 0.0, None, ALU.max)
        k.tt(xg[:, :, :], xg[:, :, :], ax[:, :, :], ALU.add)
        nea = k.sb("gNEA", [128, 8], F32)
        k.act(nea[:, :], sv[:, SV_ALOG + l * 8:SV_ALOG + l * 8 + 8], AF.Exp)
        k.ts(nea[:, :], nea[:, :], -1.0, None, ALU.mult)
        GL = k.sb("gGL", [128, NTILE, 8], F32)
        k.tt(GL[:, :, :], xg[:, :, :], nea.v(nea.h[:, :].unsqueeze(1).broadcast_to([128, NTILE, 8])), ALU.mult)
        Ur = [k.sb(f"gUr{d}", [128, 128], F32R) for d in range(2)]
        NEGr = [k.sb(f"gNEGr{d}", [128, 128], F32R) for d in range(2)]
        for d in range(2):
            k.copy(Ur[d][:, :], cst[:, C_U + d * 128:C_U + (d + 1) * 128])
            k.copy(NEGr[d][:, :], cst[:, C_NEG + d * 128:C_NEG + (d + 1) * 128])
        identr = k.sb("gIr", [128, 128], F32R)
        k.copy(identr[:, :], self.identf[:, :])
        onesr = k.sb("gOr", [128, 128], F32R)
        k.copy(onesr[:, :], self.onesf[:, :])
        rmb = k.sb("gRm", [128, 128], BF16)
        k.copy(rmb[:, :], cst[:, C_RM:C_RM + 128])
        OG = sv[:, SV_OG + l * 128:SV_OG + (l + 1) * 128]
        QT = k.sb("gQT", [128, TT], BF16)
        KT = k.sb("gKT", [128, TT], BF16)
        Ktok = k.sb("gKtok", [128, NTILE, 128], BF16)
        Vtok = k.sb("gVtok", [128, NTILE, 128], F32)

        for h in range(1 if "gdbg" in self.dbg else 4):
            k.push_scope()
            raws = [k.sb(f"gRaw{i}", [128, TT], F32) for i in range(3)]
            cvs = [k.sb(f"gCv{i}", [128, TT], F32) for i in range(2)]
            for ui_, ch_ in enumerate((h, 4 + h, 8 + h)):
                k.dma("sp", raws[ui_][:, :], DV(self.GQ[ch_], *[k.dbuf(("GQ", ch_, blk)) for blk in range(9)]))
            sqb = [k.sb(f"gSq{i}", [128, 512], BF16) for i in range(2)]
            rs_ = [k.sb(f"gRs{i}", [128, 512], F32) for i in range(2)]
            un = [k.sb(f"gUn{i}", [128, 512], F32) for i in range(2)]
            unb = [k.sb(f"gUnb{i}", [128, 512], BF16) for i in range(2)]
            t1 = [k.sb(f"gT1{i}", [128, 512], F32) for i in range(2)]
            t2 = [k.sb(f"gT2{i}", [128, 512], F32) for i in range(2)]
            rp_ = [k.sb(f"gRope{i}", [128, 2, 512], F32) for i in range(2)]
            for ui, ch in enumerate((h, 4 + h, 8 + h)):
                raw = raws[ui]
                cv = cvs[ui % 2]
                wc = lambda tap: sv[:, SV_CONV + l * 60 + tap * 12 + ch:SV_CONV + l * 60 + tap * 12 + ch + 1]
                for (a, b) in ((0, TLAT), (TLAT, TT)):
                    k.ts(cv[:, a:b], raw[:, a:b], wc(2), None, ALU.mult)
                    for tap in (0, 1, 3, 4):
                        o = tap - 2
                        lo, hi = max(a, a - o), min(b, b - o)
                        k.stt(cv[:, lo:hi], raw[:, lo + o:hi + o], wc(tap), cv[:, lo:hi], ALU.mult, ALU.add)
                k.act(cv[:, :], cv[:, :], AF.Silu)
                if ui == 2:
                    for g4 in range(0, NTILE, 4):
                        ps = self.PS[(g4 // 4) % 2]
                        nt = min(4, NTILE - g4)
                        for q in range(nt):
                            k.tr(ps[:, q * 128:(q + 1) * 128], cv[:, (g4 + q) * 128:(g4 + q + 1) * 128], self.identf[:, :])
                        src = ps.v(ps.h[:, 0:nt * 128].rearrange("p (t n) -> p t n", n=128))
                        if (g4 // 4) % 2 == 0:
                            k.copy(Vtok[:, g4:g4 + nt, :], src)
                        else:
                            k.act(Vtok[:, g4:g4 + nt, :], src, AF.Copy)
                    continue
                dstT = QT if ui == 0 else KT
                for bi, (t0, N, s) in enumerate(BLOCKS):
                    i2 = bi % 2
                    if s == 0:
                        k.dma("sp", rp_[i2][:, :, :], DV(self.rope_d[:, :, t0:t0 + N], k.dbuf("rope_d")))
                    k.act(sqb[i2][:, :N], cv[:, t0:t0 + N], AF.Square)
                    pss = self.PS[2 + i2]
                    k.mm(pss[:, :N], self.onesb[:, :], sqb[i2][:, :N])
                    k.act(rs_[i2][:, :N], pss[:, :N], AF.Sqrt, bias=self.epsc[:, 0:1])
                    k.recip(rs_[i2][:, :N], rs_[i2][:, :N])
                    if s == 1:
                        if ui == 0:
                            k.stt(dstT[:, t0:t0 + N], cv[:, t0:t0 + N], 128.0 ** -0.5, rs_[i2][:, :N], ALU.mult, ALU.mult)
                        else:
                            k.tt(dstT[:, t0:t0 + N], cv[:, t0:t0 + N], rs_[i2][:, :N], ALU.mult)
                        continue
                    if ui == 0:
                        k.stt(un[i2][:, :N], cv[:, t0:t0 + N], 128.0 ** -0.5, rs_[i2][:, :N], ALU.mult, ALU.mult)
                    else:
                        k.tt(un[i2][:, :N], cv[:, t0:t0 + N], rs_[i2][:, :N], ALU.mult)
                    k.act(unb[i2][:, :N], un[i2][:, :N], AF.Copy)
                    psr = self.PS[4 + i2]
                    k.mm(psr[:, :N], rmb[:, :], unb[i2][:, :N])
                    k.tt(t1[i2][:, :N], un[i2][:, :N], rp_[i2][:, 0, :N], ALU.mult)
                    k.tt(t2[i2][:, :N], psr[:, :N], rp_[i2][:, 1, :N], ALU.mult)
                    k.tt(dstT[:, t0:t0 + N], t1[i2][:, :N], t2[i2][:, :N], ALU.add)
            for g8 in range(0, NTILE, 8):
                ps = self.PS[5 + (g8 // 8) % 2]
                psb = ps.v(ps.h[:, :].bitcast(BF16))
                nt = min(8, NTILE - g8)
                for q in range(nt):
                    k.tr(DV(psb.ap[:, q * 128:(q + 1) * 128], *psb.bufs), KT[:, (g8 + q) * 128:(g8 + q + 1) * 128], self.identb[:, :])
                k.copy(Ktok[:, g8:g8 + nt, :], DV(psb.ap[:, 0:nt * 128].rearrange("p (t n) -> p t n", n=128), *psb.bufs))
            k.pop_scope()
            if "gdbg" in self.dbg:
                k.dma("sp", DV(self.dbg_t["dq"][:, :], k.dbuf("dq")), QT[:, :])
                k.dma("sp", DV(self.dbg_t["dk"][:, :], k.dbuf("dk")), KT[:, :])
                k.dma("sp", DV(self.dbg_t["dv"][:, :, :], k.dbuf("dv")), Vtok[:, :, :])
                k.dma("sp", DV(self.dbg_t["dGL"][:, :, :], k.dbuf("dGL")), GL[:, :, :])
                k.dma("sp", DV(self.dbg_t["dBETA"][:, :, :], k.dbuf("dBETA")), BETA[:, :, :])
                if "g_pre_only" in self.dbg:
                    continue

            k.push_scope()
            QKM = k.sb("gQKM", [128, 2, NTILE, 128], BF16)
            KTL = k.sb("gKTL", [128, 2, NTILE, 128], BF16)
            TTb = k.sb("gTTb", [128, 2, NTILE, 128], BF16)
            OA = k.sb("gOA", [128, NTILE, 128], F32)
            k.memset(OA[:, :, :], 0.0)
            ECUM = k.sb("gECUM", [128, 2, NTILE], F32)
            NECUM = k.sb("gNECUM", [128, 2, NTILE], F32)
            ETOT = k.sb("gETOT", [128, 2, NTILE], F32)
            EK = k.sb("gEK", [128, 2, NTILE], F32)
            Gd = k.sb("gGd", [128, 2, NTILE], F32R)
            for d in range(2):
                col = d * 4 + h
                k.copy(Gd[:, d, :], GL[:, :, col])
                pc = self.PS[d]
                k.mm(pc[:, 0:NTILE], Ur[d][:, :], Gd[:, d, :])
                k.mm(pc[:, 64:64 + NTILE], onesr[:, :], Gd[:, d, :])
                k.act(ECUM[:, d, :], pc[:, 0:NTILE], AF.Exp)
                k.ts(NECUM[:, d, :], ECUM[:, d, :], -1.0, None, ALU.mult)
                k.act(ETOT[:, d, :], pc[:, 64:64 + NTILE], AF.Exp)
                k.copy(EK[:, d, :], pc[:, 0:NTILE])
                k.tt(EK[:, d, :], pc[:, 64:64 + NTILE], EK[:, d, :], ALU.subtract)
                k.act(EK[:, d, :], EK[:, d, :], AF.Exp)
            k.barrier_all()
            NP = 8
            RG = []
            for p in range(NP):
                b0 = self.PS[p]
                RG.append(dict(D=(b0, 0), G=(b0, 128), KQ=(b0, 256), TR=(b0, 384), Y=(b0, 0), Z=(b0, 128)))
            k.push_scope()
            WK = []
            for p in range(NP):
                WK.append(dict(
                    gB=k.sb(f"gB{p}", [128, 128], F32R), ngB=k.sb(f"gnB{p}", [128, 128], F32R),
                    GT=k.sb(f"gGT{p}", [128, 128], F32), AT=k.sb(f"gAT{p}", [128, 128], F32),
                    E=[k.sb(f"gE{p}_{i}", [128, 128], F32R) for i in range(2)],
                    Ysb=k.sb(f"gYs{p}", [128, 128], F32R),
                    Vm=k.sb(f"gVm{p}", [128, 128], F32R), Wm=k.sb(f"gWm{p}", [128, 128], F32R)))

            def rv(p, nm):
                t, c0 = RG[p][nm]
                return t[:, c0:c0 + 128]

            probs = [(n, d) for n in range(NTILE) for d in range(2)]
            g1cut = 9
            for f_ in self.dbg:
                if f_.startswith("g1cut="):
                    g1cut = int(f_[6:])
            for g0 in range(0, len(probs), NP):
                grp = probs[g0:g0 + NP]
                if g1cut == 0 or ("g1one" in self.dbg and g0 > 0):
                    break
                for p, (n, d) in enumerate(grp):
                    w = WK[p]
                    gcol = GL[:, n, d * 4 + h:d * 4 + h + 1]
                    k.ts(w["gB"][:, :], self.onesf[:, :], gcol, None, ALU.mult)
                    k.ts(w["ngB"][:, :], self.onesf[:, :], gcol, -1.0, ALU.mult, ALU.mult)
                for p, (n, d) in enumerate(grp):
                    w = WK[p]
                    k.mm(rv(p, "D"), w["gB"][:, :], Ur[d][:, :], start=True, stop=False)
                    k.mm(rv(p, "D"), Ur[d][:, :], w["ngB"][:, :], start=False, stop=False)
                    k.mm(rv(p, "D"), identr[:, :], NEGr[d][:, :], start=False, stop=True)
                    kc = KT[:, n * 128:(n + 1) * 128]
                    k.mm(rv(p, "G"), kc, kc)
                    k.mm(rv(p, "KQ"), kc, QT[:, n * 128:(n + 1) * 128])
                for p, (n, d) in enumerate(grp):
                    w = WK[p]
                    k.act(w["GT"][:, :], rv(p, "D"), AF.Exp)
                if g1cut <= 1:
                    continue
                for p, (n, d) in enumerate(grp):
                    w = WK[p]
                    k.stt(w["AT"][:, :], rv(p, "G"), BETA[:, n, d * 4 + h:d * 4 + h + 1], w["GT"][:, :], ALU.mult, ALU.mult)
                    k.tt(QKM[:, d, n, :], rv(p, "KQ"), w["GT"][:, :], ALU.mult)
                    k.act(KTL[:, d, n, :], Ktok[:, n, :], AF.Copy, scale=EK[:, d, n:n + 1])
                lm = lambda d, lev: cst[:, C_LM + (d * 7 + lev) * 128:C_LM + (d * 7 + lev + 1) * 128]
                if g1cut <= 2:
                    continue
                for p, (n, d) in enumerate(grp):
                    w = WK[p]
                    k.tt(w["E"][0][:, :], w["AT"][:, :], lm(d, 0), ALU.mult)
                for p, (n, d) in enumerate(grp):
                    w = WK[p]
                    e0 = w["E"][0]
                    k.tr(rv(p, "TR"), e0.v(e0.h[:, :].bitcast(F32)), self.identf[:, :])
                for p, (n, d) in enumerate(grp):
                    w = WK[p]
                    e0 = w["E"][0]
                    k.tt(w["Wm"][:, :], self.identf[:, :], e0.v(e0.h[:, :].bitcast(F32)), ALU.subtract)
                    k.tt(w["Vm"][:, :], self.identf[:, :], rv(p, "TR"), ALU.subtract)
                for lev in range(1, 7 if g1cut > 3 else 1):
                    for p, (n, d) in enumerate(grp):
                        w = WK[p]
                        k.tt(w["E"][lev % 2][:, :], w["AT"][:, :], lm(d, lev), ALU.mult, eng=self.e_eng)
                    for p, (n, d) in enumerate(grp):
                        w = WK[p]
                        k.mm(rv(p, "Y"), w["E"][lev % 2][:, :], w["Vm"][:, :])
                    for p, (n, d) in enumerate(grp):
                        w = WK[p]
                        k.act(w["Ysb"][:, :], rv(p, "Y"), AF.Copy)
                    for p, (n, d) in enumerate(grp):
                        w = WK[p]
                        k.mm(rv(p, "Z"), w["Wm"][:, :], w["Ysb"][:, :])
                    for p, (n, d) in enumerate(grp):
                        w = WK[p]
                        vm = w["Vm"]
                        k.tt(vm[:, :], vm.v(vm.h[:, :].bitcast(F32)), rv(p, "Z"), ALU.subtract)
                    for p, (n, d) in enumerate(grp):
                        w = WK[p]
                        vm = w["Vm"]
                        k.tr(rv(p, "TR"), vm.v(vm.h[:, :].bitcast(F32)), self.identf[:, :])
                    for p, (n, d) in enumerate(grp):
                        w = WK[p]
                        if lev < 6:
                            k.act(w["Wm"][:, :], rv(p, "TR"), AF.Copy)
                        else:
                            k.act(TTb[:, d, n, :], rv(p, "TR"), AF.Copy)
            k.pop_scope()
            if "gdbg" in self.dbg:
                k.dma("sp", DV(self.dbg_t["dT"][:, :, :, :], k.dbuf("dT")), TTb[:, :, :, :])
                k.dma("sp", DV(self.dbg_t["dQKM"][:, :, :, :], k.dbuf("dQKM")), QKM[:, :, :, :])
                k.dma("sp", DV(self.dbg_t["dKTL"][:, :, :, :], k.dbuf("dKTL")), KTL[:, :, :, :])
                for ii, tt_ in enumerate((ECUM, NECUM, ETOT, EK)):
                    k.dma("sp", DV(self.dbg_t["dE"][:, ii, :, :], k.dbuf("dE")), tt_[:, :, :])
                if "g_g1_only" in self.dbg:
                    k.pop_scope()
                    continue
            SR = []
            for d in range(2):
                b = [self.PS[4 * d + i] for i in range(4)]
                SR.append(dict(KS=(k.region(b[0], f"sKS{d}"), 0), QS=(k.region(b[0], f"sQS{d}"), 128),
                               VN=(k.region(b[1], f"sVN{d}"), 0), O=(k.region(b[2], f"sO{d}"), 0),
                               SD=(k.region(b[3], f"sSD{d}"), 0)))

            def sv_(d, nm):
                t, c0 = SR[d][nm]
                return t[:, c0:c0 + 128]

            S = [k.sb(f"gS{d}", [128, 128], F32) for d in range(2)]
            Sb = [k.sb(f"gSb{d}", [128, 128], BF16) for d in range(2)]
            Rb = [k.sb(f"gRb{d}", [128, 128], BF16) for d in range(2)]
            VNb = [k.sb(f"gVNb{d}", [128, 128], BF16) for d in range(2)]
            for d in range(2):
                k.memset(S[d][:, :], 0.0)
                k.memset(Sb[d][:, :], 0.0)
            order = [[32, 33] + list(range(32)), [33, 32] + list(range(31, -1, -1))]
            for step in range(NTILE):
                for d in range(2):
                    n = order[d][step]
                    col = d * 4 + h
                    need_o = (n < 32) or ctx_out
                    k.mm(sv_(d, "KS"), KT[:, n * 128:(n + 1) * 128], Sb[d][:, :])
                    if need_o:
                        k.mm(sv_(d, "QS"), QT[:, n * 128:(n + 1) * 128], Sb[d][:, :])
                    k.stt(Rb[d][:, :], sv_(d, "KS"), NECUM[:, d, n:n + 1], Vtok[:, n, :], ALU.mult, ALU.add)
                    k.mm(sv_(d, "VN"), TTb[:, d, n, :], Rb[d][:, :])
                    k.act(VNb[d][:, :], sv_(d, "VN"), AF.Copy, scale=BETA[:, n, col:col + 1])
                    k.mm(sv_(d, "SD"), KTL[:, d, n, :], VNb[d][:, :])
                    if need_o:
                        k.mm(sv_(d, "O"), QKM[:, d, n, :], VNb[d][:, :])
                    k.stt(S[d][:, :], S[d][:, :], ETOT[:, d, n:n + 1], sv_(d, "SD"), ALU.mult, ALU.add)
                    k.act(Sb[d][:, :], S[d][:, :], AF.Copy)
                    if need_o:
                        k.stt(OA[:, n, :], sv_(d, "QS"), ECUM[:, d, n:n + 1], OA[:, n, :], ALU.mult, ALU.add, eng="pool") \
                            if False else k.stt(OA[:, n, :], sv_(d, "QS"), ECUM[:, d, n:n + 1], OA[:, n, :], ALU.mult, ALU.add)
                        k.tt(OA[:, n, :], OA[:, n, :], sv_(d, "O"), ALU.add)
            k.barrier_all()
            if "gdbg" in self.dbg:
                k.dma("sp", DV(self.dbg_t["dOA"][:, :, :], k.dbuf("dOA")), OA[:, :, :])
            ntl = NTILE if ctx_out else 32
            o2 = k.sb("gO2", [128, 4, 128], F32)
            junk = k.sb("gJunk", [128, 128], F32)
            ssq = k.sb("gSSQ", [128, NTILE], F32)
            k.memset(ssq[:, :], 0.0)
            for n in range(ntl):
                k.act(junk[:, :], OA[:, n, :], AF.Square, accum=ssq[:, n:n + 1])
            k.act(ssq[:, 0:ntl], ssq[:, 0:ntl], AF.Sqrt, scale=1.0 / 128, bias=self.epsc[:, 0:1])
            k.recip(ssq[:, 0:ntl], ssq[:, 0:ntl])
            ggt = [k.sb(f"gGG{i}", [128, 128], F32) for i in range(2)]
            yb = [k.sb(f"gYb{i}", [128, 4, 128], BF16) for i in range(2)]
            ybT = [k.sb(f"gYbT{i}", [128, 512], BF16) for i in range(2)]
            for g4 in range(0, ntl, 4):
                gi = (g4 // 4) % 2
                nt = min(4, ntl - g4)
                for q in range(nt):
                    n = g4 + q
                    gg = ggt[n % 2]
                    k.dma("sp", gg[:, :], DV(self.GG[n * 128:(n + 1) * 128, h * 128:(h + 1) * 128], k.dbuf(("GG", n))))
                    k.stt(o2[:, n % 4, :], OA[:, n, :], ssq[:, n:n + 1], OG, ALU.mult, ALU.mult)
                    k.tt(yb[gi][:, q, :], o2[:, n % 4, :], gg[:, :], ALU.mult)
                ps = self.PS[gi]
                psb = ps.v(ps.h[:, :].bitcast(BF16))
                for q in range(nt):
                    k.tr(DV(psb.ap[:, q * 128:(q + 1) * 128], *psb.bufs), yb[gi][:, q, :], self.identb[:, :])
                k.act(ybT[gi][:, 0:nt * 128], DV(psb.ap[:, 0:nt * 128], *psb.bufs), AF.Copy)
                blk = min(g4 // 4, 8)
                k.dma("pool", DV(self.YT[4 + h][:, g4 * 128:(g4 + nt) * 128],
                               *[k.dbuf(("YT", 4 + h, blk, q)) for q in range(4)]), ybT[gi][:, 0:nt * 128])
            k.pop_scope()
        k.pop_scope()


_CACHE = {}


def _prep_inputs(inp, ncores=8):
    cst, rope = host_consts()
    rpbg = host_rpb(np.asarray(inp["na_rpb"], np.float32))
    maps = []
    shared = {n: np.ascontiguousarray(inp[n], dtype=np.float32) for n in
              ("w_mod", "w_ffn1_in", "w_ffn2_in", "w_ffn1_out", "w_ffn2_out", "w_in", "w_out")}
    for b in range(ncores):
        m = {"x": np.ascontiguousarray(inp["x"][b], dtype=np.float32),
             "ctx": np.ascontiguousarray(inp["ctx"][b], dtype=np.float32),
             "sv": host_smallvec(inp, b), "cst": cst, "rope": rope, "rpbg": rpbg}
        m.update(shared)
        maps.append(m)
    return maps


def kernel(**inputs):
    inp = {k_: np.asarray(v) for k_, v in inputs.items()}
    if "nc" not in _CACHE:
        _CACHE["nc"] = Prog().build()
    nc = _CACHE["nc"]
    maps = _prep_inputs(inp)
    res = run_bass_kernel_spmd(nc, maps, core_ids=list(range(8)))
    out = np.stack([np.asarray(r["out"], dtype=np.float32) for r in res.results], axis=0)
    return out
```
